# Optimizing a Trainium2 kernel written in Bass

```python
import jax, jax.numpy as jnp
from jax import lax
import numpy as np

D_MODEL = 1024
BATCH = 16
SEQ = 2048
DEPTH = 1
DEC_BATCH = 32
DEC_SEQ = 4
PAST_LEN = 16384
PAGE_SIZE = 128

N_HEADS = 8
HEAD_DIM = 64
ATT_WIDTH = N_HEADS * HEAD_DIM
CONV_CH = D_MODEL // 2
CONV_WIDTH = 31
FFN_HIDDEN = -(-8 * D_MODEL // (3 * 256)) * 256
PLE_DIM = 256
Q_BLOCK = 128
LN_EPS = 1e-5
SB_BIAS_INIT = -6.0
DEEPNORM_ALPHA = (2.0 * DEPTH) ** 0.25
DEEPNORM_BETA = (8.0 * DEPTH) ** -0.25
IN_SPLITS = (CONV_CH, CONV_CH, ATT_WIDTH, ATT_WIDTH, ATT_WIDTH, D_MODEL, D_MODEL)
IN_COLS = sum(IN_SPLITS)

kernel_name = 'hybrid_conformer_conv_stickbreaking_decoder_step'


def layer_norm(x, g, b):
    xf = x.astype(jnp.float32)
    mu = jnp.mean(xf, axis=-1, keepdims=True)
    xc = xf - mu
    var = jnp.mean(xc * xc, axis=-1, keepdims=True)
    return (xc * lax.rsqrt(var + LN_EPS) * g + b).astype(x.dtype)


def stick_breaking_block(q, k, v, sb_bias, qpos, kpos):
    z = jnp.einsum('bqhd,bshd->bhqs', q, k, preferred_element_type=jnp.float32) * (HEAD_DIM ** -0.5)
    z = z + sb_bias.astype(jnp.float32)[None, :, None, None]
    mask = kpos[None, :] < qpos[:, None]
    log_1m = jnp.where(mask, jax.nn.log_sigmoid(-z), 0.0)
    between = lax.cumsum(log_1m, axis=3, reverse=True) - log_1m
    a = jnp.where(mask, jnp.exp(jax.nn.log_sigmoid(z) + between), 0.0)
    return jnp.einsum('bhqs,bshd->bqhd', a.astype(v.dtype), v)


def stick_breaking_attention(q, k, v, sb_bias, q_offset):
    b, tq = q.shape[0], q.shape[1]
    blk = Q_BLOCK if tq % Q_BLOCK == 0 else tq
    nb = tq // blk
    kpos = jnp.arange(k.shape[1], dtype=jnp.int32)
    qpos = (q_offset + jnp.arange(tq, dtype=jnp.int32)).reshape(nb, blk)
    qb = q.reshape(b, nb, blk, N_HEADS, HEAD_DIM).transpose(1, 0, 2, 3, 4)
    out = lax.map(lambda a: stick_breaking_block(a[0], k, v, sb_bias, a[1], kpos), (qb, qpos))
    return out.transpose(1, 0, 2, 3, 4).reshape(b, tq, N_HEADS, HEAD_DIM)


def causal_depthwise_conv(u, past, conv_w, conv_b):
    full = jnp.concatenate([past, u], axis=1)
    out = lax.conv_general_dilated(full, conv_w[:, None, :], window_strides=(1,), padding='VALID',
                                   dimension_numbers=('NWC', 'WIO', 'NWC'),
                                   feature_group_count=CONV_CH)
    return out + conv_b, full[:, full.shape[1] - (CONV_WIDTH - 1):]


def decoder_layer(x, p, past_k, past_v, past_conv, w_in, sb_bias, conv_w, conv_b, conv_ln_g,
                  conv_ln_b, w_conv_proj, w_att_proj, w_out, ln1_g, ln1_b, w_ffn_up, w_ffn_down,
                  w_ple_gate, w_ple, ln2_g, ln2_b):
    b, t = x.shape[0], x.shape[1]
    cuts = np.cumsum(IN_SPLITS)[:-1].tolist()
    glu_a, glu_b, q, k, v, g_conv, g_att = jnp.split(x @ w_in, cuts, axis=-1)
    u = glu_a * jax.nn.sigmoid(glu_b)
    c, new_conv = causal_depthwise_conv(u, past_conv, conv_w, conv_b)
    conv_out = jax.nn.silu(layer_norm(c, conv_ln_g, conv_ln_b)) @ w_conv_proj
    q = q.reshape(b, t, N_HEADS, HEAD_DIM)
    k = k.reshape(b, t, N_HEADS, HEAD_DIM)
    v = v.reshape(b, t, N_HEADS, HEAD_DIM)
    k_all = jnp.concatenate([past_k, k], axis=1)
    v_all = jnp.concatenate([past_v, v], axis=1)
    o = stick_breaking_attention(q, k_all, v_all, sb_bias, past_k.shape[1])
    att_out = o.reshape(b, t, ATT_WIDTH) @ w_att_proj
    mixed = (jax.nn.sigmoid(g_conv) * conv_out + jax.nn.sigmoid(g_att) * att_out) @ w_out
    x = layer_norm(DEEPNORM_ALPHA * x + mixed, ln1_g, ln1_b)
    gate, up = jnp.split(x @ w_ffn_up, 2, axis=-1)
    ffn = (jax.nn.silu(gate) * up) @ w_ffn_down
    ple = jax.nn.sigmoid(x @ w_ple_gate) * (p @ w_ple)
    x = layer_norm(DEEPNORM_ALPHA * x + ffn + ple, ln2_g, ln2_b)
    return x, k, v, new_conv


def setup_inputs(seed: int = 0) -> dict:
    key = jax.random.key(seed)
    ks = jax.random.split(key, 28)
    n_pages = PAST_LEN // PAGE_SIZE
    n_used = DEC_BATCH * n_pages
    n_pool = n_used + n_used // 4
    f32 = jnp.float32
    nrm = lambda i, shape, s: jax.random.normal(ks[i], shape, f32) * s
    page_table = jax.random.permutation(ks[27], n_pool)[:n_used].reshape(DEC_BATCH, n_pages).astype(jnp.int32)
    return {
        'x_prompt': nrm(0, (BATCH, SEQ, D_MODEL), 1.0),
        'x_sample': nrm(1, (DEC_BATCH, DEC_SEQ, D_MODEL), 1.0),
        'p_prompt': nrm(2, (DEPTH, BATCH, SEQ, PLE_DIM), 1.0),
        'p_sample': nrm(3, (DEPTH, DEC_BATCH, DEC_SEQ, PLE_DIM), 1.0),
        'cache_k': nrm(4, (DEPTH, n_pool, PAGE_SIZE, N_HEADS, HEAD_DIM), 1.0),
        'cache_v': nrm(5, (DEPTH, n_pool, PAGE_SIZE, N_HEADS, HEAD_DIM), 1.0),
        'state_conv': nrm(6, (DEPTH, DEC_BATCH, CONV_WIDTH - 1, CONV_CH), 0.5),
        'page_table': page_table,
        'w_in': nrm(7, (DEPTH, D_MODEL, IN_COLS), D_MODEL ** -0.5),
        'sb_bias': SB_BIAS_INIT + nrm(23, (DEPTH, N_HEADS), 0.1),
        'conv_w': nrm(8, (DEPTH, CONV_WIDTH, CONV_CH), CONV_WIDTH ** -0.5),
        'conv_b': nrm(9, (DEPTH, CONV_CH), 0.01),
        'conv_ln_g': 1.0 + nrm(10, (DEPTH, CONV_CH), 0.01),
        'conv_ln_b': nrm(11, (DEPTH, CONV_CH), 0.01),
        'w_conv_proj': nrm(12, (DEPTH, CONV_CH, D_MODEL), CONV_CH ** -0.5),
        'w_att_proj': nrm(13, (DEPTH, ATT_WIDTH, D_MODEL), ATT_WIDTH ** -0.5),
        'w_out': nrm(14, (DEPTH, D_MODEL, D_MODEL), D_MODEL ** -0.5 * DEEPNORM_BETA),
        'ln1_g': 1.0 + nrm(15, (DEPTH, D_MODEL), 0.01),
        'ln1_b': nrm(16, (DEPTH, D_MODEL), 0.01),
        'w_ffn_up': nrm(17, (DEPTH, D_MODEL, 2 * FFN_HIDDEN), D_MODEL ** -0.5),
        'w_ffn_down': nrm(18, (DEPTH, FFN_HIDDEN, D_MODEL), FFN_HIDDEN ** -0.5 * DEEPNORM_BETA),
        'w_ple_gate': nrm(19, (DEPTH, D_MODEL, D_MODEL), D_MODEL ** -0.5),
        'w_ple': nrm(20, (DEPTH, PLE_DIM, D_MODEL), PLE_DIM ** -0.5),
        'ln2_g': 1.0 + nrm(21, (DEPTH, D_MODEL), 0.01),
        'ln2_b': nrm(22, (DEPTH, D_MODEL), 0.01),
    }


def reference(x_prompt, x_sample, p_prompt, p_sample, cache_k, cache_v, state_conv, page_table,
              w_in, sb_bias, conv_w, conv_b, conv_ln_g, conv_ln_b, w_conv_proj, w_att_proj, w_out,
              ln1_g, ln1_b, w_ffn_up, w_ffn_down, w_ple_gate, w_ple, ln2_g, ln2_b):
    y_p, y_s = x_prompt, x_sample
    bp, bs = x_prompt.shape[0], x_sample.shape[0]
    kp_l, vp_l, cp_l, ks_l, vs_l, cs_l = [], [], [], [], [], []
    for i in range(DEPTH):
        lw = (w_in[i], sb_bias[i], conv_w[i], conv_b[i], conv_ln_g[i], conv_ln_b[i], w_conv_proj[i],
              w_att_proj[i], w_out[i], ln1_g[i], ln1_b[i], w_ffn_up[i], w_ffn_down[i],
              w_ple_gate[i], w_ple[i], ln2_g[i], ln2_b[i])
        empty_kv = jnp.zeros((bp, 0, N_HEADS, HEAD_DIM), x_prompt.dtype)
        zero_conv = jnp.zeros((bp, CONV_WIDTH - 1, CONV_CH), x_prompt.dtype)
        y_p, k1, v1, c1 = decoder_layer(y_p, p_prompt[i], empty_kv, empty_kv, zero_conv, *lw)
        past_k = cache_k[i][page_table].reshape(bs, -1, N_HEADS, HEAD_DIM)
        past_v = cache_v[i][page_table].reshape(bs, -1, N_HEADS, HEAD_DIM)
        y_s, k2, v2, c2 = decoder_layer(y_s, p_sample[i], past_k, past_v, state_conv[i], *lw)
        kp_l.append(k1); vp_l.append(v1); cp_l.append(c1)
        ks_l.append(k2); vs_l.append(v2); cs_l.append(c2)
    k_prompt = jnp.stack(kp_l)
    v_prompt = jnp.stack(vp_l)
    conv_prompt = jnp.stack(cp_l)
    k_sample = jnp.stack(ks_l)
    v_sample = jnp.stack(vs_l)
    conv_sample = jnp.stack(cs_l)
    return (y_p, y_s, k_prompt, v_prompt, conv_prompt, k_sample, v_sample, conv_sample)
```

```python
import numpy as np
from contextlib import ExitStack
import concourse.bass as bass
import concourse.mybir as mybir
from concourse.bass_utils import run_bass_kernel_spmd

F32 = mybir.dt.float32
BF16 = mybir.dt.bfloat16
I32 = mybir.dt.int32
AF = mybir.ActivationFunctionType
ALU = mybir.AluOpType

D = 1024
H = 8
DH = 64
CC = 512
FF = 2816
NJ = FF // 128
PLE = 256
INC = 4608
CW = 31
ALPHA = float(2.0 ** 0.25)
EPS = 1e-5
NSLOT = 6
SLOTW = 4096
SELF_WAIT = True
DEBUG = False
NTAP_POOL = 0

ENGS = ("pe", "act", "dve", "pool", "sp")


class Sem:
    def __init__(self, h):
        self.h = h
        self.n = 0


class Res:
    __slots__ = ("name", "w", "r")

    def __init__(self, name):
        self.name = name
        self.w = None
        self.r = []


class Prog:
    def __init__(self, nc, es):
        self.nc = nc
        self.es = es
        self.q = {e: [] for e in ENGS}
        self.psem = {e: Sem(es.enter_context(nc.semaphore("prog_" + e))) for e in ENGS}
        self.waited = {e: {} for e in ENGS}
        self.nsem = 0
        self.dma_tickets = []

    def newsem(self, name):
        self.nsem += 1
        return Sem(self.es.enter_context(self.nc.semaphore(name)))

    def _deps(self, eng, reads, writes, extra):
        deps = list(extra)
        for r in reads:
            if r.w is not None:
                deps.append(r.w)
        for w in writes:
            if w.w is not None:
                deps.append(w.w)
            deps.extend(w.r)
        best = {}
        for d in deps:
            if d is None:
                continue
            s, v = d
            if s is self.psem[eng] and (eng == "pe" or eng == "sp" or not SELF_WAIT):
                continue
            if best.get(id(s), (None, 0))[1] < v:
                best[id(s)] = (s, v)
        out = []
        for k, (s, v) in best.items():
            if self.waited[eng].get(k, 0) < v:
                self.waited[eng][k] = v
                out.append((s, v))
        return out

    def _commit(self, tk, reads, writes):
        for r in reads:
            r.r.append(tk)
            if len(r.r) > 24:
                best = {}
                for s, v in r.r:
                    if best.get(id(s), (None, 0))[1] < v:
                        best[id(s)] = (s, v)
                r.r = list(best.values())
        for w in writes:
            w.w = tk
            w.r = []

    def emit(self, eng, fn, reads=(), writes=(), sig=True, extra=()):
        waits = self._deps(eng, reads, writes, extra)
        ps = self.psem[eng]
        if sig:
            ps.n += 1
            tk = (ps, ps.n)
        else:
            tk = (ps, ps.n + 1)
        self.q[eng].append((waits, fn, sig))
        self._commit(tk, reads, writes)
        return tk

    def dma(self, eng, out, in_, sem, reads=(), writes=(), extra=(), indirect=None):
        waits = self._deps(eng, reads, writes, extra)
        sem.n += 16
        tk = (sem, sem.n)
        if indirect is None:
            fn = lambda e: e.dma_start(out=out, in_=in_)
        else:
            fn = lambda e: e.indirect_dma_start(out=out, out_offset=None, in_=in_,
                                                in_offset=bass.IndirectOffsetOnAxis(ap=indirect, axis=0))
        self.q[eng].append((waits, fn, sem))
        self._commit(tk, reads, writes)
        self.dma_tickets.append(tk)
        return tk

    def barrier(self):
        tks = [(self.psem[e], self.psem[e].n) for e in ENGS if self.psem[e].n > 0]
        best = {}
        for s, v in self.dma_tickets:
            if best.get(id(s), (None, 0))[1] < v:
                best[id(s)] = (s, v)
        tks += list(best.values())
        self.dma_tickets = []
        for e in ENGS:
            waits = self._deps(e, (), (), tks)
            if waits:
                self.q[e].append((waits, None, False))

    def replay(self, block):
        nc = self.nc

        def run(name, e):
            ps = self.psem[name]
            for waits, fn, sig in self.q[name]:
                for s, v in waits:
                    e.wait_ge(s.h, v)
                if fn is None:
                    continue
                ins = fn(e)
                if sig is True:
                    ins.then_inc(ps.h, 1)
                elif sig is not False:
                    ins.then_inc(sig.h, 16)

        @block.tensor
        def _(e):
            run("pe", e)

        @block.scalar
        def _(e):
            run("act", e)

        @block.vector
        def _(e):
            run("dve", e)

        @block.gpsimd
        def _(e):
            run("pool", e)

        @block.sync
        def _(e):
            run("sp", e)

    def mm(self, out, lhsT, rhs, start, stop, reads, writes, sig=False, extra=()):
        return self.emit("pe", lambda e: e.matmul(out, lhsT=lhsT, rhs=rhs, start=start, stop=stop,
                                                  skip_group_check=True),
                         reads, writes, sig, extra)

    def tr(self, out, in_, ident, reads, writes, sig=False):
        return self.emit("pe", lambda e: e.transpose(out=out, in_=in_, identity=ident), reads, writes, sig)

    def act(self, out, in_, func, reads, writes, bias=0.0, scale=1.0, extra=()):
        return self.emit("act", lambda e: e.activation(out=out, in_=in_, func=func, bias=bias, scale=scale),
                         reads, writes, True, extra)

    def copy(self, eng, out, in_, reads, writes):
        if eng == "act":
            return self.act(out, in_, AF.Copy, reads, writes)
        return self.emit(eng, lambda e: e.tensor_copy(out=out, in_=in_), reads, writes)

    def tt(self, eng, out, in0, in1, op, reads, writes):
        return self.emit(eng, lambda e: e.tensor_tensor(out=out, in0=in0, in1=in1, op=op), reads, writes)

    def ts(self, eng, out, in0, s1, s2, op0, op1, reads, writes):
        return self.emit(eng, lambda e: e.tensor_scalar(out=out, in0=in0, scalar1=s1, scalar2=s2, op0=op0, op1=op1),
                         reads, writes)

    def stt(self, eng, out, in0, scalar, in1, op0, op1, reads, writes):
        return self.emit(eng, lambda e: e.scalar_tensor_tensor(out=out, in0=in0, scalar=scalar, in1=in1,
                                                               op0=op0, op1=op1), reads, writes)

    def bn_stats(self, out, in_, reads, writes):
        return self.emit("dve", lambda e: e.bn_stats(out=out, in_=in_), reads, writes)

    def bn_aggr(self, out, in_, reads, writes):
        return self.emit("dve", lambda e: e.bn_aggr(out=out, in_=in_), reads, writes)

    def memset(self, eng, ap, val, writes):
        return self.emit(eng, lambda e: e.memset(ap, val), (), writes)


def slot_plan():
    slots = []
    for mh in range(2):
        c0 = mh * 512
        slots.append([("w_conv_proj", k * 128, c0, 512) for k in range(4)] +
                     [("w_att_proj", k * 128, c0, 512) for k in range(4)])
        slots.append([("w_in", k * 128, 2560 + c0, 512) for k in range(8)])
        slots.append([("w_in", k * 128, 3584 + c0, 512) for k in range(8)])
    for hf in range(2):
        slots.append([("w_out", k * 128, hf * 512, 512) for k in range(8)])
    for jp in range(NJ // 2):
        pcs = []
        for j in (2 * jp, 2 * jp + 1):
            for k in range(8):
                pcs.append(("w_ffn_up", k * 128, j * 128, 128))
                pcs.append(("w_ffn_up", k * 128, FF + j * 128, 128))
        slots.append(pcs)
    for hf in range(2):
        c0 = hf * 512
        slots.append([("w_ple_gate", k * 128, c0, 512) for k in range(8)])
        slots.append([("w_ple", k * 128, c0, 512) for k in range(2)] +
                     [("w_ffn_down", k * 128, c0, 512) for k in range(6)])
        slots.append([("w_ffn_down", k * 128, c0, 512) for k in range(6, 14)])
        slots.append([("w_ffn_down", k * 128, c0, 512) for k in range(14, 22)])
    return slots


def build(S, NPG, NPOOL, with_sample=True):
    NT = S // 128
    NG = S // 512
    nc = bass.Bass("TRN2", target_bir_lowering=False)
    dt = nc.dram_tensor

    def din(name, shape, dtype=F32):
        return dt(name, shape, dtype, kind="ExternalInput").ap()

    def dout(name, shape):
        return dt(name, shape, F32, kind="ExternalOutput").ap()

    xp = din("xp", [2 * S, D])
    pp = din("pp", [2 * S, PLE])
    xs = din("xs", [16, D])
    pps = din("pps", [16, PLE])
    ck = din("ck", [NPOOL * 128, 512])
    cv = din("cv", [NPOOL * 128, 512])
    sconv = din("sconv", [120, CC])
    ptab = din("ptab", [1, 4 * NPG], I32)
    W = {
        "w_in": din("w_in", [D, INC]),
        "w_conv_proj": din("w_conv_proj", [CC, D]),
        "w_att_proj": din("w_att_proj", [CC, D]),
        "w_out": din("w_out", [D, D]),
        "w_ffn_up": din("w_ffn_up", [D, 2 * FF]),
        "w_ffn_down": din("w_ffn_down", [FF, D]),
        "w_ple_gate": din("w_ple_gate", [D, D]),
        "w_ple": din("w_ple", [PLE, D]),
    }
    cwT = din("cwT", [128, 4 * CW])
    cvec = din("cvec", [128, 12])
    lnp = din("lnp", [4, D])
    sbb = din("sbb", [1, H])
    consts = din("consts", [128, 4 * 128])
    brow = din("brow", [1, 512])

    yp = dout("yp", [2 * S, D])
    ys = dout("ys", [16, D])
    kp = dout("kp", [2 * S, 512])
    vp = dout("vp", [2 * S, 512])
    cpo = dout("cpo", [60, CC])
    ksn = dout("ksn", [16, 512])
    vsn = dout("vsn", [16, 512])
    csn = dout("csn", [120, CC])

    slots = slot_plan()
    NSL = len(slots)
    scratch = dt("wscratch", [NSL, 128, SLOTW], BF16, kind="Internal").ap()

    with ExitStack() as es:
        P = Prog(nc, es)
        sb = lambda name, shape, dtype, st=es: st.enter_context(nc.sbuf_tensor(name, shape, dtype))
        pst = lambda name, shape, dtype, st=es: st.enter_context(nc.psum_tensor(name, shape, dtype))

        lnb = sb("lnb", [128, 4, D], F32)
        cw_sb = sb("cw_sb", [128, 4 * CW], F32)
        cvec_sb = sb("cvec_sb", [128, 12], F32)
        biasT = sb("biasT", [128, H], F32)
        cst_f = sb("cst_f", [128, 4 * 128], F32)
        identb = sb("identb", [128, 128], BF16)
        negU = sb("negU", [128, 128], BF16)
        negOnes = sb("negOnes", [128, 128], BF16)
        tri = sb("tri", [128, 128], BF16)
        onesM = sb("onesM", [128, 128], BF16)
        zer = sb("zer", [128, 512], BF16)
        identf = cst_f[:, 0:128]
        cf2 = sb("cf2", [128, 256], F32)
        brow_sb = sb("brow_sb", [1, 512], F32)

        psT = [pst("psT%d" % i, [128, 1024], BF16) for i in range(2)]
        ps = [pst("ps%d" % i, [128, 512], F32) for i in range(6)]
        R_psT = [Res("psT%d" % i) for i in range(2)]
        R_ps = [Res("ps%d" % i) for i in range(6)]
        rot = {"i": 0, "t": 0}

        def next_ps():
            rot["i"] = (rot["i"] + 1) % 6
            return ps[rot["i"]], R_ps[rot["i"]]

        def next_psT():
            rot["t"] = (rot["t"] + 1) % 2
            return psT[rot["t"]], R_psT[rot["t"]]

        R_const = Res("const")
        R_winA = Res("winA")
        s_c = P.newsem("s_const")
        for (o, i_) in ((lnb[:].rearrange("p a d -> p (a d)"), lnp.rearrange("a d -> (a d)").partition_broadcast(128)),
                        (cw_sb[:], cwT), (cvec_sb[:], cvec), (biasT[:], sbb.partition_broadcast(128)),
                        (cst_f[:], consts)):
            P.dma("sp", o, i_, s_c, writes=[R_const])
        P.copy("pool", identb[:], cst_f[:, 0:128], [R_const], [R_const])
        P.copy("pool", negU[:], cst_f[:, 128:256], [R_const], [R_const])
        P.copy("pool", tri[:], cst_f[:, 256:384], [R_const], [R_const])
        P.memset("pool", negOnes[:], -1.0, [R_const])
        P.memset("pool", onesM[:], 1.0 / 512.0, [R_const])
        P.memset("pool", zer[:], 0.0, [R_const])
        P.memset("pool", cf2[:, 0:128], -1.0, [R_const])
        P.memset("pool", cf2[:, 128:256], 1.0, [R_const])
        P.dma("sp", brow_sb[:], brow, P.newsem("s_brow"), writes=[R_const])
        R_winA = Res("winA")
        R_wascr = Res("wa_scr")
        s_wa = P.newsem("s_winA")
        s_was = P.newsem("s_winA_scr")
        wa_scr = dt("wa_scratch", [128, 8 * 2560], BF16, kind="Internal").ap()

        class WStream:
            def __init__(self):
                self.n = 0
                self.sem = [P.newsem("s_slot%d" % i) for i in range(NSLOT)]
                self.ssem = P.newsem("s_scr")
                self.res = [Res("slot%d" % i) for i in range(NSLOT)]
                self.scr = [Res("scr%d" % i) for i in range(NSL)]
                self.buf = None
                self.issued = 0
                self.total = 0

            def prefetch(self):
                while self.issued < self.limit and self.issued < self.n + NSLOT:
                    u = self.issued
                    si = u % NSLOT
                    sl = u % NSL
                    first = u < NSL
                    dst = self.buf[:, si, :]
                    if first:
                        off = 0
                        for (wn, r0, c0, ncol) in slots[sl]:
                            P.dma("pool", dst[:, off:off + ncol], W[wn][r0:r0 + 128, c0:c0 + ncol], self.sem[si],
                                  writes=[self.res[si]])
                            off += ncol
                        P.dma("sp", scratch[sl], dst, self.ssem, reads=[self.res[si]], writes=[self.scr[sl]])
                    else:
                        P.dma("sp", dst, scratch[sl], self.sem[si], reads=[self.scr[sl]], writes=[self.res[si]])
                    self.issued += 1

            def get(self, ahead=0):
                u = self.n + ahead
                assert u < self.issued, "weight unit consumed before its load was emitted"
                si = u % NSLOT
                return self.buf[:, si, :], self.res[si]

            def done(self):
                self.n += 1
                self.prefetch()

        WS = WStream()
        WS.limit = 0


        samp_es = ExitStack()

        def sample_alloc():
            sbP = lambda name, shape, dtype: sb("SP_" + name, shape, dtype, samp_es)
            d_ = dict(
                s_sT=sbP("s_sT", [128, 4, 16], BF16), o_sT=sbP("o_sT", [128, 4, 16], BF16),
                Qblk=sbP("Qblk", [128, 4, 4, 32], BF16),
                knT=[sbP("knT%d" % b, [128, 4, 4], BF16) for b in range(4)],
                vnb=[sbP("vnb%d" % b, [4, 512], BF16) for b in range(4)],
                idx=sbP("idx", [128, 4 * NPG], I32))
            return d_

        def sample_S1(winA, SBP):
            s1_es = ExitStack()
            sbS = lambda name, shape, dtype: sb("S_" + name, shape, dtype, s1_es)
            xsT = sbS("xsT", [128, 8, 16], BF16)
            s_sT, o_sT, Qblk, knT, vnb, idx = [SBP[k] for k in ("s_sT", "o_sT", "Qblk", "knT", "vnb", "idx")]
            R_xsT, R_ssT, R_osT = Res("xsT"), Res("s_sT"), Res("o_sT")
            PB = min(16, NPG)
            NBT = NPG // PB
            NW = PB * 32
            xs_f = sbS("xs_f", [16, D], F32)
            xs_b = sbS("xs_b", [16, D], BF16)
            qs_b = sbS("qs_b", [16, 512], BF16)
            us = sbS("us", [16, 512], F32)
            sgs = sbS("sgs", [16, 512], F32)
            sc_sb = sbS("sc_sb", [120, CC], F32)
            fullT = sbS("fullT", [128, 4, 4, 34], F32)
            kn = [sbS("kn%d" % b, [4, 512], F32) for b in range(4)]
            vn = [sbS("vn%d" % b, [4, 512], F32) for b in range(4)]
            knb = [sbS("knb%d" % b, [4, 512], BF16) for b in range(4)]
            qsT = sbS("qsT", [128, 4, 16], BF16)
            accs = sbS("accs", [128, 4, 16], F32)
            cbs = sbS("cbs", [128, 4, 16], BF16)
            sqs = sbS("sqs", [128, 4, 16], BF16)
            rstd_s = sbS("rstd_s", [128, 16], F32)
            pt_i = sbS("pt_i", [128, 4 * NPG], I32)
            pt_f = sbS("pt_f", [128, 4 * NPG], F32)
            io_f = sbS("io_f", [128, 1], F32)
            R = {n: Res(n) for n in ("xsf", "xsb", "qsb", "us", "sgs", "sc", "fullT", "qsT", "Qblk", "accs", "cbs", "sqs",
                                     "rstd", "idx", "En", "Lnw", "an", "Es", "Lbs", "Lsuf", "o32s", "otok", "otb")}
            R_kn = [Res("kn") for _ in range(4)]; R_vn = [Res("vn") for _ in range(4)]
            R_knb = [Res("knb") for _ in range(4)]; R_vnb = [Res("vnb") for _ in range(4)]
            R_knT = [Res("knT") for _ in range(4)]
            R_kpg = [Res("kpg") for _ in range(2)]; R_vpg = [Res("vpg") for _ in range(2)]
            R_kTp = [Res("kTp") for _ in range(4)]
            R_aTs = [Res("aTs") for _ in range(2)]
            s_in = P.newsem("s_sin")
            s_so = P.newsem("s_sout")
            s_kp = [P.newsem("s_kp%d" % i) for i in range(2)]
            s_vp = [P.newsem("s_vp%d" % i) for i in range(2)]
            s_o32 = P.newsem("s_o32")
            mnew = cst_f[0:4, 384:416]

            P.dma("sp", xs_f[:], xs[:, :], s_in, writes=[R["xsf"]])
            P.dma("sp", sc_sb[:], sconv[:, :], P.newsem("s_sin2"), writes=[R["sc"]])
            P.dma("sp", pt_i[:], ptab.partition_broadcast(128), P.newsem("s_sin3"), writes=[R["idx"]])
            P.emit("pool", lambda e: e.iota(io_f[:], pattern=[[0, 1]], base=0, channel_multiplier=1,
                                            allow_small_or_imprecise_dtypes=True), (), [R["idx"]])
            P.copy("pool", pt_f[:], pt_i[:], [R["idx"]], [R["idx"]])
            P.ts("pool", idx[:], pt_f[:], 128.0, io_f[:, 0:1], ALU.mult, ALU.add, [R["idx"]], [R["idx"]])
            P.copy("pool", xs_b[:], xs_f[:], [R["xsf"]], [R["xsb"]])
            pt_, Rpt = next_psT()
            for k in range(8):
                P.tr(pt_[:, k * 128:k * 128 + 16], xs_b[:, k * 128:(k + 1) * 128], identb[0:16, 0:16],
                     [R["xsb"], R_const], [Rpt], sig=(k == 7))
            P.copy("dve", xsT[:], pt_[:].rearrange("p (k n) -> p k n", k=8)[:, :, 0:16], [Rpt], [R_xsT])

            def proj16(cb):
                pk, Rpk = next_ps()
                for k in range(8):
                    P.mm(pk[0:16, :], xsT[:, k, :], winA[:, k, cb * 512:(cb + 1) * 512], k == 0, k == 7,
                         [R_xsT, R_winA], [Rpk], sig=(k == 7))
                return pk, Rpk

            pa, Rpa = proj16(0)
            pb, Rpb = proj16(1)
            P.act(sgs[:], pb[0:16, :], AF.Sigmoid, [Rpb], [R["sgs"]])
            P.tt("dve", us[:], pa[0:16, :], sgs[:], ALU.mult, [Rpa, R["sgs"]], [R["us"]])
            pq, Rpq = proj16(2)
            P.act(qs_b[:], pq[0:16, :], AF.Copy, [Rpq], [R["qsb"]], scale=0.125)
            for b in range(4):
                P.dma("sp", csn[b * 30:b * 30 + 26, :], sc_sb[b * 30 + 4:b * 30 + 30, :], s_so, reads=[R["sc"]])
                P.dma("sp", csn[b * 30 + 26:b * 30 + 30, :], us[b * 4:b * 4 + 4, :], s_so, reads=[R["us"]])
            for b in range(4):
                for which in range(2):
                    dst, Rd, dstb, Rdb, outd = ((kn, R_kn, knb, R_knb, ksn), (vn, R_vn, vnb, R_vnb, vsn))[which]
                    pk, Rpk = next_ps()
                    c0 = 1536 + which * 512
                    for k in range(8):
                        P.mm(pk[0:4, :], xsT[:, k, b * 4:(b + 1) * 4], winA[:, k, c0:c0 + 512], k == 0, k == 7,
                             [R_xsT, R_winA], [Rpk], sig=(k == 7))
                    P.copy("dve", dst[b][:], pk[0:4, :], [Rpk], [Rd[b]])
                    P.dma("sp", outd[b * 4:(b + 1) * 4, :], dst[b][:], s_so, reads=[Rd[b]])
                    P.copy("pool", dstb[b][:], dst[b][:], [Rd[b]], [Rdb[b]])
                pt_, Rpt = next_psT()
                for c in range(4):
                    P.tr(pt_[:, c * 128:c * 128 + 4], knb[b][:, c * 128:(c + 1) * 128], identb[0:4, 0:4],
                         [R_knb[b], R_const], [Rpt], sig=(c == 3))
                P.copy("dve", knT[b][:], pt_[:, 0:512].rearrange("p (k n) -> p k n", k=4)[:, :, 0:4], [Rpt], [R_knT[b]])
            pt_, Rpt = next_psT()
            for c in range(4):
                P.tr(pt_[:, c * 128:c * 128 + 16], qs_b[:, c * 128:(c + 1) * 128], identb[0:16, 0:16],
                     [R["qsb"], R_const], [Rpt], sig=(c == 3))
            P.copy("dve", qsT[:], pt_[:, 0:512].rearrange("p (k n) -> p k n", k=4)[:, :, 0:16], [Rpt], [R["qsT"]])
            P.memset("pool", Qblk[:], 0.0, [R["Qblk"]])
            for c in range(4):
                for hh in range(2):
                    h = 2 * c + hh
                    P.copy("pool", Qblk[hh * 64:(hh + 1) * 64, c, :, h * 4:(h + 1) * 4],
                           qsT[hh * 64:(hh + 1) * 64, c, :].rearrange("p (b q) -> p b q", b=4),
                           [R["qsT"], R["Qblk"]], [R["Qblk"]])
            for c in range(4):
                pf, Rpf = next_ps()
                P.tr(pf[:, 0:120], sc_sb[0:120, c * 128:(c + 1) * 128], identf[0:120, 0:120], [R["sc"], R_const], [Rpf])
                P.tr(pf[:, 128:144], us[0:16, c * 128:(c + 1) * 128], identf[0:16, 0:16], [R["us"], R_const], [Rpf], sig=True)
                P.copy("dve", fullT[:, c, :, 0:30], pf[:, 0:120].rearrange("p (b r) -> p b r", b=4), [Rpf], [R["fullT"]])
                P.copy("dve", fullT[:, c, :, 30:34], pf[:, 128:144].rearrange("p (b r) -> p b r", b=4), [Rpf], [R["fullT"]])
            for w in range(CW):
                for c in range(4):
                    src = fullT[:, c, :, w:w + 4]
                    dst = accs[:, c, :].rearrange("p (b t) -> p b t", b=4)
                    wv = cw_sb[:, c * CW + w:c * CW + w + 1]
                    if w == 0:
                        P.ts("dve", dst, src, wv, cvec_sb[:, c:c + 1], ALU.mult, ALU.add, [R["fullT"], R_const], [R["accs"]])
                    else:
                        P.stt("dve", dst, src, wv, dst, ALU.mult, ALU.add, [R["fullT"], R_const, R["accs"]], [R["accs"]])
            P.copy("pool", cbs[:], accs[:], [R["accs"]], [R["cbs"]])
            pm, Rpm = next_ps()
            for c in range(4):
                P.mm(pm[:, 0:16], onesM[:], cbs[:, c, :], c == 0, c == 3, [R_const, R["cbs"]], [Rpm], sig=(c == 3))
            for c in range(4):
                P.tt("dve", accs[:, c, :], accs[:, c, :], pm[:, 0:16], ALU.subtract, [R["accs"], Rpm], [R["accs"]])
            P.act(sqs[:], accs[:], AF.Square, [R["accs"]], [R["sqs"]])
            pv, Rpv = next_ps()
            for c in range(4):
                P.mm(pv[:, 0:16], onesM[:], sqs[:, c, :], c == 0, c == 3, [R_const, R["sqs"]], [Rpv], sig=(c == 3))
            P.act(rstd_s[:], pv[:, 0:16], AF.Ln, [Rpv], [R["rstd"]], bias=EPS)
            P.act(rstd_s[:], rstd_s[:], AF.Exp, [R["rstd"]], [R["rstd"]], scale=-0.5)
            for c in range(4):
                P.tt("dve", accs[:, c, :], accs[:, c, :], rstd_s[:], ALU.mult, [R["accs"], R["rstd"]], [R["accs"]])
                P.act(s_sT[:, c, :], accs[:, c, :], AF.Silu, [R["accs"], R_const], [R_ssT],
                      bias=cvec_sb[:, 8 + c:9 + c], scale=cvec_sb[:, 4 + c:5 + c])

            P.barrier()
            s1_es.close()
            return dict(locals())

        def sample_S2(SB):
            g_ = SB
            (PB, NBT, NW, R, Qblk, knT, vnb, idx, mnew, s_sT, o_sT, R_ssT, R_osT,
             R_knT, R_vnb, s_kp, s_vp, s_o32, s_so) = [g_[k] for k in (
                "PB", "NBT", "NW", "R", "Qblk", "knT", "vnb", "idx", "mnew",
                "s_sT", "o_sT", "R_ssT", "R_osT", "R_knT", "R_vnb", "s_kp", "s_vp", "s_o32", "s_so")]
            R_kpg, R_vpg, R_kTp, R_aTs = g_["R_kpg"], g_["R_vpg"], g_["R_kTp"], g_["R_aTs"]
            s2_es = ExitStack()
            sb2 = lambda name, shape, dtype: sb("S2_" + name, shape, dtype, s2_es)
            kpg = [sb2("kpg%d" % i, [128, PB, 512], BF16) for i in range(2)]
            vpg = [sb2("vpg%d" % i, [128, PB, 512], BF16) for i in range(2)]
            kTp = [sb2("kTp%d" % i, [128, 512], BF16) for i in range(4)]
            Es = sb2("Es", [128, NW], F32)
            Lbs = sb2("Lbs", [128, PB, 32], F32)
            Lsuf = sb2("Lsuf", [128, PB + 1, 32], F32)
            aTs = [sb2("aTs%d" % i, [128, NW], BF16) for i in range(2)]
            En = sb2("En", [4, 32], F32)
            Lnw = sb2("Lnw", [4, 32], F32)
            an = sb2("an", [4, 32], BF16)
            o32s = sb2("o32s", [32, 512], F32)
            o_tok = sb2("o_tok", [16, 512], F32)
            o_tb = sb2("o_tb", [16, 512], BF16)
            negUf = cst_f[:, 128:256]
            cnt = 0
            rk = 0
            o32, Ro32 = ps[2], R_ps[2]
            zn, Rzn = ps[3], R_ps[3]
            for b in range(4):
                for c in range(4):
                    P.mm(zn[0:4, 0:32], knT[b][:, c, :], Qblk[:, c, b, :], c == 0, False, [R_knT[b], R["Qblk"]], [Rzn])
                P.mm(zn[0:4, 0:32], cf2[0:1, 128:132], brow_sb[0:1, 0:32], False, True, [R_const], [Rzn], sig=True)
                P.act(En[:], zn[0:4, 0:32], AF.Exp, [Rzn], [R["En"]])
                P.act(Lnw[:], En[:], AF.Ln, [R["En"]], [R["Lnw"]], bias=1.0)
                P.tt("dve", Lnw[:], Lnw[:], mnew, ALU.mult, [R["Lnw"], R_const], [R["Lnw"]])
                P.mm(zn[0:4, 0:32], cst_f[0:4, 128:132], Lnw[:], False, True, [R_const, R["Lnw"]], [Rzn], sig=True)
                P.act(an[:], zn[0:4, 0:32], AF.Exp, [Rzn], [R["an"]])
                P.tt("dve", an[:], an[:], mnew, ALU.mult, [R["an"], R_const], [R["an"]])
                P.memset("pool", Lsuf[:, PB, :], 0.0, [R["Lsuf"]])
                P.copy("pool", Lsuf[0:4, PB, :], Lnw[:], [R["Lnw"], R["Lsuf"]], [R["Lsuf"]])
                P.mm(o32[0:32, :], an[:], vnb[b][:], True, False, [R["an"], R_vnb[b]], [Ro32], sig=True)
                for nb in reversed(range(NBT)):
                    buf = cnt % 2
                    zb, Rz = ps[cnt % 2], R_ps[cnt % 2]
                    cnt += 1
                    for pi in range(PB):
                        j = b * NPG + nb * PB + pi
                        P.dma("pool", kpg[buf][:, pi, :], ck, s_kp[buf], reads=[R["idx"]], writes=[R_kpg[buf]],
                              indirect=idx[:, j:j + 1])
                        P.dma("pool", vpg[buf][:, pi, :], cv, s_vp[buf], reads=[R["idx"]], writes=[R_vpg[buf]],
                              indirect=idx[:, j:j + 1])
                    P.mm(zb[:, 0:NW], cf2[0:1, 128:256], brow_sb[0:1, 0:NW], True, False, [R_const], [Rz], sig=True)
                    for pi in range(PB):
                        pt_, Rpt = next_psT()
                        for c in range(4):
                            P.tr(pt_[:, c * 128:(c + 1) * 128], kpg[buf][:, pi, c * 128:(c + 1) * 128], identb[:],
                                 [R_kpg[buf], R_const], [Rpt], sig=(c == 3))
                        r = rk % 4
                        rk += 1
                        P.copy("act" if r % 2 else "dve", kTp[r][:], pt_[:, 0:512], [Rpt], [R_kTp[r]])
                        for c in range(4):
                            P.mm(zb[:, pi * 32:(pi + 1) * 32], kTp[r][:, c * 128:(c + 1) * 128], Qblk[:, c, b, :], False, False,
                                 [R_kTp[r], R["Qblk"]], [Rz], sig=(c == 3))
                    P.act(Es[:], zb[:, 0:NW], AF.Exp, [Rz], [R["Es"]])
                    P.act(Lbs[:].rearrange("p a b -> p (a b)"), Es[:], AF.Ln, [R["Es"]], [R["Lbs"]], bias=1.0)
                    for pi in reversed(range(PB)):
                        P.tt("dve", Lsuf[:, pi, :], Lsuf[:, pi + 1, :], Lbs[:, pi, :], ALU.add, [R["Lsuf"], R["Lbs"]], [R["Lsuf"]])
                    P.mm(zb[:, 0:NW], negUf, Lbs[:].rearrange("p a b -> p (a b)"), False, False, [R_const, R["Lbs"]], [Rz])
                    P.mm(zb[:, 0:NW], cf2[:, 0:128], Lsuf[:, 1:PB + 1, :].rearrange("p a b -> p (a b)"), False, True,
                         [R_const, R["Lsuf"]], [Rz], sig=True)
                    P.act(aTs[buf][:], zb[:, 0:NW], AF.Exp, [Rz], [R_aTs[buf]])
                    P.copy("dve", Lsuf[:, PB, :], Lsuf[:, 0, :], [R["Lsuf"]], [R["Lsuf"]])
                    for pi in range(PB):
                        P.mm(o32[0:32, :], aTs[buf][:, pi * 32:(pi + 1) * 32], vpg[buf][:, pi, :], False,
                             (nb == 0 and pi == PB - 1), [R_aTs[buf], R_vpg[buf]], [Ro32], sig=(pi == PB - 1))
                P.copy("dve", o32s[:], o32[0:32, :], [Ro32], [R["o32s"]])
                for h in range(H):
                    P.dma("sp", o_tok[b * 4:(b + 1) * 4, h * 64:(h + 1) * 64], o32s[h * 4:(h + 1) * 4, h * 64:(h + 1) * 64],
                          s_o32, reads=[R["o32s"]], writes=[R["otok"]])
            P.copy("pool", o_tb[:], o_tok[:], [R["otok"]], [R["otb"]])
            if DEBUG:
                P.dma("sp", ys[:, 0:512], o_tok[:], s_so, reads=[R["otok"]])
            pt_, Rpt = next_psT()
            for c in range(4):
                P.tr(pt_[:, c * 128:c * 128 + 16], o_tb[:, c * 128:(c + 1) * 128], identb[0:16, 0:16],
                     [R["otb"], R_const], [Rpt], sig=(c == 3))
            P.copy("dve", o_sT[:], pt_[:, 0:512].rearrange("p (k n) -> p k n", k=4)[:, :, 0:16], [Rpt], [R_osT])
            P.barrier()
            s2_es.close()
            return s_sT, o_sT, R_ssT, R_osT

        R_out = Res("outs")
        out_sems = []

        for seq in range(2):
            seq_es = ExitStack()
            sbs = lambda name, shape, dtype: sb("%s_q%d" % (name, seq), shape, dtype, seq_es)
            sT = sbs("sT", [128, 4, S], BF16)
            oT = sbs("oT", [128, 4, S], BF16)
            do_s = with_sample and seq == 1
            if do_s:
                SBP = sample_alloc()
            R_sT = [Res("sT%d" % g) for g in range(NG)]
            R_oT = [Res("oT%d" % g) for g in range(NG)]

            abc_es = ExitStack()
            sba = lambda name, shape, dtype: sb("%s_q%d" % (name, seq), shape, dtype, abc_es)
            qT = sba("qT", [128, 4, S], BF16)
            kT = sba("kT", [128, 4, S], BF16)
            vv = sba("vv", [128, NT, 512], BF16)
            uT = sba("uT", [128, 4, 32 + S], BF16)
            R_qT = [Res("qT%d" % g) for g in range(NG)]
            R_kT = [Res("kT%d" % t) for t in range(NT)]
            R_vv = [Res("vv%d" % t) for t in range(NT)]
            R_uT = [Res("uT%d" % g) for g in range(NG)]
            R_uh = Res("uThist")
            P.memset("pool", uT[:, :, 0:32], 0.0, [R_uh])

            wa_es = ExitStack()
            winA = sb("winA_q%d" % seq, [128, 8, 2560], BF16, wa_es)
            a_es = ExitStack()
            sbA = lambda name, shape, dtype: sb("%s_q%d" % (name, seq), shape, dtype, a_es)
            R_winA.w = None
            R_winA.r = []
            if seq == 0:
                for k in range(8):
                    for cb in range(5):
                        P.dma("pool", winA[:, k, cb * 512:(cb + 1) * 512],
                              W["w_in"][k * 128:(k + 1) * 128, cb * 512:(cb + 1) * 512], s_wa, writes=[R_winA])
                P.dma("sp", wa_scr, winA[:].rearrange("p k n -> p (k n)"), s_was, reads=[R_winA], writes=[R_wascr])
            else:
                P.dma("sp", winA[:].rearrange("p k n -> p (k n)"), wa_scr, s_wa, reads=[R_wascr], writes=[R_winA])
            xst = [sbA("xst%d" % i, [128, D], F32) for i in range(2)]
            xb = [sbA("xb%d" % i, [128, D], BF16) for i in range(2)]
            xT = [sbA("xT%d" % i, [128, 8, 512], BF16) for i in range(1)] * 2
            kst = [sbA("kst%d" % i, [128, 512], F32) for i in range(1)] * 2
            vst = [sbA("vst%d" % i, [128, 512], F32) for i in range(1)] * 2
            kb = [sbA("kb%d" % i, [128, 512], BF16) for i in range(2)]
            sg = [sbA("sg%d" % i, [128, 512], F32) for i in range(2)]
            ust = sbA("ust", [128, 512], F32)
            R_xst = [Res("xst") for _ in range(2)]
            R_xb = [Res("xb") for _ in range(2)]
            R_xT = [Res("xT")] * 2
            R_kst = [Res("kst")] * 2
            R_vst = [Res("vst")] * 2
            R_kb = [Res("kb") for _ in range(2)]
            R_sg = [Res("sg") for _ in range(2)]
            R_ust = Res("ust")
            s_x = [P.newsem("s_x%d_%d" % (seq, i)) for i in range(2)]
            s_ko = [P.newsem("s_ko%d_%d" % (seq, i)) for i in range(2)]
            s_vo = [P.newsem("s_vo%d_%d" % (seq, i)) for i in range(2)]
            s_co = P.newsem("s_co%d" % seq)
            out_sems += s_ko + s_vo + [s_co]

            for g in range(NG):
                xTg, RxTg = xT[g % 2], R_xT[g % 2]
                for t4 in range(4):
                    t = g * 4 + t4
                    b = t % 2
                    row0 = seq * S + t * 128
                    P.dma("sp", xst[b][:], xp[row0:row0 + 128, :], s_x[b], writes=[R_xst[b]])
                    P.copy("pool", xb[b][:], xst[b][:], [R_xst[b]], [R_xb[b]])
                    pt_, Rpt = next_psT()
                    for k in range(8):
                        P.tr(pt_[:, k * 128:(k + 1) * 128], xb[b][:, k * 128:(k + 1) * 128], identb[:],
                             [R_xb[b], R_const], [Rpt], sig=(k == 7))
                    P.copy("dve", xTg[:, :, t4 * 128:(t4 + 1) * 128], pt_[:].rearrange("p (k n) -> p k n", k=8),
                           [Rpt], [RxTg])
                    for which in range(2):
                        pk, Rpk = next_ps()
                        c0 = 1536 + which * 512
                        for k in range(8):
                            P.mm(pk[:], xTg[:, k, t4 * 128:(t4 + 1) * 128], winA[:, k, c0:c0 + 512], k == 0, k == 7,
                                 [RxTg, R_winA], [Rpk], sig=(k == 7))
                        if which == 0:
                            P.copy("dve", kst[b][:], pk[:], [Rpk], [R_kst[b]])
                            P.dma("sp", kp[row0:row0 + 128, :], kst[b][:], s_ko[b], reads=[R_kst[b]])
                            P.copy("pool", kb[b][:], kst[b][:], [R_kst[b]], [R_kb[b]])
                            pt2, Rpt2 = next_psT()
                            for c in range(4):
                                P.tr(pt2[:, c * 128:(c + 1) * 128], kb[b][:, c * 128:(c + 1) * 128], identb[:],
                                     [R_kb[b], R_const], [Rpt2], sig=(c == 3))
                            P.copy("act", kT[:, :, t * 128:(t + 1) * 128],
                                   pt2[:, 0:512].rearrange("p (k n) -> p k n", k=4), [Rpt2], [R_kT[t]])
                        else:
                            P.copy("act", vst[b][:], pk[:], [Rpk], [R_vst[b]])
                            P.dma("sp", vp[row0:row0 + 128, :], vst[b][:], s_vo[b], reads=[R_vst[b]])
                            P.copy("pool", vv[:, t, :], vst[b][:], [R_vst[b]], [R_vv[t]])
                    if t == NT - 1:
                        pa, Rpa = next_ps()
                        pb, Rpb = next_ps()
                        for k in range(8):
                            P.mm(pa[:], xTg[:, k, t4 * 128:(t4 + 1) * 128], winA[:, k, 0:512], k == 0, k == 7,
                                 [RxTg, R_winA], [Rpa], sig=(k == 7))
                        for k in range(8):
                            P.mm(pb[:], xTg[:, k, t4 * 128:(t4 + 1) * 128], winA[:, k, 512:1024], k == 0, k == 7,
                                 [RxTg, R_winA], [Rpb], sig=(k == 7))
                        P.act(sg[0][:], pb[:], AF.Sigmoid, [Rpb], [R_sg[0]])
                        P.tt("dve", ust[:], pa[:], sg[0][:], ALU.mult, [Rpa, R_sg[0]], [R_ust])
                        P.dma("sp", cpo[seq * 30:(seq + 1) * 30, :], ust[98:128, :], s_co, reads=[R_ust])
                gs = slice(g * 512, (g + 1) * 512)
                for c in range(4):
                    pq, Rpq = next_ps()
                    for k in range(8):
                        P.mm(pq[:], winA[:, k, 1024 + c * 128:1024 + (c + 1) * 128], xTg[:, k, :], k == 0, k == 7,
                             [RxTg, R_winA], [Rpq], sig=(k == 7))
                    P.act(qT[:, c, gs], pq[:], AF.Copy, [Rpq], [R_qT[g]], scale=0.125)
                for c in range(4):
                    pa, Rpa = next_ps()
                    pb, Rpb = next_ps()
                    for k in range(8):
                        P.mm(pb[:], winA[:, k, 512 + c * 128:512 + (c + 1) * 128], xTg[:, k, :], k == 0, k == 7,
                             [RxTg, R_winA], [Rpb], sig=(k == 7))
                    for k in range(8):
                        P.mm(pa[:], winA[:, k, c * 128:(c + 1) * 128], xTg[:, k, :], k == 0, k == 7,
                             [RxTg, R_winA], [Rpa], sig=(k == 7))
                    P.act(sg[c % 2][:], pb[:], AF.Sigmoid, [Rpb], [R_sg[c % 2]])
                    P.tt("dve", uT[:, c, 32 + g * 512:32 + (g + 1) * 512], pa[:], sg[c % 2][:], ALU.mult,
                         [Rpa, R_sg[c % 2]], [R_uT[g]])
            P.barrier()
            a_es.close()
            if do_s:
                SB = sample_S1(winA, SBP)
            wa_es.close()

            b_es = ExitStack()
            sbB = lambda name, shape, dtype: sb("%s_q%d" % (name, seq), shape, dtype, b_es)
            acc = [sbB("acc%d" % i, [128, 512], F32) for i in range(4)]
            acp = [sbB("acp%d" % i, [128, 512], F32) for i in range(4)]
            cbf = [sbB("cbf%d" % i, [128, 512], BF16) for i in range(4)]
            sqb = [sbB("sqb%d" % i, [128, 512], BF16) for i in range(4)]
            rstd = sbB("rstd", [128, 512], F32)
            R_acc = [Res("acc") for _ in range(4)]
            R_acp = [Res("acp") for _ in range(4)]
            R_cbf = [Res("cbf") for _ in range(4)]
            R_sqb = [Res("sqb") for _ in range(4)]
            R_rstd = Res("rstd")
            ND = CW - NTAP_POOL
            for g in range(NG):
                base = 2 + g * 512
                rd = [R_uT[g], R_uh] + ([R_uT[g - 1]] if g > 0 else [])
                for w in range(ND):
                    for c in range(4):
                        src = uT[:, c, base + w:base + w + 512]
                        wv = cw_sb[:, c * CW + w:c * CW + w + 1]
                        if w == 0:
                            P.ts("dve", acc[c][:], src, wv, cvec_sb[:, c:c + 1], ALU.mult, ALU.add,
                                 rd + [R_const], [R_acc[c]])
                        else:
                            P.stt("dve", acc[c][:], src, wv, acc[c][:], ALU.mult, ALU.add,
                                  rd + [R_const, R_acc[c]], [R_acc[c]])
                for w in range(ND, CW):
                    for c in range(4):
                        src = uT[:, c, base + w:base + w + 512]
                        wv = cw_sb[:, c * CW + w:c * CW + w + 1]
                        if w == ND:
                            P.ts("pool", acp[c][:], src, wv, None, ALU.mult, ALU.bypass, rd + [R_const], [R_acp[c]])
                        else:
                            P.stt("pool", acp[c][:], src, wv, acp[c][:], ALU.mult, ALU.add,
                                  rd + [R_const, R_acp[c]], [R_acp[c]])
                pm, Rpm = next_ps()
                for c in range(4):
                    if NTAP_POOL > 0:
                        P.tt("dve", acc[c][:], acc[c][:], acp[c][:], ALU.add, [R_acc[c], R_acp[c]], [R_acc[c]])
                    P.copy("pool", cbf[c][:], acc[c][:], [R_acc[c]], [R_cbf[c]])
                for c in range(4):
                    P.mm(pm[:], onesM[:], cbf[c][:], c == 0, c == 3, [R_const, R_cbf[c]], [Rpm], sig=(c == 3))
                pv, Rpv = next_ps()
                for c in range(4):
                    P.tt("dve", acc[c][:], acc[c][:], pm[:], ALU.subtract, [R_acc[c], Rpm], [R_acc[c]])
                    P.act(sqb[c][:], acc[c][:], AF.Square, [R_acc[c]], [R_sqb[c]])
                for c in range(4):
                    P.mm(pv[:], onesM[:], sqb[c][:], c == 0, c == 3, [R_const, R_sqb[c]], [Rpv], sig=(c == 3))
                P.act(rstd[:], pv[:], AF.Ln, [Rpv], [R_rstd], bias=EPS)
                P.act(rstd[:], rstd[:], AF.Exp, [R_rstd], [R_rstd], scale=-0.5)
                for c in range(4):
                    P.tt("dve", acc[c][:], acc[c][:], rstd[:], ALU.mult, [R_acc[c], R_rstd], [R_acc[c]])
                    P.act(sT[:, c, g * 512:(g + 1) * 512], acc[c][:], AF.Silu, [R_acc[c], R_const], [R_sT[g]],
                          bias=cvec_sb[:, 8 + c:9 + c], scale=cvec_sb[:, 4 + c:5 + c])
            P.barrier()
            b_es.close()

            c_es = ExitStack()
            sbC = lambda name, shape, dtype: sb("%s_q%d" % (name, seq), shape, dtype, c_es)
            NB = 4
            Ef = [sbC("Ef%d" % i, [128, 512], F32) for i in range(NB)]
            Lb = [sbC("Lb%d" % i, [128, 512], BF16) for i in range(NB)]
            aT = [sbC("aT%d" % i, [128, 512], BF16) for i in range(NB)]
            Lsum = [sbC("Lsum%d" % i, [128, 512], BF16) for i in range(2)]
            R_Ef = [Res("Ef") for _ in range(NB)]
            R_Lb = [Res("Lb") for _ in range(NB)]
            R_aT = [Res("aT") for _ in range(NB)]
            R_Lsum = [Res("Lsum") for _ in range(2)]
            zbank = [(ps[i], R_ps[i]) for i in range(4)]
            obank = [(ps[4], R_ps[4]), (ps[5], R_ps[5])]
            units = []
            hc = 0
            for c in range(NG):
                for hp in range(4):
                    for hh in range(2):
                        h = hp * 2 + hh
                        nkb = 4 * c + 4
                        for ui, kbk in enumerate(range(nkb - 1, -1, -1)):
                            i_ = kbk - 4 * c
                            col0 = max(i_, 0) * 128
                            units.append(dict(h=h, c=c, kb=kbk, diag=(i_ >= 0), col0=col0, first=(ui == 0),
                                              last=(kbk == 0), hc=hc, ob=(c * 4 + hp) % 2))
                        hc += 1
            NU = len(units)

            def P1(u):
                U = units[u]
                zb, Rz = zbank[u % 4]
                h, c, kbk, col0 = U["h"], U["c"], U["kb"], U["col0"]
                p0 = (h % 2) * 64
                P.mm(zb[:, col0:512], kT[p0:p0 + 64, h // 2, kbk * 128:(kbk + 1) * 128],
                     qT[p0:p0 + 64, h // 2, c * 512 + col0:(c + 1) * 512], True, True,
                     [R_kT[kbk], R_qT[c]], [Rz], sig=True)

            def A12(u):
                U = units[u]
                zb, Rz = zbank[u % 4]
                h, col0 = U["h"], U["col0"]
                b = u % NB
                P.act(Ef[b][:, col0:512], zb[:, col0:512], AF.Exp, [Rz, R_const], [R_Ef[b]], bias=biasT[:, h:h + 1])
                P.act(Lb[b][:, col0:512], Ef[b][:, col0:512], AF.Ln, [R_Ef[b]], [R_Lb[b]], bias=1.0)
                if U["diag"]:
                    P.tt("dve", Lb[b][:, col0:col0 + 128], Lb[b][:, col0:col0 + 128], tri[:], ALU.mult,
                         [R_Lb[b], R_const], [R_Lb[b]])

            def P2G(u):
                U = units[u]
                zb, Rz = zbank[u % 4]
                col0 = U["col0"]
                b = u % NB
                ls = U["hc"] % 2
                P.mm(zb[:, col0:512], negU[:], Lb[b][:, col0:512], False, U["first"], [R_const, R_Lb[b]], [Rz],
                     sig=U["first"])
                if not U["first"]:
                    P.mm(zb[:, col0:512], negOnes[:], Lsum[ls][:, col0:512], False, True, [R_const, R_Lsum[ls]], [Rz],
                         sig=True)
                if not U["last"]:
                    if U["first"]:
                        P.memset("pool", Lsum[ls][:], 0.0, [R_Lsum[ls]])
                    P.tt("pool", Lsum[ls][:, col0:512], Lsum[ls][:, col0:512], Lb[b][:, col0:512], ALU.add,
                         [R_Lsum[ls], R_Lb[b]], [R_Lsum[ls]])

            def A3(u):
                U = units[u]
                zb, Rz = zbank[u % 4]
                h, col0 = U["h"], U["col0"]
                b = u % NB
                P.act(aT[b][:, col0:512], zb[:, col0:512], AF.Exp, [Rz, R_const], [R_aT[b]], bias=biasT[:, h:h + 1])
                if U["diag"]:
                    P.tt("dve", aT[b][:, col0:col0 + 128], aT[b][:, col0:col0 + 128], tri[:], ALU.mult,
                         [R_aT[b], R_const], [R_aT[b]])

            def P3(u):
                U = units[u]
                h, c, kbk, col0 = U["h"], U["c"], U["kb"], U["col0"]
                b = u % NB
                ob, Rob = obank[U["ob"]]
                p0 = (h % 2) * 64
                if U["first"]:
                    P.mm(ob[p0:p0 + 64, :], zer[:, 0:64], zer[:, :], True, False, [R_const], [Rob])
                P.mm(ob[p0:p0 + 64, col0:512], vv[:, kbk, h * 64:(h + 1) * 64], aT[b][:, col0:512], False, U["last"],
                     [R_vv[kbk], R_aT[b]], [Rob], sig=True)
                if U["last"]:
                    P.copy("dve", oT[p0:p0 + 64, h // 2, c * 512:(c + 1) * 512], ob[p0:p0 + 64, :], [Rob], [R_oT[c]])

            for p in range(-1, NU + 2):
                if 0 <= p - 2 < NU:
                    P3(p - 2)
                if 0 <= p + 1 < NU:
                    P1(p + 1)
                if 0 <= p < NU:
                    A12(p)
                    P2G(p)
                if 0 <= p - 1 < NU:
                    A3(p - 1)
            P.barrier()
            c_es.close()
            abc_es.close()

            if do_s:
                s_sT, o_sT, R_ssT, R_osT = sample_S2(SB)
            d_es = ExitStack()
            sbD = lambda name, shape, dtype: sb("%s_q%d" % (name, seq), shape, dtype, d_es)
            ring = sbD("ring", [128, NSLOT, SLOTW], BF16)
            WS.buf = ring
            NTL = 5 if do_s else 4
            WD = 528 if do_s else 512
            xg = sbD("xg", [128, NTL, D], F32)
            xbD = [sbD("xbD%d" % i, [128, D], BF16) for i in range(2)]
            xTd = sbD("xTd", [128, 8, WD], BF16)
            actT = sbD("actT", [128, NJ, WD], BF16)
            mixT = actT
            pst_ = [sbD("pst%d" % i, [128, PLE], F32) for i in range(2)]
            pbD = [sbD("pbD%d" % i, [128, PLE], BF16) for i in range(2)]
            pT = sbD("pT", [128, 2, WD], BF16)
            sgD = [sbD("sgD%d" % i, [128, 512], F32) for i in range(3)]
            stat = sbD("stat", [128, NTL, 16], F32)
            R_xg = [Res("xg%d" % i) for i in range(NTL)]
            R_xbD = [Res("xbD") for _ in range(2)]
            R_xTd = Res("xTd")
            R_actT = Res("actT")
            R_pst = [Res("pst") for _ in range(2)]
            R_pbD = [Res("pbD") for _ in range(2)]
            R_pT = Res("pT")
            R_sgD = [Res("sgD") for _ in range(3)]
            R_stat = Res("stat")
            s_xg = [P.newsem("s_xg%d_%d" % (seq, i)) for i in range(NTL)]
            s_pl = [P.newsem("s_pl%d_%d" % (seq, i)) for i in range(2)]
            s_yo = [P.newsem("s_yo%d_%d" % (seq, i)) for i in range(NTL)]
            out_sems += s_yo
            for r_ in WS.res:
                r_.w = None
                r_.r = []
            WS.limit = (seq + 1) * NG * NSL
            WS.prefetch()
            sgi = 0

            def layer_norm(t4, npt, gcol, bcol):
                xt = xg[0:npt, t4, :]
                for hf in range(2):
                    P.bn_stats(stat[0:npt, t4, hf * 6:hf * 6 + 6], xg[0:npt, t4, hf * 512:(hf + 1) * 512], [R_xg[t4]], [R_stat])
                P.bn_aggr(stat[0:npt, t4, 12:14], stat[0:npt, t4, 0:12], [R_stat], [R_stat])
                P.act(stat[0:npt, t4, 14:15], stat[0:npt, t4, 13:14], AF.Ln, [R_stat], [R_stat], bias=EPS)
                P.act(stat[0:npt, t4, 14:15], stat[0:npt, t4, 14:15], AF.Exp, [R_stat], [R_stat], scale=-0.5)
                P.ts("dve", xt, xt, stat[0:npt, t4, 12:13], stat[0:npt, t4, 14:15], ALU.subtract, ALU.mult, [R_xg[t4], R_stat], [R_xg[t4]])
                P.tt("pool", xt, xt, lnb[0:npt, gcol, :], ALU.mult, [R_xg[t4], R_const], [R_xg[t4]])
                P.tt("pool", xt, xt, lnb[0:npt, bcol, :], ALU.add, [R_xg[t4], R_const], [R_xg[t4]])

            def to_T(t4, npt, tsl, src_rows, width, dstT, R_dst, stage, R_stage, eng_copy):
                nk = width // 128
                P.copy("pool", stage[0:npt, :], src_rows, R_stage[0], R_stage[1])
                pt_, Rpt = next_psT()
                for k in range(nk):
                    P.tr(pt_[:, k * 128:k * 128 + npt], stage[0:npt, k * 128:(k + 1) * 128], identb[0:npt, 0:npt],
                         R_stage[1] + [R_const], [Rpt], sig=(k == nk - 1))
                P.copy(eng_copy, dstT[:, :, tsl], pt_[:, 0:nk * 128].rearrange("p (k n) -> p k n", k=nk)[:, :, 0:npt],
                       [Rpt], [R_dst])

            for g in range(NG):
                gs = slice(g * 512, (g + 1) * 512)
                last = do_s and g == NG - 1
                tiles = [(t4, 128, slice(t4 * 128, (t4 + 1) * 128)) for t4 in range(4)]
                cgs = [(slice(0, 512), 512, 0)]
                if last:
                    tiles.append((4, 16, slice(512, 528)))
                    cgs.append((slice(512, 528), 16, 1))
                for (t4, npt, tsl) in tiles:
                    b = t4 % 2
                    if t4 < 4:
                        row0 = seq * S + (g * 4 + t4) * 128
                        P.dma("sp", xg[:, t4, :], xp[row0:row0 + 128, :], s_xg[t4], writes=[R_xg[t4]])
                    else:
                        P.dma("sp", xg[0:16, t4, :], xs[:, :], s_xg[t4], writes=[R_xg[t4]])
                    to_T(t4, npt, tsl, xg[0:npt, t4, :], D, xTd, R_xTd, xbD[b], ([R_xg[t4]], [R_xbD[b]]), "dve")
                for mh in range(2):
                    wP, RwP = WS.get(0)
                    wC, RwC = WS.get(1)
                    wA, RwA = WS.get(2)
                    for m4 in range(4):
                        m = mh * 4 + m4
                        cs = slice(m4 * 128, (m4 + 1) * 128)
                        for (csl, n, isS) in cgs:
                            sTs = s_sT if isS else sT
                            oTs = o_sT if isS else oT
                            Rs_ = R_ssT if isS else R_sT[g]
                            Ro_ = R_osT if isS else R_oT[g]
                            ssl = slice(0, 16) if isS else gs
                            pc, Rpc = next_ps()
                            for k in range(8):
                                P.mm(pc[:, 0:n], wC[:, k * 512:(k + 1) * 512][:, cs], xTd[:, k, csl], k == 0, k == 7,
                                     [RwC, R_xTd], [Rpc], sig=(k == 7))
                            s1 = sgi % 3; sgi += 1
                            P.act(sgD[s1][:, 0:n], pc[:, 0:n], AF.Sigmoid, [Rpc], [R_sgD[s1]])
                            pa, Rpa = next_ps()
                            for k in range(8):
                                P.mm(pa[:, 0:n], wA[:, k * 512:(k + 1) * 512][:, cs], xTd[:, k, csl], k == 0, k == 7,
                                     [RwA, R_xTd], [Rpa], sig=(k == 7))
                            s2 = sgi % 3; sgi += 1
                            P.act(sgD[s2][:, 0:n], pa[:, 0:n], AF.Sigmoid, [Rpa], [R_sgD[s2]])
                            po, Rpo = next_ps()
                            for k in range(4):
                                P.mm(po[:, 0:n], wP[:, k * 512:(k + 1) * 512][:, cs], sTs[:, k, ssl], k == 0, k == 3,
                                     [RwP, Rs_], [Rpo], sig=(k == 3))
                            P.tt("dve", sgD[s1][:, 0:n], sgD[s1][:, 0:n], po[:, 0:n], ALU.mult, [R_sgD[s1], Rpo], [R_sgD[s1]])
                            po2, Rpo2 = next_ps()
                            for k in range(4):
                                P.mm(po2[:, 0:n], wP[:, (4 + k) * 512:(5 + k) * 512][:, cs], oTs[:, k, ssl], k == 0, k == 3,
                                     [RwP, Ro_], [Rpo2], sig=(k == 3))
                            P.tt("dve", sgD[s2][:, 0:n], sgD[s2][:, 0:n], po2[:, 0:n], ALU.mult, [R_sgD[s2], Rpo2], [R_sgD[s2]])
                            P.tt("pool", mixT[:, m, csl], sgD[s1][:, 0:n], sgD[s2][:, 0:n], ALU.add, [R_sgD[s1], R_sgD[s2]], [R_actT])
                    WS.done(); WS.done(); WS.done()
                for hf in range(2):
                    wO, RwO = WS.get()
                    for (t4, npt, tsl) in tiles:
                        pw, Rpw = next_ps()
                        for k in range(8):
                            P.mm(pw[0:npt, :], mixT[:, k, tsl], wO[:, k * 512:(k + 1) * 512], k == 0, k == 7,
                                 [R_actT, RwO], [Rpw], sig=(k == 7))
                        P.stt("dve", xg[0:npt, t4, hf * 512:(hf + 1) * 512], xg[0:npt, t4, hf * 512:(hf + 1) * 512], ALPHA, pw[0:npt, :],
                              ALU.mult, ALU.add, [R_xg[t4], Rpw], [R_xg[t4]])
                    WS.done()
                for (t4, npt, tsl) in tiles:
                    layer_norm(t4, npt, 0, 1)
                    b = t4 % 2
                    to_T(t4, npt, tsl, xg[0:npt, t4, :], D, xTd, R_xTd, xbD[b], ([R_xg[t4]], [R_xbD[b]]), "dve")
                    if t4 < 4:
                        row0 = seq * S + (g * 4 + t4) * 128
                        P.dma("sp", pst_[b][:], pp[row0:row0 + 128, :], s_pl[b], writes=[R_pst[b]])
                    else:
                        P.dma("sp", pst_[b][0:16, :], pps[:, :], s_pl[b], writes=[R_pst[b]])
                    to_T(t4, npt, tsl, pst_[b][0:npt, :], PLE, pT, R_pT, pbD[b], ([R_pst[b]], [R_pbD[b]]), "act")
                for jp in range(NJ // 2):
                    wU, RwU = WS.get()
                    for jj in range(2):
                        j = jp * 2 + jj
                        for (csl, n, isS) in cgs:
                            pg_, Rpg = next_ps()
                            pu, Rpu = next_ps()
                            for k in range(8):
                                o_ = (jj * 16 + k * 2) * 128
                                P.mm(pg_[:, 0:n], wU[:, o_:o_ + 128], xTd[:, k, csl], k == 0, k == 7, [RwU, R_xTd], [Rpg], sig=(k == 7))
                            for k in range(8):
                                o_ = (jj * 16 + k * 2 + 1) * 128
                                P.mm(pu[:, 0:n], wU[:, o_:o_ + 128], xTd[:, k, csl], k == 0, k == 7, [RwU, R_xTd], [Rpu], sig=(k == 7))
                            s1 = sgi % 3; sgi += 1
                            P.act(sgD[s1][:, 0:n], pg_[:, 0:n], AF.Silu, [Rpg], [R_sgD[s1]])
                            P.tt("dve", actT[:, j, csl], sgD[s1][:, 0:n], pu[:, 0:n], ALU.mult, [R_sgD[s1], Rpu], [R_actT])
                    WS.done()
                fbank = [0, 1, 2, 3, 4]
                for hf in range(2):
                    hs = slice(hf * 512, (hf + 1) * 512)
                    wG, RwG = WS.get(0)
                    wL, RwL = WS.get(1)
                    for (t4, npt, tsl) in tiles:
                        pgt, Rpgt = next_ps()
                        for k in range(8):
                            P.mm(pgt[0:npt, :], xTd[:, k, tsl], wG[:, k * 512:(k + 1) * 512], k == 0, k == 7,
                                 [R_xTd, RwG], [Rpgt], sig=(k == 7))
                        s1 = sgi % 3; sgi += 1
                        P.act(sgD[s1][0:npt, :], pgt[0:npt, :], AF.Sigmoid, [Rpgt], [R_sgD[s1]])
                        ppl, Rppl = next_ps()
                        for k in range(2):
                            P.mm(ppl[0:npt, :], pT[:, k, tsl], wL[:, k * 512:(k + 1) * 512], k == 0, k == 1,
                                 [R_pT, RwL], [Rppl], sig=(k == 1))
                        P.tt("dve", sgD[s1][0:npt, :], sgD[s1][0:npt, :], ppl[0:npt, :], ALU.mult, [R_sgD[s1], Rppl], [R_sgD[s1]])
                        P.stt("dve", xg[0:npt, t4, hs], xg[0:npt, t4, hs], ALPHA, sgD[s1][0:npt, :], ALU.mult, ALU.add,
                              [R_xg[t4], R_sgD[s1]], [R_xg[t4]])
                    WS.done()
                    kblk = 0
                    for si_ in range(3):
                        if si_ == 0:
                            wD, RwD, offs = wL, RwL, list(range(2, 8))
                        else:
                            wD, RwD = WS.get()
                            offs = list(range(8))
                        for (t4, npt, tsl) in tiles:
                            fb = fbank[t4]
                            for oi, o_ in enumerate(offs):
                                kk = kblk + oi
                                P.mm(ps[fb][0:npt, :], actT[:, kk, tsl], wD[:, o_ * 512:(o_ + 1) * 512], kk == 0, kk == NJ - 1,
                                     [R_actT, RwD], [R_ps[fb]], sig=(oi == len(offs) - 1))
                        kblk += len(offs)
                        WS.done()
                    for (t4, npt, tsl) in tiles:
                        fb = fbank[t4]
                        P.tt("dve", xg[0:npt, t4, hs], xg[0:npt, t4, hs], ps[fb][0:npt, :], ALU.add, [R_xg[t4], R_ps[fb]], [R_xg[t4]])
                for (t4, npt, tsl) in tiles:
                    layer_norm(t4, npt, 2, 3)
                    if t4 < 4:
                        row0 = seq * S + (g * 4 + t4) * 128
                        P.dma("sp", yp[row0:row0 + 128, :], xg[:, t4, :], s_yo[t4], reads=[R_xg[t4]])
                    elif not DEBUG:
                        P.dma("sp", ys[:, :], xg[0:16, t4, :], s_yo[t4], reads=[R_xg[t4]])
            P.barrier()
            d_es.close()
            if do_s:
                samp_es.close()
            seq_es.close()

        P.barrier()
        block = es.enter_context(nc.Block())
        P.replay(block)
    return nc


def host_consts():
    c = np.zeros((128, 4 * 128), np.float32)
    c[:, 0:128] = np.eye(128, dtype=np.float32)
    j = np.arange(128)[:, None]
    s = np.arange(128)[None, :]
    c[:, 128:256] = np.where(j >= s, -1.0, 0.0)
    c[:, 256:384] = np.where(j < s, 1.0, 0.0)
    for jj in range(4):
        for hh in range(H):
            for q in range(4):
                c[jj, 384 + hh * 4 + q] = 1.0 if jj < q else 0.0
    return c


def make_in_maps(inputs, n_cores, S, NPG):
    f = lambda a: np.ascontiguousarray(np.asarray(a))
    x_prompt = f(inputs["x_prompt"]); p_prompt = f(inputs["p_prompt"])[0]
    x_sample = f(inputs["x_sample"]); p_sample = f(inputs["p_sample"])[0]
    ck = f(inputs["cache_k"])[0]; cv = f(inputs["cache_v"])[0]
    npool = ck.shape[0]
    ck = ck.reshape(npool * 128, 512); cv = cv.reshape(npool * 128, 512)
    sconv = f(inputs["state_conv"])[0]
    pt = f(inputs["page_table"]).astype(np.int32)
    cw = f(inputs["conv_w"])[0]
    cwT = np.ascontiguousarray(cw.reshape(CW, 4, 128).transpose(2, 1, 0).reshape(128, 4 * CW))
    vec = lambda a: f(a)[0].reshape(4, 128).T
    cvec = np.ascontiguousarray(np.concatenate([vec(inputs["conv_b"]), vec(inputs["conv_ln_g"]), vec(inputs["conv_ln_b"])], axis=1))
    lnp = np.ascontiguousarray(np.stack([f(inputs["ln1_g"])[0], f(inputs["ln1_b"])[0], f(inputs["ln2_g"])[0], f(inputs["ln2_b"])[0]]))
    sbb = f(inputs["sb_bias"]).reshape(1, H)
    brow = np.ascontiguousarray(np.broadcast_to(sbb.reshape(1, H, 1), (16, H, 4)).reshape(1, 512))
    consts = host_consts()
    shared = {
        "ck": ck, "cv": cv, "cwT": cwT, "cvec": cvec, "lnp": lnp, "sbb": sbb, "consts": consts, "brow": brow,
        "w_in": f(inputs["w_in"])[0], "w_conv_proj": f(inputs["w_conv_proj"])[0], "w_att_proj": f(inputs["w_att_proj"])[0],
        "w_out": f(inputs["w_out"])[0], "w_ffn_up": f(inputs["w_ffn_up"])[0], "w_ffn_down": f(inputs["w_ffn_down"])[0],
        "w_ple_gate": f(inputs["w_ple_gate"])[0], "w_ple": f(inputs["w_ple"])[0],
    }
    maps = []
    for c in range(n_cores):
        m = dict(shared)
        m["xp"] = x_prompt[2 * c:2 * c + 2].reshape(2 * S, D)
        m["pp"] = p_prompt[2 * c:2 * c + 2].reshape(2 * S, PLE)
        m["xs"] = x_sample[4 * c:4 * c + 4].reshape(16, D)
        m["pps"] = p_sample[4 * c:4 * c + 4].reshape(16, PLE)
        m["sconv"] = sconv[4 * c:4 * c + 4].reshape(120, CC)
        m["ptab"] = pt[4 * c:4 * c + 4].reshape(1, 4 * NPG)
        maps.append(m)
    return maps, npool


def assemble(results, n_cores, S):
    cat = lambda k: np.concatenate([np.asarray(r[k]) for r in results], axis=0)
    B = 2 * n_cores
    DB = 4 * n_cores
    y_p = cat("yp").reshape(B, S, D)
    y_s = cat("ys").reshape(DB, 4, D)
    k_p = cat("kp").reshape(1, B, S, H, DH)
    v_p = cat("vp").reshape(1, B, S, H, DH)
    c_p = cat("cpo").reshape(1, B, 30, CC)
    k_s = cat("ksn").reshape(1, DB, 4, H, DH)
    v_s = cat("vsn").reshape(1, DB, 4, H, DH)
    c_s = cat("csn").reshape(1, DB, 30, CC)
    return tuple(np.ascontiguousarray(a, dtype=np.float32) for a in (y_p, y_s, k_p, v_p, c_p, k_s, v_s, c_s))


def run(inputs, n_cores, S, NPG):
    maps, npool = make_in_maps(inputs, n_cores, S, NPG)
    nc = build(S, NPG, npool)
    res = run_bass_kernel_spmd(nc, maps, core_ids=list(range(n_cores)))
    return assemble(res.results, n_cores, S)


def kernel(**inputs):
    return run(inputs, 8, 2048, 128)
```

```python
import numpy as np
from contextlib import ExitStack
import concourse.bass as bass
import concourse.mybir as mybir
from concourse.bass_utils import run_bass_kernel_spmd

F32 = mybir.dt.float32
BF16 = mybir.dt.bfloat16
I32 = mybir.dt.int32
AF = mybir.ActivationFunctionType
ALU = mybir.AluOpType

D = 1024
H = 8
DH = 64
CC = 512
FF = 2816
NJ = FF // 128
PLE = 256
INC = 4608
CW = 31
ALPHA = float(2.0 ** 0.25)
EPS = 1e-5
NSLOT = 6
SLOTW = 4096
SELF_WAIT = ('act', 'dve', 'pool')
DEBUG = False
NTAP_POOL = 0

ENGS = ("pe", "act", "dve", "pool", "sp")


class Sem:
    def __init__(self, h):
        self.h = h
        self.n = 0


class Res:
    __slots__ = ("name", "w", "r")

    def __init__(self, name):
        self.name = name
        self.w = None
        self.r = []


class Prog:
    def __init__(self, nc, es):
        self.nc = nc
        self.es = es
        self.q = {e: [] for e in ENGS}
        self.psem = {e: Sem(es.enter_context(nc.semaphore("prog_" + e))) for e in ENGS}
        self.waited = {e: {} for e in ENGS}
        self.nsem = 0
        self.dma_tickets = []

    def newsem(self, name):
        self.nsem += 1
        return Sem(self.es.enter_context(self.nc.semaphore(name)))

    def _deps(self, eng, reads, writes, extra):
        deps = list(extra)
        for r in reads:
            if r.w is not None:
                deps.append(r.w)
        for w in writes:
            if w.w is not None:
                deps.append(w.w)
            deps.extend(w.r)
        best = {}
        for d in deps:
            if d is None:
                continue
            s, v = d
            if s is self.psem[eng] and eng not in SELF_WAIT:
                continue
            if best.get(id(s), (None, 0))[1] < v:
                best[id(s)] = (s, v)
        out = []
        for k, (s, v) in best.items():
            if self.waited[eng].get(k, 0) < v:
                self.waited[eng][k] = v
                out.append((s, v))
        return out

    def _commit(self, tk, reads, writes):
        for r in reads:
            r.r.append(tk)
            if len(r.r) > 24:
                best = {}
                for s, v in r.r:
                    if best.get(id(s), (None, 0))[1] < v:
                        best[id(s)] = (s, v)
                r.r = list(best.values())
        for w in writes:
            w.w = tk
            w.r = []

    def emit(self, eng, fn, reads=(), writes=(), sig=True, extra=()):
        waits = self._deps(eng, reads, writes, extra)
        ps = self.psem[eng]
        if sig:
            ps.n += 1
            tk = (ps, ps.n)
        else:
            tk = (ps, ps.n + 1)
        self.q[eng].append((waits, fn, sig))
        self._commit(tk, reads, writes)
        return tk

    def dma(self, eng, out, in_, sem, reads=(), writes=(), extra=(), indirect=None):
        waits = self._deps(eng, reads, writes, extra)
        sem.n += 16
        tk = (sem, sem.n)
        if indirect is None:
            fn = lambda e: e.dma_start(out=out, in_=in_)
        else:
            fn = lambda e: e.indirect_dma_start(out=out, out_offset=None, in_=in_,
                                                in_offset=bass.IndirectOffsetOnAxis(ap=indirect, axis=0))
        self.q[eng].append((waits, fn, sem))
        self._commit(tk, reads, writes)
        self.dma_tickets.append(tk)
        return tk

    def barrier(self):
        tks = [(self.psem[e], self.psem[e].n) for e in ENGS if self.psem[e].n > 0]
        best = {}
        for s, v in self.dma_tickets:
            if best.get(id(s), (None, 0))[1] < v:
                best[id(s)] = (s, v)
        tks += list(best.values())
        self.dma_tickets = []
        for e in ENGS:
            waits = self._deps(e, (), (), tks)
            if waits:
                self.q[e].append((waits, None, False))

    def replay(self, block):
        nc = self.nc

        def run(name, e):
            ps = self.psem[name]
            for waits, fn, sig in self.q[name]:
                for s, v in waits:
                    e.wait_ge(s.h, v)
                if fn is None:
                    continue
                ins = fn(e)
                if sig is True:
                    ins.then_inc(ps.h, 1)
                elif sig is not False:
                    ins.then_inc(sig.h, 16)

        @block.tensor
        def _(e):
            run("pe", e)

        @block.scalar
        def _(e):
            run("act", e)

        @block.vector
        def _(e):
            run("dve", e)

        @block.gpsimd
        def _(e):
            run("pool", e)

        @block.sync
        def _(e):
            run("sp", e)

    def mm(self, out, lhsT, rhs, start, stop, reads, writes, sig=False, extra=()):
        return self.emit("pe", lambda e: e.matmul(out, lhsT=lhsT, rhs=rhs, start=start, stop=stop,
                                                  skip_group_check=True),
                         reads, writes, sig, extra)

    def tr(self, out, in_, ident, reads, writes, sig=False):
        return self.emit("pe", lambda e: e.transpose(out=out, in_=in_, identity=ident), reads, writes, sig)

    def act(self, out, in_, func, reads, writes, bias=0.0, scale=1.0, extra=()):
        return self.emit("act", lambda e: e.activation(out=out, in_=in_, func=func, bias=bias, scale=scale),
                         reads, writes, True, extra)

    def copy(self, eng, out, in_, reads, writes):
        if eng == "act":
            return self.act(out, in_, AF.Copy, reads, writes)
        return self.emit(eng, lambda e: e.tensor_copy(out=out, in_=in_), reads, writes)

    def tt(self, eng, out, in0, in1, op, reads, writes):
        return self.emit(eng, lambda e: e.tensor_tensor(out=out, in0=in0, in1=in1, op=op), reads, writes)

    def ts(self, eng, out, in0, s1, s2, op0, op1, reads, writes):
        return self.emit(eng, lambda e: e.tensor_scalar(out=out, in0=in0, scalar1=s1, scalar2=s2, op0=op0, op1=op1),
                         reads, writes)

    def stt(self, eng, out, in0, scalar, in1, op0, op1, reads, writes):
        return self.emit(eng, lambda e: e.scalar_tensor_tensor(out=out, in0=in0, scalar=scalar, in1=in1,
                                                               op0=op0, op1=op1), reads, writes)

    def bn_stats(self, out, in_, reads, writes):
        return self.emit("dve", lambda e: e.bn_stats(out=out, in_=in_), reads, writes)

    def bn_aggr(self, out, in_, reads, writes):
        return self.emit("dve", lambda e: e.bn_aggr(out=out, in_=in_), reads, writes)

    def memset(self, eng, ap, val, writes):
        return self.emit(eng, lambda e: e.memset(ap, val), (), writes)


def slot_plan():
    slots = []
    for mh in range(2):
        c0 = mh * 512
        slots.append([("w_conv_proj", 0, 4, c0, 512, 0, 512), ("w_att_proj", 0, 4, c0, 512, 2048, 512)])
        slots.append([("w_in", 0, 8, 2560 + c0, 512, 0, 512)])
        slots.append([("w_in", 0, 8, 3584 + c0, 512, 0, 512)])
    for hf in range(2):
        slots.append([("w_out", 0, 8, hf * 512, 512, 0, 512)])
    for jp in range(NJ // 2):
        slots.append([("w_ffn_up", 0, 8, 2 * jp * 128, 256, 0, 512), ("w_ffn_up", 0, 8, FF + 2 * jp * 128, 256, 256, 512)])
    for hf in range(2):
        c0 = hf * 512
        slots.append([("w_ple_gate", 0, 8, c0, 512, 0, 512)])
        slots.append([("w_ple", 0, 2, c0, 512, 0, 512), ("w_ffn_down", 0, 6, c0, 512, 1024, 512)])
        slots.append([("w_ffn_down", 6, 8, c0, 512, 0, 512)])
        slots.append([("w_ffn_down", 14, 8, c0, 512, 0, 512)])
    return slots


def build(S, NPG, NPOOL, with_sample=True):
    NT = S // 128
    NG = S // 512
    nc = bass.Bass("TRN2", target_bir_lowering=False)
    dt = nc.dram_tensor

    def din(name, shape, dtype=F32):
        return dt(name, shape, dtype, kind="ExternalInput").ap()

    def dout(name, shape):
        return dt(name, shape, F32, kind="ExternalOutput").ap()

    xp = din("xp", [2 * S, D])
    pp = din("pp", [2 * S, PLE])
    xs = din("xs", [16, D])
    pps = din("pps", [16, PLE])
    ck = din("ck", [NPOOL * 128, 512])
    cv = din("cv", [NPOOL * 128, 512])
    sconv = din("sconv", [120, CC])
    ptab = din("ptab", [1, 4 * NPG], I32)
    W = {
        "w_in": din("w_in", [D, INC]),
        "w_conv_proj": din("w_conv_proj", [CC, D]),
        "w_att_proj": din("w_att_proj", [CC, D]),
        "w_out": din("w_out", [D, D]),
        "w_ffn_up": din("w_ffn_up", [D, 2 * FF]),
        "w_ffn_down": din("w_ffn_down", [FF, D]),
        "w_ple_gate": din("w_ple_gate", [D, D]),
        "w_ple": din("w_ple", [PLE, D]),
    }
    cwT = din("cwT", [128, 4 * CW])
    cvec = din("cvec", [128, 12])
    lnp = din("lnp", [4, D])
    sbb = din("sbb", [1, H])
    consts = din("consts", [128, 4 * 128])
    brow = din("brow", [1, 512])

    yp = dout("yp", [2 * S, D])
    ys = dout("ys", [16, D])
    kp = dout("kp", [2 * S, 512])
    vp = dout("vp", [2 * S, 512])
    cpo = dout("cpo", [60, CC])
    ksn = dout("ksn", [16, 512])
    vsn = dout("vsn", [16, 512])
    csn = dout("csn", [120, CC])

    slots = slot_plan()
    NSL = len(slots)
    scratch = dt("wscratch", [NSL, 128, SLOTW], BF16, kind="Internal").ap()
    scratchA = dt("wscratchA", [5, 128, SLOTW], BF16, kind="Internal").ap()

    with ExitStack() as es:
        P = Prog(nc, es)
        sb = lambda name, shape, dtype, st=es: st.enter_context(nc.sbuf_tensor(name, shape, dtype))
        pst = lambda name, shape, dtype, st=es: st.enter_context(nc.psum_tensor(name, shape, dtype))

        lnb = sb("lnb", [128, 4, D], F32)
        cw_sb = sb("cw_sb", [128, 4 * CW], F32)
        cvec_sb = sb("cvec_sb", [128, 12], F32)
        biasT = sb("biasT", [128, H], F32)
        cst_f = sb("cst_f", [128, 4 * 128], F32)
        identb = sb("identb", [128, 128], BF16)
        negU = sb("negU", [128, 128], BF16)
        negOnes = sb("negOnes", [128, 128], BF16)
        tri = sb("tri", [128, 128], BF16)
        onesM = sb("onesM", [128, 128], BF16)
        zer = sb("zer", [128, 512], BF16)
        identf = cst_f[:, 0:128]
        cf2 = sb("cf2", [128, 256], F32)
        brow_sb = sb("brow_sb", [1, 512], F32)

        psT = [pst("psT%d" % i, [128, 1024], BF16) for i in range(2)]
        ps = [pst("ps%d" % i, [128, 512], F32) for i in range(6)]
        R_psT = [Res("psT%d" % i) for i in range(2)]
        R_ps = [Res("ps%d" % i) for i in range(6)]
        rot = {"i": 0, "t": 0}

        def next_ps():
            rot["i"] = (rot["i"] + 1) % 6
            return ps[rot["i"]], R_ps[rot["i"]]

        def next_psT():
            rot["t"] = (rot["t"] + 1) % 2
            return psT[rot["t"]], R_psT[rot["t"]]

        R_const = Res("const")
        R_winA = Res("winA")
        s_c = P.newsem("s_const")
        for (o, i_) in ((lnb[:].rearrange("p a d -> p (a d)"), lnp.rearrange("a d -> (a d)").partition_broadcast(128)),
                        (cw_sb[:], cwT), (cvec_sb[:], cvec), (biasT[:], sbb.partition_broadcast(128)),
                        (cst_f[:], consts)):
            P.dma("sp", o, i_, s_c, writes=[R_const])
        P.copy("pool", identb[:], cst_f[:, 0:128], [R_const], [R_const])
        P.copy("pool", negU[:], cst_f[:, 128:256], [R_const], [R_const])
        P.copy("pool", tri[:], cst_f[:, 256:384], [R_const], [R_const])
        P.memset("pool", negOnes[:], -1.0, [R_const])
        P.memset("pool", onesM[:], 1.0 / 512.0, [R_const])
        P.memset("pool", zer[:], 0.0, [R_const])
        P.memset("pool", cf2[:, 0:128], -1.0, [R_const])
        P.memset("pool", cf2[:, 128:256], 1.0, [R_const])
        P.dma("sp", brow_sb[:], brow, P.newsem("s_brow"), writes=[R_const])
        R_winA = Res("winA")
        R_scrA = [Res("scrA%d" % i) for i in range(5)]
        s_wa = P.newsem("s_winA")

        class WStream:
            def __init__(self):
                self.n = 0
                self.sem = [P.newsem("s_slot%d" % i) for i in range(NSLOT)]
                self.res = [Res("slot%d" % i) for i in range(NSLOT)]
                self.scr = [Res("scr%d" % i) for i in range(NSL)]
                self.buf = None
                self.issued = 0
                self.total = 0

            def prefetch(self):
                while self.issued < self.limit and self.issued < self.n + NSLOT:
                    u = self.issued
                    si = u % NSLOT
                    sl = u % NSL
                    dst = self.buf[:, si, :]
                    P.dma("sp", dst, scratch[sl], self.sem[si], reads=[self.scr[sl]], writes=[self.res[si]])
                    self.issued += 1

            def get(self, ahead=0):
                u = self.n + ahead
                assert u < self.issued, "weight unit consumed before its load was emitted"
                si = u % NSLOT
                return self.buf[:, si, :], self.res[si]

            def done(self):
                self.n += 1
                self.prefetch()

        WS = WStream()

        pro_es = ExitStack()
        stf = [sb("stf%d" % i, [128, SLOTW], F32, pro_es) for i in range(3)]
        stb = [sb("stb%d" % i, [128, SLOTW], BF16, pro_es) for i in range(3)]
        R_stf = [Res("stf") for _ in range(3)]
        R_stb = [Res("stb") for _ in range(3)]
        s_pld = [P.newsem("s_pld%d" % i) for i in range(3)]
        s_pst = [P.newsem("s_pst%d" % i) for i in range(3)]
        allslots = [(scratchA[cb], [("w_in", 0, 8, cb * 512, 512, 0, 512)], R_scrA[cb]) for cb in range(5)]
        allslots += [(scratch[sl], slots[sl], WS.scr[sl]) for sl in range(NSL)]
        for n, (dst_dram, plan, Rscr) in enumerate(allslots):
            i = n % 3
            for (wn, k0, nk, c0, ncols, off, ks) in plan:
                srcv = W[wn][k0 * 128:(k0 + nk) * 128, c0:c0 + ncols].rearrange("(k p) c -> p k c", p=128)
                dstv = stf[i][:, :].rearrange("p (k c) -> p k c", c=ks)[:, off // ks:off // ks + nk, off % ks:off % ks + ncols]
                P.dma("sp", dstv, srcv, s_pld[i], writes=[R_stf[i]])
            P.copy(("act", "dve", "pool")[n % 3], stb[i][:], stf[i][:], [R_stf[i]], [R_stb[i]])
            P.dma("sp", dst_dram, stb[i][:], s_pst[i], reads=[R_stb[i]], writes=[Rscr])
        P.barrier()
        pro_es.close()
        WS.limit = 0


        samp_es = ExitStack()

        def sample_alloc():
            sbP = lambda name, shape, dtype: sb("SP_" + name, shape, dtype, samp_es)
            d_ = dict(
                s_sT=sbP("s_sT", [128, 4, 16], BF16), o_sT=sbP("o_sT", [128, 4, 16], BF16),
                Qblk=sbP("Qblk", [128, 4, 4, 32], BF16),
                knT=[sbP("knT%d" % b, [128, 4, 4], BF16) for b in range(4)],
                vnb=[sbP("vnb%d" % b, [4, 512], BF16) for b in range(4)],
                idx=sbP("idx", [128, 4 * NPG], I32))
            return d_

        def sample_S1(winA, SBP):
            s1_es = ExitStack()
            sbS = lambda name, shape, dtype: sb("S_" + name, shape, dtype, s1_es)
            xsT = sbS("xsT", [128, 8, 16], BF16)
            s_sT, o_sT, Qblk, knT, vnb, idx = [SBP[k] for k in ("s_sT", "o_sT", "Qblk", "knT", "vnb", "idx")]
            R_xsT, R_ssT, R_osT = Res("xsT"), Res("s_sT"), Res("o_sT")
            PB = min(16, NPG)
            NBT = NPG // PB
            NW = PB * 32
            xs_f = sbS("xs_f", [16, D], F32)
            xs_b = sbS("xs_b", [16, D], BF16)
            qs_b = sbS("qs_b", [16, 512], BF16)
            us = sbS("us", [16, 512], F32)
            sgs = sbS("sgs", [16, 512], F32)
            sc_sb = sbS("sc_sb", [120, CC], F32)
            fullT = sbS("fullT", [128, 4, 4, 34], F32)
            kn = [sbS("kn%d" % b, [4, 512], F32) for b in range(4)]
            vn = [sbS("vn%d" % b, [4, 512], F32) for b in range(4)]
            knb = [sbS("knb%d" % b, [4, 512], BF16) for b in range(4)]
            qsT = sbS("qsT", [128, 4, 16], BF16)
            accs = sbS("accs", [128, 4, 16], F32)
            cbs = sbS("cbs", [128, 4, 16], BF16)
            sqs = sbS("sqs", [128, 4, 16], BF16)
            rstd_s = sbS("rstd_s", [128, 16], F32)
            pt_i = sbS("pt_i", [128, 4 * NPG], I32)
            pt_f = sbS("pt_f", [128, 4 * NPG], F32)
            io_f = sbS("io_f", [128, 1], F32)
            R = {n: Res(n) for n in ("xsf", "xsb", "qsb", "us", "sgs", "sc", "fullT", "qsT", "Qblk", "accs", "cbs", "sqs",
                                     "rstd", "idx", "En", "Lnw", "an", "Es", "Lbs", "Lsuf", "o32s", "otok", "otb")}
            R_kn = [Res("kn") for _ in range(4)]; R_vn = [Res("vn") for _ in range(4)]
            R_knb = [Res("knb") for _ in range(4)]; R_vnb = [Res("vnb") for _ in range(4)]
            R_knT = [Res("knT") for _ in range(4)]
            R_kpg = [Res("kpg") for _ in range(2)]; R_vpg = [Res("vpg") for _ in range(2)]
            R_kTp = [Res("kTp") for _ in range(4)]
            R_aTs = [Res("aTs") for _ in range(2)]
            s_in = P.newsem("s_sin")
            s_so = P.newsem("s_sout")
            s_kp = [P.newsem("s_kp%d" % i) for i in range(2)]
            s_vp = [P.newsem("s_vp%d" % i) for i in range(2)]
            s_o32 = P.newsem("s_o32")
            mnew = cst_f[0:4, 384:416]

            P.dma("sp", xs_f[:], xs[:, :], s_in, writes=[R["xsf"]])
            P.dma("sp", sc_sb[:], sconv[:, :], P.newsem("s_sin2"), writes=[R["sc"]])
            P.dma("sp", pt_i[:], ptab.partition_broadcast(128), P.newsem("s_sin3"), writes=[R["idx"]])
            P.emit("pool", lambda e: e.iota(io_f[:], pattern=[[0, 1]], base=0, channel_multiplier=1,
                                            allow_small_or_imprecise_dtypes=True), (), [R["idx"]])
            P.copy("pool", pt_f[:], pt_i[:], [R["idx"]], [R["idx"]])
            P.ts("pool", idx[:], pt_f[:], 128.0, io_f[:, 0:1], ALU.mult, ALU.add, [R["idx"]], [R["idx"]])
            P.copy("pool", xs_b[:], xs_f[:], [R["xsf"]], [R["xsb"]])
            pt_, Rpt = next_psT()
            for k in range(8):
                P.tr(pt_[:, k * 128:k * 128 + 16], xs_b[:, k * 128:(k + 1) * 128], identb[0:16, 0:16],
                     [R["xsb"], R_const], [Rpt], sig=(k == 7))
            P.copy("dve", xsT[:], pt_[:].rearrange("p (k n) -> p k n", k=8)[:, :, 0:16], [Rpt], [R_xsT])

            def proj16(cb):
                pk, Rpk = next_ps()
                for k in range(8):
                    P.mm(pk[0:16, :], xsT[:, k, :], winA[:, k, cb * 512:(cb + 1) * 512], k == 0, k == 7,
                         [R_xsT, R_winA], [Rpk], sig=(k == 7))
                return pk, Rpk

            pa, Rpa = proj16(0)
            pb, Rpb = proj16(1)
            P.act(sgs[:], pb[0:16, :], AF.Sigmoid, [Rpb], [R["sgs"]])
            P.tt("dve", us[:], pa[0:16, :], sgs[:], ALU.mult, [Rpa, R["sgs"]], [R["us"]])
            pq, Rpq = proj16(2)
            P.act(qs_b[:], pq[0:16, :], AF.Copy, [Rpq], [R["qsb"]], scale=0.125)
            for b in range(4):
                P.dma("sp", csn[b * 30:b * 30 + 26, :], sc_sb[b * 30 + 4:b * 30 + 30, :], s_so, reads=[R["sc"]])
                P.dma("sp", csn[b * 30 + 26:b * 30 + 30, :], us[b * 4:b * 4 + 4, :], s_so, reads=[R["us"]])
            for b in range(4):
                for which in range(2):
                    dst, Rd, dstb, Rdb, outd = ((kn, R_kn, knb, R_knb, ksn), (vn, R_vn, vnb, R_vnb, vsn))[which]
                    pk, Rpk = next_ps()
                    c0 = 1536 + which * 512
                    for k in range(8):
                        P.mm(pk[0:4, :], xsT[:, k, b * 4:(b + 1) * 4], winA[:, k, c0:c0 + 512], k == 0, k == 7,
                             [R_xsT, R_winA], [Rpk], sig=(k == 7))
                    P.copy("dve", dst[b][:], pk[0:4, :], [Rpk], [Rd[b]])
                    P.dma("sp", outd[b * 4:(b + 1) * 4, :], dst[b][:], s_so, reads=[Rd[b]])
                    P.copy("pool", dstb[b][:], dst[b][:], [Rd[b]], [Rdb[b]])
                pt_, Rpt = next_psT()
                for c in range(4):
                    P.tr(pt_[:, c * 128:c * 128 + 4], knb[b][:, c * 128:(c + 1) * 128], identb[0:4, 0:4],
                         [R_knb[b], R_const], [Rpt], sig=(c == 3))
                P.copy("dve", knT[b][:], pt_[:, 0:512].rearrange("p (k n) -> p k n", k=4)[:, :, 0:4], [Rpt], [R_knT[b]])
            pt_, Rpt = next_psT()
            for c in range(4):
                P.tr(pt_[:, c * 128:c * 128 + 16], qs_b[:, c * 128:(c + 1) * 128], identb[0:16, 0:16],
                     [R["qsb"], R_const], [Rpt], sig=(c == 3))
            P.copy("dve", qsT[:], pt_[:, 0:512].rearrange("p (k n) -> p k n", k=4)[:, :, 0:16], [Rpt], [R["qsT"]])
            P.memset("pool", Qblk[:], 0.0, [R["Qblk"]])
            for c in range(4):
                for hh in range(2):
                    h = 2 * c + hh
                    P.copy("pool", Qblk[hh * 64:(hh + 1) * 64, c, :, h * 4:(h + 1) * 4],
                           qsT[hh * 64:(hh + 1) * 64, c, :].rearrange("p (b q) -> p b q", b=4),
                           [R["qsT"], R["Qblk"]], [R["Qblk"]])
            for c in range(4):
                pf, Rpf = next_ps()
                P.tr(pf[:, 0:120], sc_sb[0:120, c * 128:(c + 1) * 128], identf[0:120, 0:120], [R["sc"], R_const], [Rpf])
                P.tr(pf[:, 128:144], us[0:16, c * 128:(c + 1) * 128], identf[0:16, 0:16], [R["us"], R_const], [Rpf], sig=True)
                P.copy("dve", fullT[:, c, :, 0:30], pf[:, 0:120].rearrange("p (b r) -> p b r", b=4), [Rpf], [R["fullT"]])
                P.copy("dve", fullT[:, c, :, 30:34], pf[:, 128:144].rearrange("p (b r) -> p b r", b=4), [Rpf], [R["fullT"]])
            for w in range(CW):
                for c in range(4):
                    src = fullT[:, c, :, w:w + 4]
                    dst = accs[:, c, :].rearrange("p (b t) -> p b t", b=4)
                    wv = cw_sb[:, c * CW + w:c * CW + w + 1]
                    if w == 0:
                        P.ts("dve", dst, src, wv, cvec_sb[:, c:c + 1], ALU.mult, ALU.add, [R["fullT"], R_const], [R["accs"]])
                    else:
                        P.stt("dve", dst, src, wv, dst, ALU.mult, ALU.add, [R["fullT"], R_const, R["accs"]], [R["accs"]])
            P.copy("pool", cbs[:], accs[:], [R["accs"]], [R["cbs"]])
            pm, Rpm = next_ps()
            for c in range(4):
                P.mm(pm[:, 0:16], onesM[:], cbs[:, c, :], c == 0, c == 3, [R_const, R["cbs"]], [Rpm], sig=(c == 3))
            for c in range(4):
                P.tt("dve", accs[:, c, :], accs[:, c, :], pm[:, 0:16], ALU.subtract, [R["accs"], Rpm], [R["accs"]])
            P.act(sqs[:], accs[:], AF.Square, [R["accs"]], [R["sqs"]])
            pv, Rpv = next_ps()
            for c in range(4):
                P.mm(pv[:, 0:16], onesM[:], sqs[:, c, :], c == 0, c == 3, [R_const, R["sqs"]], [Rpv], sig=(c == 3))
            P.act(rstd_s[:], pv[:, 0:16], AF.Ln, [Rpv], [R["rstd"]], bias=EPS)
            P.act(rstd_s[:], rstd_s[:], AF.Exp, [R["rstd"]], [R["rstd"]], scale=-0.5)
            for c in range(4):
                P.tt("dve", accs[:, c, :], accs[:, c, :], rstd_s[:], ALU.mult, [R["accs"], R["rstd"]], [R["accs"]])
                P.act(s_sT[:, c, :], accs[:, c, :], AF.Silu, [R["accs"], R_const], [R_ssT],
                      bias=cvec_sb[:, 8 + c:9 + c], scale=cvec_sb[:, 4 + c:5 + c])

            P.barrier()
            s1_es.close()
            return dict(locals())

        def sample_S2(SB):
            g_ = SB
            (PB, NBT, NW, R, Qblk, knT, vnb, idx, mnew, s_sT, o_sT, R_ssT, R_osT,
             R_knT, R_vnb, s_kp, s_vp, s_o32, s_so) = [g_[k] for k in (
                "PB", "NBT", "NW", "R", "Qblk", "knT", "vnb", "idx", "mnew",
                "s_sT", "o_sT", "R_ssT", "R_osT", "R_knT", "R_vnb", "s_kp", "s_vp", "s_o32", "s_so")]
            R_kpg, R_vpg, R_kTp, R_aTs = g_["R_kpg"], g_["R_vpg"], g_["R_kTp"], g_["R_aTs"]
            s2_es = ExitStack()
            sb2 = lambda name, shape, dtype: sb("S2_" + name, shape, dtype, s2_es)
            kpg = [sb2("kpg%d" % i, [128, PB, 512], BF16) for i in range(2)]
            vpg = [sb2("vpg%d" % i, [128, PB, 512], BF16) for i in range(2)]
            kTp = [sb2("kTp%d" % i, [128, 512], BF16) for i in range(4)]
            Es = sb2("Es", [128, NW], F32)
            Lbs = sb2("Lbs", [128, PB, 32], F32)
            Lsuf = sb2("Lsuf", [128, PB + 1, 32], F32)
            aTs = [sb2("aTs%d" % i, [128, NW], BF16) for i in range(2)]
            En = sb2("En", [4, 32], F32)
            Lnw = sb2("Lnw", [4, 32], F32)
            an = sb2("an", [4, 32], BF16)
            o32s = sb2("o32s", [32, 512], F32)
            o_tok = sb2("o_tok", [16, 512], F32)
            o_tb = sb2("o_tb", [16, 512], BF16)
            negUf = cst_f[:, 128:256]
            cnt = 0
            rk = 0
            o32, Ro32 = ps[2], R_ps[2]
            zn, Rzn = ps[3], R_ps[3]
            for b in range(4):
                for c in range(4):
                    P.mm(zn[0:4, 0:32], knT[b][:, c, :], Qblk[:, c, b, :], c == 0, False, [R_knT[b], R["Qblk"]], [Rzn])
                P.mm(zn[0:4, 0:32], cf2[0:1, 128:132], brow_sb[0:1, 0:32], False, True, [R_const], [Rzn], sig=True)
                P.act(En[:], zn[0:4, 0:32], AF.Exp, [Rzn], [R["En"]])
                P.act(Lnw[:], En[:], AF.Ln, [R["En"]], [R["Lnw"]], bias=1.0)
                P.tt("dve", Lnw[:], Lnw[:], mnew, ALU.mult, [R["Lnw"], R_const], [R["Lnw"]])
                P.mm(zn[0:4, 0:32], cst_f[0:4, 128:132], Lnw[:], False, True, [R_const, R["Lnw"]], [Rzn], sig=True)
                P.act(an[:], zn[0:4, 0:32], AF.Exp, [Rzn], [R["an"]])
                P.tt("dve", an[:], an[:], mnew, ALU.mult, [R["an"], R_const], [R["an"]])
                P.memset("pool", Lsuf[:, PB, :], 0.0, [R["Lsuf"]])
                P.copy("pool", Lsuf[0:4, PB, :], Lnw[:], [R["Lnw"], R["Lsuf"]], [R["Lsuf"]])
                P.mm(o32[0:32, :], an[:], vnb[b][:], True, False, [R["an"], R_vnb[b]], [Ro32], sig=True)
                for nb in reversed(range(NBT)):
                    buf = cnt % 2
                    zb, Rz = ps[cnt % 2], R_ps[cnt % 2]
                    cnt += 1
                    for pi in range(PB):
                        j = b * NPG + nb * PB + pi
                        P.dma("pool", kpg[buf][:, pi, :], ck, s_kp[buf], reads=[R["idx"]], writes=[R_kpg[buf]],
                              indirect=idx[:, j:j + 1])
                        P.dma("pool", vpg[buf][:, pi, :], cv, s_vp[buf], reads=[R["idx"]], writes=[R_vpg[buf]],
                              indirect=idx[:, j:j + 1])
                    P.mm(zb[:, 0:NW], cf2[0:1, 128:256], brow_sb[0:1, 0:NW], True, False, [R_const], [Rz], sig=True)
                    for pi in range(PB):
                        pt_, Rpt = next_psT()
                        for c in range(4):
                            P.tr(pt_[:, c * 128:(c + 1) * 128], kpg[buf][:, pi, c * 128:(c + 1) * 128], identb[:],
                                 [R_kpg[buf], R_const], [Rpt], sig=(c == 3))
                        r = rk % 4
                        rk += 1
                        P.copy("act" if r % 2 else "dve", kTp[r][:], pt_[:, 0:512], [Rpt], [R_kTp[r]])
                        for c in range(4):
                            P.mm(zb[:, pi * 32:(pi + 1) * 32], kTp[r][:, c * 128:(c + 1) * 128], Qblk[:, c, b, :], False, False,
                                 [R_kTp[r], R["Qblk"]], [Rz], sig=(c == 3))
                    P.act(Es[:], zb[:, 0:NW], AF.Exp, [Rz], [R["Es"]])
                    P.act(Lbs[:].rearrange("p a b -> p (a b)"), Es[:], AF.Ln, [R["Es"]], [R["Lbs"]], bias=1.0)
                    for pi in reversed(range(PB)):
                        P.tt("dve", Lsuf[:, pi, :], Lsuf[:, pi + 1, :], Lbs[:, pi, :], ALU.add, [R["Lsuf"], R["Lbs"]], [R["Lsuf"]])
                    P.mm(zb[:, 0:NW], negUf, Lbs[:].rearrange("p a b -> p (a b)"), False, False, [R_const, R["Lbs"]], [Rz])
                    P.mm(zb[:, 0:NW], cf2[:, 0:128], Lsuf[:, 1:PB + 1, :].rearrange("p a b -> p (a b)"), False, True,
                         [R_const, R["Lsuf"]], [Rz], sig=True)
                    P.act(aTs[buf][:], zb[:, 0:NW], AF.Exp, [Rz], [R_aTs[buf]])
                    P.copy("dve", Lsuf[:, PB, :], Lsuf[:, 0, :], [R["Lsuf"]], [R["Lsuf"]])
                    for pi in range(PB):
                        P.mm(o32[0:32, :], aTs[buf][:, pi * 32:(pi + 1) * 32], vpg[buf][:, pi, :], False,
                             (nb == 0 and pi == PB - 1), [R_aTs[buf], R_vpg[buf]], [Ro32], sig=(pi == PB - 1))
                P.copy("dve", o32s[:], o32[0:32, :], [Ro32], [R["o32s"]])
                for h in range(H):
                    P.dma("sp", o_tok[b * 4:(b + 1) * 4, h * 64:(h + 1) * 64], o32s[h * 4:(h + 1) * 4, h * 64:(h + 1) * 64],
                          s_o32, reads=[R["o32s"]], writes=[R["otok"]])
            P.copy("pool", o_tb[:], o_tok[:], [R["otok"]], [R["otb"]])
            if DEBUG:
                P.dma("sp", ys[:, 0:512], o_tok[:], s_so, reads=[R["otok"]])
            pt_, Rpt = next_psT()
            for c in range(4):
                P.tr(pt_[:, c * 128:c * 128 + 16], o_tb[:, c * 128:(c + 1) * 128], identb[0:16, 0:16],
                     [R["otb"], R_const], [Rpt], sig=(c == 3))
            P.copy("dve", o_sT[:], pt_[:, 0:512].rearrange("p (k n) -> p k n", k=4)[:, :, 0:16], [Rpt], [R_osT])
            P.barrier()
            s2_es.close()
            return s_sT, o_sT, R_ssT, R_osT

        R_out = Res("outs")
        out_sems = []

        for seq in range(2):
            seq_es = ExitStack()
            sbs = lambda name, shape, dtype: sb("%s_q%d" % (name, seq), shape, dtype, seq_es)
            sT = sbs("sT", [128, 4, S], BF16)
            oT = sbs("oT", [128, 4, S], BF16)
            do_s = with_sample and seq == 1
            if do_s:
                SBP = sample_alloc()
            R_sT = [Res("sT%d" % g) for g in range(NG)]
            R_oT = [Res("oT%d" % g) for g in range(NG)]

            abc_es = ExitStack()
            sba = lambda name, shape, dtype: sb("%s_q%d" % (name, seq), shape, dtype, abc_es)
            qT = sba("qT", [128, 4, S], BF16)
            kT = sba("kT", [128, 4, S], BF16)
            vv = sba("vv", [128, NT, 512], BF16)
            uT = sba("uT", [128, 4, 32 + S], BF16)
            R_qT = [Res("qT%d" % g) for g in range(NG)]
            R_kT = [Res("kT%d" % t) for t in range(NT)]
            R_vv = [Res("vv%d" % t) for t in range(NT)]
            R_uT = [Res("uT%d" % g) for g in range(NG)]
            R_uh = Res("uThist")
            P.memset("pool", uT[:, :, 0:32], 0.0, [R_uh])

            wa_es = ExitStack()
            winA = sb("winA_q%d" % seq, [128, 8, 2560], BF16, wa_es)
            a_es = ExitStack()
            sbA = lambda name, shape, dtype: sb("%s_q%d" % (name, seq), shape, dtype, a_es)
            R_winA.w = None
            R_winA.r = []
            for cb in range(5):
                P.dma("sp", winA[:, :, cb * 512:(cb + 1) * 512], scratchA[cb].rearrange("p (k c) -> p k c", c=512), s_wa,
                      reads=[R_scrA[cb]], writes=[R_winA])
            xst = [sbA("xst%d" % i, [128, D], F32) for i in range(2)]
            xb = [sbA("xb%d" % i, [128, D], BF16) for i in range(2)]
            xT = [sbA("xT%d" % i, [128, 8, 512], BF16) for i in range(1)] * 2
            kst = [sbA("kst%d" % i, [128, 512], F32) for i in range(1)] * 2
            vst = [sbA("vst%d" % i, [128, 512], F32) for i in range(1)] * 2
            kb = [sbA("kb%d" % i, [128, 512], BF16) for i in range(2)]
            sg = [sbA("sg%d" % i, [128, 512], F32) for i in range(2)]
            ust = sbA("ust", [128, 512], F32)
            R_xst = [Res("xst") for _ in range(2)]
            R_xb = [Res("xb") for _ in range(2)]
            R_xT = [Res("xT")] * 2
            R_kst = [Res("kst")] * 2
            R_vst = [Res("vst")] * 2
            R_kb = [Res("kb") for _ in range(2)]
            R_sg = [Res("sg") for _ in range(2)]
            R_ust = Res("ust")
            s_x = [P.newsem("s_x%d_%d" % (seq, i)) for i in range(2)]
            s_ko = [P.newsem("s_ko%d_%d" % (seq, i)) for i in range(2)]
            s_vo = [P.newsem("s_vo%d_%d" % (seq, i)) for i in range(2)]
            s_co = P.newsem("s_co%d" % seq)
            out_sems += s_ko + s_vo + [s_co]

            for g in range(NG):
                xTg, RxTg = xT[g % 2], R_xT[g % 2]
                for t4 in range(4):
                    t = g * 4 + t4
                    b = t % 2
                    row0 = seq * S + t * 128
                    P.dma("sp", xst[b][:], xp[row0:row0 + 128, :], s_x[b], writes=[R_xst[b]])
                    P.copy("pool", xb[b][:], xst[b][:], [R_xst[b]], [R_xb[b]])
                    pt_, Rpt = next_psT()
                    for k in range(8):
                        P.tr(pt_[:, k * 128:(k + 1) * 128], xb[b][:, k * 128:(k + 1) * 128], identb[:],
                             [R_xb[b], R_const], [Rpt], sig=(k == 7))
                    P.copy("dve", xTg[:, :, t4 * 128:(t4 + 1) * 128], pt_[:].rearrange("p (k n) -> p k n", k=8),
                           [Rpt], [RxTg])
                    for which in range(2):
                        pk, Rpk = next_ps()
                        c0 = 1536 + which * 512
                        for k in range(8):
                            P.mm(pk[:], xTg[:, k, t4 * 128:(t4 + 1) * 128], winA[:, k, c0:c0 + 512], k == 0, k == 7,
                                 [RxTg, R_winA], [Rpk], sig=(k == 7))
                        if which == 0:
                            P.copy("dve", kst[b][:], pk[:], [Rpk], [R_kst[b]])
                            P.dma("sp", kp[row0:row0 + 128, :], kst[b][:], s_ko[b], reads=[R_kst[b]])
                            P.copy("pool", kb[b][:], kst[b][:], [R_kst[b]], [R_kb[b]])
                            pt2, Rpt2 = next_psT()
                            for c in range(4):
                                P.tr(pt2[:, c * 128:(c + 1) * 128], kb[b][:, c * 128:(c + 1) * 128], identb[:],
                                     [R_kb[b], R_const], [Rpt2], sig=(c == 3))
                            P.copy("act", kT[:, :, t * 128:(t + 1) * 128],
                                   pt2[:, 0:512].rearrange("p (k n) -> p k n", k=4), [Rpt2], [R_kT[t]])
                        else:
                            P.copy("act", vst[b][:], pk[:], [Rpk], [R_vst[b]])
                            P.dma("sp", vp[row0:row0 + 128, :], vst[b][:], s_vo[b], reads=[R_vst[b]])
                            P.copy("pool", vv[:, t, :], vst[b][:], [R_vst[b]], [R_vv[t]])
                    if t == NT - 1:
                        pa, Rpa = next_ps()
                        pb, Rpb = next_ps()
                        for k in range(8):
                            P.mm(pa[:], xTg[:, k, t4 * 128:(t4 + 1) * 128], winA[:, k, 0:512], k == 0, k == 7,
                                 [RxTg, R_winA], [Rpa], sig=(k == 7))
                        for k in range(8):
                            P.mm(pb[:], xTg[:, k, t4 * 128:(t4 + 1) * 128], winA[:, k, 512:1024], k == 0, k == 7,
                                 [RxTg, R_winA], [Rpb], sig=(k == 7))
                        P.act(sg[0][:], pb[:], AF.Sigmoid, [Rpb], [R_sg[0]])
                        P.tt("dve", ust[:], pa[:], sg[0][:], ALU.mult, [Rpa, R_sg[0]], [R_ust])
                        P.dma("sp", cpo[seq * 30:(seq + 1) * 30, :], ust[98:128, :], s_co, reads=[R_ust])
                gs = slice(g * 512, (g + 1) * 512)
                for c in range(4):
                    pq, Rpq = next_ps()
                    for k in range(8):
                        P.mm(pq[:], winA[:, k, 1024 + c * 128:1024 + (c + 1) * 128], xTg[:, k, :], k == 0, k == 7,
                             [RxTg, R_winA], [Rpq], sig=(k == 7))
                    P.act(qT[:, c, gs], pq[:], AF.Copy, [Rpq], [R_qT[g]], scale=0.125)
                for c in range(4):
                    pa, Rpa = next_ps()
                    pb, Rpb = next_ps()
                    for k in range(8):
                        P.mm(pb[:], winA[:, k, 512 + c * 128:512 + (c + 1) * 128], xTg[:, k, :], k == 0, k == 7,
                             [RxTg, R_winA], [Rpb], sig=(k == 7))
                    for k in range(8):
                        P.mm(pa[:], winA[:, k, c * 128:(c + 1) * 128], xTg[:, k, :], k == 0, k == 7,
                             [RxTg, R_winA], [Rpa], sig=(k == 7))
                    P.act(sg[c % 2][:], pb[:], AF.Sigmoid, [Rpb], [R_sg[c % 2]])
                    P.tt("dve", uT[:, c, 32 + g * 512:32 + (g + 1) * 512], pa[:], sg[c % 2][:], ALU.mult,
                         [Rpa, R_sg[c % 2]], [R_uT[g]])
            P.barrier()
            a_es.close()
            if do_s:
                SB = sample_S1(winA, SBP)
            wa_es.close()

            b_es = ExitStack()
            sbB = lambda name, shape, dtype: sb("%s_q%d" % (name, seq), shape, dtype, b_es)
            acc = [sbB("acc%d" % i, [128, 512], F32) for i in range(4)]
            acp = [sbB("acp%d" % i, [128, 512], F32) for i in range(4)]
            cbf = [sbB("cbf%d" % i, [128, 512], BF16) for i in range(4)]
            sqb = [sbB("sqb%d" % i, [128, 512], BF16) for i in range(4)]
            rstd = sbB("rstd", [128, 512], F32)
            R_acc = [Res("acc") for _ in range(4)]
            R_acp = [Res("acp") for _ in range(4)]
            R_cbf = [Res("cbf") for _ in range(4)]
            R_sqb = [Res("sqb") for _ in range(4)]
            R_rstd = Res("rstd")
            ND = CW - NTAP_POOL
            for g in range(NG):
                base = 2 + g * 512
                rd = [R_uT[g], R_uh] + ([R_uT[g - 1]] if g > 0 else [])
                for w in range(ND):
                    for c in range(4):
                        src = uT[:, c, base + w:base + w + 512]
                        wv = cw_sb[:, c * CW + w:c * CW + w + 1]
                        if w == 0:
                            P.ts("dve", acc[c][:], src, wv, cvec_sb[:, c:c + 1], ALU.mult, ALU.add,
                                 rd + [R_const], [R_acc[c]])
                        else:
                            P.stt("dve", acc[c][:], src, wv, acc[c][:], ALU.mult, ALU.add,
                                  rd + [R_const, R_acc[c]], [R_acc[c]])
                for w in range(ND, CW):
                    for c in range(4):
                        src = uT[:, c, base + w:base + w + 512]
                        wv = cw_sb[:, c * CW + w:c * CW + w + 1]
                        if w == ND:
                            P.ts("pool", acp[c][:], src, wv, None, ALU.mult, ALU.bypass, rd + [R_const], [R_acp[c]])
                        else:
                            P.stt("pool", acp[c][:], src, wv, acp[c][:], ALU.mult, ALU.add,
                                  rd + [R_const, R_acp[c]], [R_acp[c]])
                pm, Rpm = next_ps()
                for c in range(4):
                    if NTAP_POOL > 0:
                        P.tt("dve", acc[c][:], acc[c][:], acp[c][:], ALU.add, [R_acc[c], R_acp[c]], [R_acc[c]])
                    P.copy("pool", cbf[c][:], acc[c][:], [R_acc[c]], [R_cbf[c]])
                for c in range(4):
                    P.mm(pm[:], onesM[:], cbf[c][:], c == 0, c == 3, [R_const, R_cbf[c]], [Rpm], sig=(c == 3))
                pv, Rpv = next_ps()
                for c in range(4):
                    P.tt("dve", acc[c][:], acc[c][:], pm[:], ALU.subtract, [R_acc[c], Rpm], [R_acc[c]])
                    P.act(sqb[c][:], acc[c][:], AF.Square, [R_acc[c]], [R_sqb[c]])
                for c in range(4):
                    P.mm(pv[:], onesM[:], sqb[c][:], c == 0, c == 3, [R_const, R_sqb[c]], [Rpv], sig=(c == 3))
                P.act(rstd[:], pv[:], AF.Ln, [Rpv], [R_rstd], bias=EPS)
                P.act(rstd[:], rstd[:], AF.Exp, [R_rstd], [R_rstd], scale=-0.5)
                for c in range(4):
                    P.tt("dve", acc[c][:], acc[c][:], rstd[:], ALU.mult, [R_acc[c], R_rstd], [R_acc[c]])
                    P.act(sT[:, c, g * 512:(g + 1) * 512], acc[c][:], AF.Silu, [R_acc[c], R_const], [R_sT[g]],
                          bias=cvec_sb[:, 8 + c:9 + c], scale=cvec_sb[:, 4 + c:5 + c])
            P.barrier()
            b_es.close()

            c_es = ExitStack()
            sbC = lambda name, shape, dtype: sb("%s_q%d" % (name, seq), shape, dtype, c_es)
            NB = 4
            Ef = [sbC("Ef%d" % i, [128, 512], F32) for i in range(NB)]
            Lb = [sbC("Lb%d" % i, [128, 512], BF16) for i in range(NB)]
            aT = [sbC("aT%d" % i, [128, 512], BF16) for i in range(NB)]
            Lsum = [sbC("Lsum%d" % i, [128, 512], BF16) for i in range(2)]
            R_Ef = [Res("Ef") for _ in range(NB)]
            R_Lb = [Res("Lb") for _ in range(NB)]
            R_aT = [Res("aT") for _ in range(NB)]
            R_Lsum = [Res("Lsum") for _ in range(2)]
            zbank = [(ps[i], R_ps[i]) for i in range(4)]
            obank = [(ps[4], R_ps[4]), (ps[5], R_ps[5])]
            units = []
            hc = 0
            for c in range(NG):
                for hp in range(4):
                    for hh in range(2):
                        h = hp * 2 + hh
                        nkb = 4 * c + 4
                        for ui, kbk in enumerate(range(nkb - 1, -1, -1)):
                            i_ = kbk - 4 * c
                            col0 = max(i_, 0) * 128
                            units.append(dict(h=h, c=c, kb=kbk, diag=(i_ >= 0), col0=col0, first=(ui == 0),
                                              last=(kbk == 0), hc=hc, ob=(c * 4 + hp) % 2))
                        hc += 1
            NU = len(units)

            def P1(u):
                U = units[u]
                zb, Rz = zbank[u % 4]
                h, c, kbk, col0 = U["h"], U["c"], U["kb"], U["col0"]
                p0 = (h % 2) * 64
                P.mm(zb[:, col0:512], kT[p0:p0 + 64, h // 2, kbk * 128:(kbk + 1) * 128],
                     qT[p0:p0 + 64, h // 2, c * 512 + col0:(c + 1) * 512], True, True,
                     [R_kT[kbk], R_qT[c]], [Rz], sig=True)

            def A1(u):
                U = units[u]
                zb, Rz = zbank[u % 4]
                h, col0 = U["h"], U["col0"]
                b = u % NB
                P.act(Ef[b][:, col0:512], zb[:, col0:512], AF.Exp, [Rz, R_const], [R_Ef[b]], bias=biasT[:, h:h + 1])

            def A2(u):
                U = units[u]
                col0 = U["col0"]
                b = u % NB
                P.act(Lb[b][:, col0:512], Ef[b][:, col0:512], AF.Ln, [R_Ef[b]], [R_Lb[b]], bias=1.0)
                if U["diag"]:
                    P.tt("dve", Lb[b][:, col0:col0 + 128], Lb[b][:, col0:col0 + 128], tri[:], ALU.mult,
                         [R_Lb[b], R_const], [R_Lb[b]])

            def P2G(u):
                U = units[u]
                zb, Rz = zbank[u % 4]
                col0 = U["col0"]
                b = u % NB
                ls = U["hc"] % 2
                P.mm(zb[:, col0:512], negU[:], Lb[b][:, col0:512], False, U["first"], [R_const, R_Lb[b]], [Rz],
                     sig=U["first"])
                if not U["first"]:
                    P.mm(zb[:, col0:512], negOnes[:], Lsum[ls][:, col0:512], False, True, [R_const, R_Lsum[ls]], [Rz],
                         sig=True)
                if not U["last"]:
                    if U["first"]:
                        P.memset("pool", Lsum[ls][:], 0.0, [R_Lsum[ls]])
                    P.tt("pool", Lsum[ls][:, col0:512], Lsum[ls][:, col0:512], Lb[b][:, col0:512], ALU.add,
                         [R_Lsum[ls], R_Lb[b]], [R_Lsum[ls]])

            def A3(u):
                U = units[u]
                zb, Rz = zbank[u % 4]
                h, col0 = U["h"], U["col0"]
                b = u % NB
                P.act(aT[b][:, col0:512], zb[:, col0:512], AF.Exp, [Rz, R_const], [R_aT[b]], bias=biasT[:, h:h + 1])
                if U["diag"]:
                    P.tt("dve", aT[b][:, col0:col0 + 128], aT[b][:, col0:col0 + 128], tri[:], ALU.mult,
                         [R_aT[b], R_const], [R_aT[b]])

            def P3(u):
                U = units[u]
                h, c, kbk, col0 = U["h"], U["c"], U["kb"], U["col0"]
                b = u % NB
                ob, Rob = obank[U["ob"]]
                p0 = (h % 2) * 64
                if U["first"]:
                    P.mm(ob[p0:p0 + 64, :], zer[:, 0:64], zer[:, :], True, False, [R_const], [Rob])
                P.mm(ob[p0:p0 + 64, col0:512], vv[:, kbk, h * 64:(h + 1) * 64], aT[b][:, col0:512], False, U["last"],
                     [R_vv[kbk], R_aT[b]], [Rob], sig=True)
                if U["last"]:
                    P.copy("dve", oT[p0:p0 + 64, h // 2, c * 512:(c + 1) * 512], ob[p0:p0 + 64, :], [Rob], [R_oT[c]])

            for p in range(-1, NU + 3):
                if 0 <= p - 3 < NU:
                    P3(p - 3)
                if 0 <= p + 1 < NU:
                    P1(p + 1)
                if 0 <= p < NU:
                    A1(p)
                if 0 <= p - 2 < NU:
                    A3(p - 2)
                if 0 <= p < NU:
                    A2(p)
                if 0 <= p - 1 < NU:
                    P2G(p - 1)
            P.barrier()
            c_es.close()
            abc_es.close()

            if do_s:
                s_sT, o_sT, R_ssT, R_osT = sample_S2(SB)
            d_es = ExitStack()
            sbD = lambda name, shape, dtype: sb("%s_q%d" % (name, seq), shape, dtype, d_es)
            ring = sbD("ring", [128, NSLOT, SLOTW], BF16)
            WS.buf = ring
            NTL = 5 if do_s else 4
            WD = 528 if do_s else 512
            xg = sbD("xg", [128, NTL, D], F32)
            xbD = [sbD("xbD%d" % i, [128, D], BF16) for i in range(2)]
            xTd = sbD("xTd", [128, 8, WD], BF16)
            actT = sbD("actT", [128, NJ, WD], BF16)
            mixT = actT
            pst_ = [sbD("pst%d" % i, [128, PLE], F32) for i in range(2)]
            pbD = [sbD("pbD%d" % i, [128, PLE], BF16) for i in range(2)]
            pT = sbD("pT", [128, 2, WD], BF16)
            sgD = [sbD("sgD%d" % i, [128, 512], F32) for i in range(3)]
            stat = sbD("stat", [128, NTL, 16], F32)
            R_xg = [Res("xg%d" % i) for i in range(NTL)]
            R_xbD = [Res("xbD") for _ in range(2)]
            R_xTd = Res("xTd")
            R_actT = Res("actT")
            R_pst = [Res("pst") for _ in range(2)]
            R_pbD = [Res("pbD") for _ in range(2)]
            R_pT = Res("pT")
            R_sgD = [Res("sgD") for _ in range(3)]
            R_st = [Res("stat%d" % i) for i in range(NTL)]
            s_xg = [P.newsem("s_xg%d_%d" % (seq, i)) for i in range(NTL)]
            s_pl = [P.newsem("s_pl%d_%d" % (seq, i)) for i in range(2)]
            s_yo = [P.newsem("s_yo%d_%d" % (seq, i)) for i in range(NTL)]
            out_sems += s_yo
            for r_ in WS.res:
                r_.w = None
                r_.r = []
            WS.limit = (seq + 1) * NG * NSL
            WS.prefetch()
            sgi = 0

            def layer_norm_all(tl, gcol, bcol):
                for hf in range(2):
                    for (t4, npt, _) in tl:
                        P.bn_stats(stat[0:npt, t4, hf * 6:hf * 6 + 6], xg[0:npt, t4, hf * 512:(hf + 1) * 512], [R_xg[t4]], [R_st[t4]])
                for (t4, npt, _) in tl:
                    P.bn_aggr(stat[0:npt, t4, 12:14], stat[0:npt, t4, 0:12], [R_st[t4]], [R_st[t4]])
                for (t4, npt, _) in tl:
                    P.act(stat[0:npt, t4, 14:15], stat[0:npt, t4, 13:14], AF.Ln, [R_st[t4]], [R_st[t4]], bias=EPS)
                for (t4, npt, _) in tl:
                    P.act(stat[0:npt, t4, 14:15], stat[0:npt, t4, 14:15], AF.Exp, [R_st[t4]], [R_st[t4]], scale=-0.5)
                for (t4, npt, _) in tl:
                    xt = xg[0:npt, t4, :]
                    P.ts("dve", xt, xt, stat[0:npt, t4, 12:13], stat[0:npt, t4, 14:15], ALU.subtract, ALU.mult, [R_xg[t4], R_st[t4]], [R_xg[t4]])
                for (t4, npt, _) in tl:
                    xt = xg[0:npt, t4, :]
                    P.tt("pool", xt, xt, lnb[0:npt, gcol, :], ALU.mult, [R_xg[t4], R_const], [R_xg[t4]])
                for (t4, npt, _) in tl:
                    xt = xg[0:npt, t4, :]
                    P.tt("pool", xt, xt, lnb[0:npt, bcol, :], ALU.add, [R_xg[t4], R_const], [R_xg[t4]])

            def to_T(t4, npt, tsl, src_rows, width, dstT, R_dst, stage, R_stage, eng_copy):
                nk = width // 128
                P.copy("pool", stage[0:npt, :], src_rows, R_stage[0], R_stage[1])
                pt_, Rpt = next_psT()
                for k in range(nk):
                    P.tr(pt_[:, k * 128:k * 128 + npt], stage[0:npt, k * 128:(k + 1) * 128], identb[0:npt, 0:npt],
                         R_stage[1] + [R_const], [Rpt], sig=(k == nk - 1))
                P.copy(eng_copy, dstT[:, :, tsl], pt_[:, 0:nk * 128].rearrange("p (k n) -> p k n", k=nk)[:, :, 0:npt],
                       [Rpt], [R_dst])

            for g in range(NG):
                gs = slice(g * 512, (g + 1) * 512)
                last = do_s and g == NG - 1
                tiles = [(t4, 128, slice(t4 * 128, (t4 + 1) * 128)) for t4 in range(4)]
                cgs = [(slice(0, 512), 512, 0)]
                if last:
                    tiles.append((4, 16, slice(512, 528)))
                    cgs.append((slice(512, 528), 16, 1))
                for (t4, npt, tsl) in tiles:
                    b = t4 % 2
                    if t4 < 4:
                        row0 = seq * S + (g * 4 + t4) * 128
                        P.dma("sp", xg[:, t4, :], xp[row0:row0 + 128, :], s_xg[t4], writes=[R_xg[t4]])
                    else:
                        P.dma("sp", xg[0:16, t4, :], xs[:, :], s_xg[t4], writes=[R_xg[t4]])
                    to_T(t4, npt, tsl, xg[0:npt, t4, :], D, xTd, R_xTd, xbD[b], ([R_xg[t4]], [R_xbD[b]]), "dve")
                for mh in range(2):
                    wP, RwP = WS.get(0)
                    wC, RwC = WS.get(1)
                    wA, RwA = WS.get(2)
                    for m4 in range(4):
                        m = mh * 4 + m4
                        cs = slice(m4 * 128, (m4 + 1) * 128)
                        for (csl, n, isS) in cgs:
                            sTs = s_sT if isS else sT
                            oTs = o_sT if isS else oT
                            Rs_ = R_ssT if isS else R_sT[g]
                            Ro_ = R_osT if isS else R_oT[g]
                            ssl = slice(0, 16) if isS else gs
                            pc, Rpc = next_ps()
                            for k in range(8):
                                P.mm(pc[:, 0:n], wC[:, k * 512:(k + 1) * 512][:, cs], xTd[:, k, csl], k == 0, k == 7,
                                     [RwC, R_xTd], [Rpc], sig=(k == 7))
                            s1 = sgi % 3; sgi += 1
                            P.act(sgD[s1][:, 0:n], pc[:, 0:n], AF.Sigmoid, [Rpc], [R_sgD[s1]])
                            pa, Rpa = next_ps()
                            for k in range(8):
                                P.mm(pa[:, 0:n], wA[:, k * 512:(k + 1) * 512][:, cs], xTd[:, k, csl], k == 0, k == 7,
                                     [RwA, R_xTd], [Rpa], sig=(k == 7))
                            s2 = sgi % 3; sgi += 1
                            P.act(sgD[s2][:, 0:n], pa[:, 0:n], AF.Sigmoid, [Rpa], [R_sgD[s2]])
                            po, Rpo = next_ps()
                            for k in range(4):
                                P.mm(po[:, 0:n], wP[:, k * 512:(k + 1) * 512][:, cs], sTs[:, k, ssl], k == 0, k == 3,
                                     [RwP, Rs_], [Rpo], sig=(k == 3))
                            P.tt("dve", sgD[s1][:, 0:n], sgD[s1][:, 0:n], po[:, 0:n], ALU.mult, [R_sgD[s1], Rpo], [R_sgD[s1]])
                            po2, Rpo2 = next_ps()
                            for k in range(4):
                                P.mm(po2[:, 0:n], wP[:, (4 + k) * 512:(5 + k) * 512][:, cs], oTs[:, k, ssl], k == 0, k == 3,
                                     [RwP, Ro_], [Rpo2], sig=(k == 3))
                            P.tt("dve", sgD[s2][:, 0:n], sgD[s2][:, 0:n], po2[:, 0:n], ALU.mult, [R_sgD[s2], Rpo2], [R_sgD[s2]])
                            P.tt("pool", mixT[:, m, csl], sgD[s1][:, 0:n], sgD[s2][:, 0:n], ALU.add, [R_sgD[s1], R_sgD[s2]], [R_actT])
                    WS.done(); WS.done(); WS.done()
                for hf in range(2):
                    wO, RwO = WS.get()
                    for (t4, npt, tsl) in tiles:
                        pw, Rpw = next_ps()
                        for k in range(8):
                            P.mm(pw[0:npt, :], mixT[:, k, tsl], wO[:, k * 512:(k + 1) * 512], k == 0, k == 7,
                                 [R_actT, RwO], [Rpw], sig=(k == 7))
                        P.stt("dve", xg[0:npt, t4, hf * 512:(hf + 1) * 512], xg[0:npt, t4, hf * 512:(hf + 1) * 512], ALPHA, pw[0:npt, :],
                              ALU.mult, ALU.add, [R_xg[t4], Rpw], [R_xg[t4]])
                    WS.done()
                layer_norm_all(tiles, 0, 1)
                for (t4, npt, tsl) in tiles:
                    b = t4 % 2
                    to_T(t4, npt, tsl, xg[0:npt, t4, :], D, xTd, R_xTd, xbD[b], ([R_xg[t4]], [R_xbD[b]]), "dve")
                    if t4 < 4:
                        row0 = seq * S + (g * 4 + t4) * 128
                        P.dma("sp", pst_[b][:], pp[row0:row0 + 128, :], s_pl[b], writes=[R_pst[b]])
                    else:
                        P.dma("sp", pst_[b][0:16, :], pps[:, :], s_pl[b], writes=[R_pst[b]])
                    to_T(t4, npt, tsl, pst_[b][0:npt, :], PLE, pT, R_pT, pbD[b], ([R_pst[b]], [R_pbD[b]]), "act")
                for jp in range(NJ // 2):
                    wU, RwU = WS.get()
                    for jj in range(2):
                        j = jp * 2 + jj
                        for (csl, n, isS) in cgs:
                            pg_, Rpg = next_ps()
                            pu, Rpu = next_ps()
                            for k in range(8):
                                o_ = k * 512 + jj * 128
                                P.mm(pg_[:, 0:n], wU[:, o_:o_ + 128], xTd[:, k, csl], k == 0, k == 7, [RwU, R_xTd], [Rpg], sig=(k == 7))
                            for k in range(8):
                                o_ = k * 512 + 256 + jj * 128
                                P.mm(pu[:, 0:n], wU[:, o_:o_ + 128], xTd[:, k, csl], k == 0, k == 7, [RwU, R_xTd], [Rpu], sig=(k == 7))
                            s1 = sgi % 3; sgi += 1
                            P.act(sgD[s1][:, 0:n], pg_[:, 0:n], AF.Silu, [Rpg], [R_sgD[s1]])
                            P.tt("dve", actT[:, j, csl], sgD[s1][:, 0:n], pu[:, 0:n], ALU.mult, [R_sgD[s1], Rpu], [R_actT])
                    WS.done()
                fbank = [0, 1, 2, 3, 4]
                for hf in range(2):
                    hs = slice(hf * 512, (hf + 1) * 512)
                    wG, RwG = WS.get(0)
                    wL, RwL = WS.get(1)
                    for (t4, npt, tsl) in tiles:
                        pgt, Rpgt = next_ps()
                        for k in range(8):
                            P.mm(pgt[0:npt, :], xTd[:, k, tsl], wG[:, k * 512:(k + 1) * 512], k == 0, k == 7,
                                 [R_xTd, RwG], [Rpgt], sig=(k == 7))
                        s1 = sgi % 3; sgi += 1
                        P.act(sgD[s1][0:npt, :], pgt[0:npt, :], AF.Sigmoid, [Rpgt], [R_sgD[s1]])
                        ppl, Rppl = next_ps()
                        for k in range(2):
                            P.mm(ppl[0:npt, :], pT[:, k, tsl], wL[:, k * 512:(k + 1) * 512], k == 0, k == 1,
                                 [R_pT, RwL], [Rppl], sig=(k == 1))
                        P.tt("dve", sgD[s1][0:npt, :], sgD[s1][0:npt, :], ppl[0:npt, :], ALU.mult, [R_sgD[s1], Rppl], [R_sgD[s1]])
                        P.stt("dve", xg[0:npt, t4, hs], xg[0:npt, t4, hs], ALPHA, sgD[s1][0:npt, :], ALU.mult, ALU.add,
                              [R_xg[t4], R_sgD[s1]], [R_xg[t4]])
                    WS.done()
                    kblk = 0
                    for si_ in range(3):
                        if si_ == 0:
                            wD, RwD, offs = wL, RwL, list(range(2, 8))
                        else:
                            wD, RwD = WS.get()
                            offs = list(range(8))
                        for (t4, npt, tsl) in tiles:
                            fb = fbank[t4]
                            for oi, o_ in enumerate(offs):
                                kk = kblk + oi
                                P.mm(ps[fb][0:npt, :], actT[:, kk, tsl], wD[:, o_ * 512:(o_ + 1) * 512], kk == 0, kk == NJ - 1,
                                     [R_actT, RwD], [R_ps[fb]], sig=(oi == len(offs) - 1))
                        kblk += len(offs)
                        WS.done()
                    for (t4, npt, tsl) in tiles:
                        fb = fbank[t4]
                        P.tt("dve", xg[0:npt, t4, hs], xg[0:npt, t4, hs], ps[fb][0:npt, :], ALU.add, [R_xg[t4], R_ps[fb]], [R_xg[t4]])
                layer_norm_all(tiles, 2, 3)
                for (t4, npt, tsl) in tiles:
                    if t4 < 4:
                        row0 = seq * S + (g * 4 + t4) * 128
                        P.dma("sp", yp[row0:row0 + 128, :], xg[:, t4, :], s_yo[t4], reads=[R_xg[t4]])
                    elif not DEBUG:
                        P.dma("sp", ys[:, :], xg[0:16, t4, :], s_yo[t4], reads=[R_xg[t4]])
            P.barrier()
            d_es.close()
            if do_s:
                samp_es.close()
            seq_es.close()

        P.barrier()
        block = es.enter_context(nc.Block())
        P.replay(block)
    return nc


def host_consts():
    c = np.zeros((128, 4 * 128), np.float32)
    c[:, 0:128] = np.eye(128, dtype=np.float32)
    j = np.arange(128)[:, None]
    s = np.arange(128)[None, :]
    c[:, 128:256] = np.where(j >= s, -1.0, 0.0)
    c[:, 256:384] = np.where(j < s, 1.0, 0.0)
    for jj in range(4):
        for hh in range(H):
            for q in range(4):
                c[jj, 384 + hh * 4 + q] = 1.0 if jj < q else 0.0
    return c


def make_in_maps(inputs, n_cores, S, NPG):
    f = lambda a: np.ascontiguousarray(np.asarray(a))
    x_prompt = f(inputs["x_prompt"]); p_prompt = f(inputs["p_prompt"])[0]
    x_sample = f(inputs["x_sample"]); p_sample = f(inputs["p_sample"])[0]
    ck = f(inputs["cache_k"])[0]; cv = f(inputs["cache_v"])[0]
    npool = ck.shape[0]
    ck = ck.reshape(npool * 128, 512); cv = cv.reshape(npool * 128, 512)
    sconv = f(inputs["state_conv"])[0]
    pt = f(inputs["page_table"]).astype(np.int32)
    cw = f(inputs["conv_w"])[0]
    cwT = np.ascontiguousarray(cw.reshape(CW, 4, 128).transpose(2, 1, 0).reshape(128, 4 * CW))
    vec = lambda a: f(a)[0].reshape(4, 128).T
    cvec = np.ascontiguousarray(np.concatenate([vec(inputs["conv_b"]), vec(inputs["conv_ln_g"]), vec(inputs["conv_ln_b"])], axis=1))
    lnp = np.ascontiguousarray(np.stack([f(inputs["ln1_g"])[0], f(inputs["ln1_b"])[0], f(inputs["ln2_g"])[0], f(inputs["ln2_b"])[0]]))
    sbb = f(inputs["sb_bias"]).reshape(1, H)
    brow = np.ascontiguousarray(np.broadcast_to(sbb.reshape(1, H, 1), (16, H, 4)).reshape(1, 512))
    consts = host_consts()
    shared = {
        "ck": ck, "cv": cv, "cwT": cwT, "cvec": cvec, "lnp": lnp, "sbb": sbb, "consts": consts, "brow": brow,
        "w_in": f(inputs["w_in"])[0], "w_conv_proj": f(inputs["w_conv_proj"])[0], "w_att_proj": f(inputs["w_att_proj"])[0],
        "w_out": f(inputs["w_out"])[0], "w_ffn_up": f(inputs["w_ffn_up"])[0], "w_ffn_down": f(inputs["w_ffn_down"])[0],
        "w_ple_gate": f(inputs["w_ple_gate"])[0], "w_ple": f(inputs["w_ple"])[0],
    }
    maps = []
    for c in range(n_cores):
        m = dict(shared)
        m["xp"] = x_prompt[2 * c:2 * c + 2].reshape(2 * S, D)
        m["pp"] = p_prompt[2 * c:2 * c + 2].reshape(2 * S, PLE)
        m["xs"] = x_sample[4 * c:4 * c + 4].reshape(16, D)
        m["pps"] = p_sample[4 * c:4 * c + 4].reshape(16, PLE)
        m["sconv"] = sconv[4 * c:4 * c + 4].reshape(120, CC)
        m["ptab"] = pt[4 * c:4 * c + 4].reshape(1, 4 * NPG)
        maps.append(m)
    return maps, npool


def assemble(results, n_cores, S):
    cat = lambda k: np.concatenate([np.asarray(r[k]) for r in results], axis=0)
    B = 2 * n_cores
    DB = 4 * n_cores
    y_p = cat("yp").reshape(B, S, D)
    y_s = cat("ys").reshape(DB, 4, D)
    k_p = cat("kp").reshape(1, B, S, H, DH)
    v_p = cat("vp").reshape(1, B, S, H, DH)
    c_p = cat("cpo").reshape(1, B, 30, CC)
    k_s = cat("ksn").reshape(1, DB, 4, H, DH)
    v_s = cat("vsn").reshape(1, DB, 4, H, DH)
    c_s = cat("csn").reshape(1, DB, 30, CC)
    return tuple(np.ascontiguousarray(a, dtype=np.float32) for a in (y_p, y_s, k_p, v_p, c_p, k_s, v_s, c_s))


def run(inputs, n_cores, S, NPG):
    maps, npool = make_in_maps(inputs, n_cores, S, NPG)
    nc = build(S, NPG, npool)
    res = run_bass_kernel_spmd(nc, maps, core_ids=list(range(n_cores)))
    return assemble(res.results, n_cores, S)


def kernel(**inputs):
    return run(inputs, 8, 2048, 128)
```

```python
import numpy as np
from contextlib import ExitStack
import concourse.bass as bass
import concourse.mybir as mybir
from concourse.bass_utils import run_bass_kernel_spmd

F32 = mybir.dt.float32
BF16 = mybir.dt.bfloat16
I32 = mybir.dt.int32
AF = mybir.ActivationFunctionType
ALU = mybir.AluOpType

D = 1024
H = 8
DH = 64
CC = 512
FF = 2816
NJ = FF // 128
PLE = 256
INC = 4608
CW = 31
ALPHA = float(2.0 ** 0.25)
EPS = 1e-5
NSLOT = 6
SLOTW = 4096
SELF_WAIT = ('act', 'dve', 'pool')
DEBUG = False
NTAP_POOL = 0

ENGS = ("pe", "act", "dve", "pool", "sp")


class Sem:
    def __init__(self, h):
        self.h = h
        self.n = 0


class Res:
    __slots__ = ("name", "w", "r")

    def __init__(self, name):
        self.name = name
        self.w = None
        self.r = []


class Prog:
    def __init__(self, nc, es):
        self.nc = nc
        self.es = es
        self.q = {e: [] for e in ENGS}
        self.psem = {e: Sem(es.enter_context(nc.semaphore("prog_" + e))) for e in ENGS}
        self.waited = {e: {} for e in ENGS}
        self.nsem = 0
        self.dma_tickets = []

    def newsem(self, name):
        self.nsem += 1
        return Sem(self.es.enter_context(self.nc.semaphore(name)))

    def _deps(self, eng, reads, writes, extra):
        deps = list(extra)
        for r in reads:
            if r.w is not None:
                deps.append(r.w)
        for w in writes:
            if w.w is not None:
                deps.append(w.w)
            deps.extend(w.r)
        best = {}
        for d in deps:
            if d is None:
                continue
            s, v = d
            if s is self.psem[eng] and eng not in SELF_WAIT:
                continue
            if best.get(id(s), (None, 0))[1] < v:
                best[id(s)] = (s, v)
        out = []
        for k, (s, v) in best.items():
            if self.waited[eng].get(k, 0) < v:
                self.waited[eng][k] = v
                out.append((s, v))
        return out

    def _commit(self, tk, reads, writes):
        for r in reads:
            r.r.append(tk)
            if len(r.r) > 24:
                best = {}
                for s, v in r.r:
                    if best.get(id(s), (None, 0))[1] < v:
                        best[id(s)] = (s, v)
                r.r = list(best.values())
        for w in writes:
            w.w = tk
            w.r = []

    def emit(self, eng, fn, reads=(), writes=(), sig=True, extra=()):
        waits = self._deps(eng, reads, writes, extra)
        ps = self.psem[eng]
        if sig:
            ps.n += 1
            tk = (ps, ps.n)
        else:
            tk = (ps, ps.n + 1)
        self.q[eng].append((waits, fn, sig))
        self._commit(tk, reads, writes)
        return tk

    def dma(self, eng, out, in_, sem, reads=(), writes=(), extra=(), indirect=None):
        waits = self._deps(eng, reads, writes, extra)
        sem.n += 16
        tk = (sem, sem.n)
        if indirect is None:
            fn = lambda e: e.dma_start(out=out, in_=in_)
        else:
            fn = lambda e: e.indirect_dma_start(out=out, out_offset=None, in_=in_,
                                                in_offset=bass.IndirectOffsetOnAxis(ap=indirect, axis=0))
        self.q[eng].append((waits, fn, sem))
        self._commit(tk, reads, writes)
        self.dma_tickets.append(tk)
        return tk

    def barrier(self):
        tks = [(self.psem[e], self.psem[e].n) for e in ENGS if self.psem[e].n > 0]
        best = {}
        for s, v in self.dma_tickets:
            if best.get(id(s), (None, 0))[1] < v:
                best[id(s)] = (s, v)
        tks += list(best.values())
        self.dma_tickets = []
        for e in ENGS:
            waits = self._deps(e, (), (), tks)
            if waits:
                self.q[e].append((waits, None, False))

    def replay(self, block):
        nc = self.nc

        def run(name, e):
            ps = self.psem[name]
            for waits, fn, sig in self.q[name]:
                for s, v in waits:
                    e.wait_ge(s.h, v)
                if fn is None:
                    continue
                ins = fn(e)
                if sig is True:
                    ins.then_inc(ps.h, 1)
                elif sig is not False:
                    ins.then_inc(sig.h, 16)

        @block.tensor
        def _(e):
            run("pe", e)

        @block.scalar
        def _(e):
            run("act", e)

        @block.vector
        def _(e):
            run("dve", e)

        @block.gpsimd
        def _(e):
            run("pool", e)

        @block.sync
        def _(e):
            run("sp", e)

    def mm(self, out, lhsT, rhs, start, stop, reads, writes, sig=False, extra=()):
        return self.emit("pe", lambda e: e.matmul(out, lhsT=lhsT, rhs=rhs, start=start, stop=stop,
                                                  skip_group_check=True),
                         reads, writes, sig, extra)

    def tr(self, out, in_, ident, reads, writes, sig=False):
        return self.emit("pe", lambda e: e.transpose(out=out, in_=in_, identity=ident), reads, writes, sig)

    def act(self, out, in_, func, reads, writes, bias=0.0, scale=1.0, extra=()):
        return self.emit("act", lambda e: e.activation(out=out, in_=in_, func=func, bias=bias, scale=scale),
                         reads, writes, True, extra)

    def copy(self, eng, out, in_, reads, writes):
        if eng == "act":
            return self.act(out, in_, AF.Copy, reads, writes)
        return self.emit(eng, lambda e: e.tensor_copy(out=out, in_=in_), reads, writes)

    def tt(self, eng, out, in0, in1, op, reads, writes):
        return self.emit(eng, lambda e: e.tensor_tensor(out=out, in0=in0, in1=in1, op=op), reads, writes)

    def ts(self, eng, out, in0, s1, s2, op0, op1, reads, writes):
        return self.emit(eng, lambda e: e.tensor_scalar(out=out, in0=in0, scalar1=s1, scalar2=s2, op0=op0, op1=op1),
                         reads, writes)

    def stt(self, eng, out, in0, scalar, in1, op0, op1, reads, writes):
        return self.emit(eng, lambda e: e.scalar_tensor_tensor(out=out, in0=in0, scalar=scalar, in1=in1,
                                                               op0=op0, op1=op1), reads, writes)

    def bn_stats(self, out, in_, reads, writes):
        return self.emit("dve", lambda e: e.bn_stats(out=out, in_=in_), reads, writes)

    def bn_aggr(self, out, in_, reads, writes):
        return self.emit("dve", lambda e: e.bn_aggr(out=out, in_=in_), reads, writes)

    def memset(self, eng, ap, val, writes):
        return self.emit(eng, lambda e: e.memset(ap, val), (), writes)


def slot_plan():
    slots = []
    for mh in range(2):
        c0 = mh * 512
        slots.append([("w_conv_proj", 0, 4, c0, 512, 0, 512), ("w_att_proj", 0, 4, c0, 512, 2048, 512)])
        slots.append([("w_in", 0, 8, 2560 + c0, 512, 0, 512)])
        slots.append([("w_in", 0, 8, 3584 + c0, 512, 0, 512)])
    for hf in range(2):
        slots.append([("w_out", 0, 8, hf * 512, 512, 0, 512)])
    for jp in range(NJ // 2):
        slots.append([("w_ffn_up", 0, 8, 2 * jp * 128, 256, 0, 512), ("w_ffn_up", 0, 8, FF + 2 * jp * 128, 256, 256, 512)])
    for hf in range(2):
        c0 = hf * 512
        slots.append([("w_ple_gate", 0, 8, c0, 512, 0, 512)])
        slots.append([("w_ple", 0, 2, c0, 512, 0, 512), ("w_ffn_down", 0, 6, c0, 512, 1024, 512)])
        slots.append([("w_ffn_down", 6, 8, c0, 512, 0, 512)])
        slots.append([("w_ffn_down", 14, 8, c0, 512, 0, 512)])
    return slots


def build(S, NPG, NPOOL, with_sample=True):
    NT = S // 128
    NG = S // 512
    nc = bass.Bass("TRN2", target_bir_lowering=False)
    dt = nc.dram_tensor

    def din(name, shape, dtype=F32):
        return dt(name, shape, dtype, kind="ExternalInput").ap()

    def dout(name, shape):
        return dt(name, shape, F32, kind="ExternalOutput").ap()

    xp = din("xp", [2 * S, D])
    pp = din("pp", [2 * S, PLE])
    xs = din("xs", [16, D])
    pps = din("pps", [16, PLE])
    ck = din("ck", [NPOOL * 128, 512])
    cv = din("cv", [NPOOL * 128, 512])
    sconv = din("sconv", [120, CC])
    ptab = din("ptab", [1, 4 * NPG], I32)
    W = {
        "w_in": din("w_in", [D, INC]),
        "w_conv_proj": din("w_conv_proj", [CC, D]),
        "w_att_proj": din("w_att_proj", [CC, D]),
        "w_out": din("w_out", [D, D]),
        "w_ffn_up": din("w_ffn_up", [D, 2 * FF]),
        "w_ffn_down": din("w_ffn_down", [FF, D]),
        "w_ple_gate": din("w_ple_gate", [D, D]),
        "w_ple": din("w_ple", [PLE, D]),
    }
    cwT = din("cwT", [128, 4 * CW])
    cvec = din("cvec", [128, 12])
    lnp = din("lnp", [4, D])
    sbb = din("sbb", [1, H])
    consts = din("consts", [128, 4 * 128])
    brow = din("brow", [1, 512])

    yp = dout("yp", [2 * S, D])
    ys = dout("ys", [16, D])
    kp = dout("kp", [2 * S, 512])
    vp = dout("vp", [2 * S, 512])
    cpo = dout("cpo", [60, CC])
    ksn = dout("ksn", [16, 512])
    vsn = dout("vsn", [16, 512])
    csn = dout("csn", [120, CC])

    slots = slot_plan()
    NSL = len(slots)
    scratch = dt("wscratch", [NSL, 128, SLOTW], BF16, kind="Internal").ap()
    scratchA = dt("wscratchA", [5, 128, SLOTW], BF16, kind="Internal").ap()

    with ExitStack() as es:
        P = Prog(nc, es)
        sb = lambda name, shape, dtype, st=es: st.enter_context(nc.sbuf_tensor(name, shape, dtype))
        pst = lambda name, shape, dtype, st=es: st.enter_context(nc.psum_tensor(name, shape, dtype))

        lnb = sb("lnb", [128, 4, D], F32)
        cw_sb = sb("cw_sb", [128, 4 * CW], F32)
        cvec_sb = sb("cvec_sb", [128, 12], F32)
        biasT = sb("biasT", [128, H], F32)
        cst_f = sb("cst_f", [128, 4 * 128], F32)
        identb = sb("identb", [128, 128], BF16)
        negU = sb("negU", [128, 128], BF16)
        negOnes = sb("negOnes", [128, 128], BF16)
        tri = sb("tri", [128, 128], BF16)
        onesM = sb("onesM", [128, 128], BF16)
        zer = sb("zer", [128, 512], BF16)
        identf = cst_f[:, 0:128]
        cf2 = sb("cf2", [128, 256], F32)
        brow_sb = sb("brow_sb", [1, 512], F32)

        psT = [pst("psT%d" % i, [128, 1024], BF16) for i in range(2)]
        ps = [pst("ps%d" % i, [128, 512], F32) for i in range(6)]
        R_psT = [Res("psT%d" % i) for i in range(2)]
        R_ps = [Res("ps%d" % i) for i in range(6)]
        rot = {"i": 0, "t": 0}

        def next_ps():
            rot["i"] = (rot["i"] + 1) % 6
            return ps[rot["i"]], R_ps[rot["i"]]

        def next_psT():
            rot["t"] = (rot["t"] + 1) % 2
            return psT[rot["t"]], R_psT[rot["t"]]

        R_const = Res("const")
        R_winA = Res("winA")
        s_c = P.newsem("s_const")
        for (o, i_) in ((lnb[:].rearrange("p a d -> p (a d)"), lnp.rearrange("a d -> (a d)").partition_broadcast(128)),
                        (cw_sb[:], cwT), (cvec_sb[:], cvec), (biasT[:], sbb.partition_broadcast(128)),
                        (cst_f[:], consts)):
            P.dma("sp", o, i_, s_c, writes=[R_const])
        P.copy("pool", identb[:], cst_f[:, 0:128], [R_const], [R_const])
        P.copy("pool", negU[:], cst_f[:, 128:256], [R_const], [R_const])
        P.copy("pool", tri[:], cst_f[:, 256:384], [R_const], [R_const])
        P.memset("pool", negOnes[:], -1.0, [R_const])
        P.memset("pool", onesM[:], 1.0 / 512.0, [R_const])
        P.memset("pool", zer[:], 0.0, [R_const])
        P.memset("pool", cf2[:, 0:128], -1.0, [R_const])
        P.memset("pool", cf2[:, 128:256], 1.0, [R_const])
        P.dma("sp", brow_sb[:], brow, P.newsem("s_brow"), writes=[R_const])
        R_winA = Res("winA")
        R_scrA = [Res("scrA%d" % i) for i in range(5)]
        s_wa = P.newsem("s_winA")

        class WStream:
            def __init__(self):
                self.n = 0
                self.sem = [P.newsem("s_slot%d" % i) for i in range(NSLOT)]
                self.res = [Res("slot%d" % i) for i in range(NSLOT)]
                self.scr = [Res("scr%d" % i) for i in range(NSL)]
                self.buf = None
                self.issued = 0
                self.total = 0

            def prefetch(self):
                while self.issued < self.limit and self.issued < self.n + NSLOT:
                    u = self.issued
                    si = u % NSLOT
                    sl = u % NSL
                    dst = self.buf[:, si, :]
                    P.dma("sp", dst, scratch[sl], self.sem[si], reads=[self.scr[sl]], writes=[self.res[si]])
                    self.issued += 1

            def get(self, ahead=0):
                u = self.n + ahead
                assert u < self.issued, "weight unit consumed before its load was emitted"
                si = u % NSLOT
                return self.buf[:, si, :], self.res[si]

            def done(self):
                self.n += 1
                self.prefetch()

        WS = WStream()

        pro_es = ExitStack()
        stf = [sb("stf%d" % i, [128, SLOTW], F32, pro_es) for i in range(3)]
        stb = [sb("stb%d" % i, [128, SLOTW], BF16, pro_es) for i in range(3)]
        R_stf = [Res("stf") for _ in range(3)]
        R_stb = [Res("stb") for _ in range(3)]
        s_pld = [P.newsem("s_pld%d" % i) for i in range(3)]
        s_pst = [P.newsem("s_pst%d" % i) for i in range(3)]
        allslots = [(scratchA[cb], [("w_in", 0, 8, cb * 512, 512, 0, 512)], R_scrA[cb]) for cb in range(5)]
        for n, (dst_dram, plan, Rscr) in enumerate(allslots):
            i = n % 3
            for (wn, k0, nk, c0, ncols, off, ks) in plan:
                srcv = W[wn][k0 * 128:(k0 + nk) * 128, c0:c0 + ncols].rearrange("(k p) c -> p k c", p=128)
                dstv = stf[i][:, :].rearrange("p (k c) -> p k c", c=ks)[:, off // ks:off // ks + nk, off % ks:off % ks + ncols]
                P.dma("sp", dstv, srcv, s_pld[i], writes=[R_stf[i]])
            P.copy(("act", "dve", "pool")[n % 3], stb[i][:], stf[i][:], [R_stf[i]], [R_stb[i]])
            P.dma("sp", dst_dram, stb[i][:], s_pst[i], reads=[R_stb[i]], writes=[Rscr])
        P.barrier()
        pro_es.close()
        WS.limit = 0


        samp_es = ExitStack()

        def sample_alloc():
            sbP = lambda name, shape, dtype: sb("SP_" + name, shape, dtype, samp_es)
            d_ = dict(
                s_sT=sbP("s_sT", [128, 4, 16], BF16), o_sT=sbP("o_sT", [128, 4, 16], BF16),
                Qblk=sbP("Qblk", [128, 4, 4, 32], BF16),
                knT=[sbP("knT%d" % b, [128, 4, 4], BF16) for b in range(4)],
                vnb=[sbP("vnb%d" % b, [4, 512], BF16) for b in range(4)],
                idx=sbP("idx", [128, 4 * NPG], I32))
            return d_

        def sample_S1(winA, SBP):
            s1_es = ExitStack()
            sbS = lambda name, shape, dtype: sb("S_" + name, shape, dtype, s1_es)
            xsT = sbS("xsT", [128, 8, 16], BF16)
            s_sT, o_sT, Qblk, knT, vnb, idx = [SBP[k] for k in ("s_sT", "o_sT", "Qblk", "knT", "vnb", "idx")]
            R_xsT, R_ssT, R_osT = Res("xsT"), Res("s_sT"), Res("o_sT")
            PB = min(16, NPG)
            NBT = NPG // PB
            NW = PB * 32
            xs_f = sbS("xs_f", [16, D], F32)
            xs_b = sbS("xs_b", [16, D], BF16)
            qs_b = sbS("qs_b", [16, 512], BF16)
            us = sbS("us", [16, 512], F32)
            sgs = sbS("sgs", [16, 512], F32)
            sc_sb = sbS("sc_sb", [120, CC], F32)
            fullT = sbS("fullT", [128, 4, 4, 34], F32)
            kn = [sbS("kn%d" % b, [4, 512], F32) for b in range(4)]
            vn = [sbS("vn%d" % b, [4, 512], F32) for b in range(4)]
            knb = [sbS("knb%d" % b, [4, 512], BF16) for b in range(4)]
            qsT = sbS("qsT", [128, 4, 16], BF16)
            accs = sbS("accs", [128, 4, 16], F32)
            cbs = sbS("cbs", [128, 4, 16], BF16)
            sqs = sbS("sqs", [128, 4, 16], BF16)
            rstd_s = sbS("rstd_s", [128, 16], F32)
            pt_i = sbS("pt_i", [128, 4 * NPG], I32)
            pt_f = sbS("pt_f", [128, 4 * NPG], F32)
            io_f = sbS("io_f", [128, 1], F32)
            R = {n: Res(n) for n in ("xsf", "xsb", "qsb", "us", "sgs", "sc", "fullT", "qsT", "Qblk", "accs", "cbs", "sqs",
                                     "rstd", "idx", "En", "Lnw", "an", "Es", "Lbs", "Lsuf", "o32s", "otok", "otb")}
            R_kn = [Res("kn") for _ in range(4)]; R_vn = [Res("vn") for _ in range(4)]
            R_knb = [Res("knb") for _ in range(4)]; R_vnb = [Res("vnb") for _ in range(4)]
            R_knT = [Res("knT") for _ in range(4)]
            R_kpg = [Res("kpg") for _ in range(2)]; R_vpg = [Res("vpg") for _ in range(2)]
            R_kTp = [Res("kTp") for _ in range(4)]
            R_aTs = [Res("aTs") for _ in range(2)]
            s_in = P.newsem("s_sin")
            s_so = P.newsem("s_sout")
            s_kp = [P.newsem("s_kp%d" % i) for i in range(2)]
            s_vp = [P.newsem("s_vp%d" % i) for i in range(2)]
            s_o32 = P.newsem("s_o32")
            mnew = cst_f[0:4, 384:416]

            P.dma("sp", xs_f[:], xs[:, :], s_in, writes=[R["xsf"]])
            P.dma("sp", sc_sb[:], sconv[:, :], P.newsem("s_sin2"), writes=[R["sc"]])
            P.dma("sp", pt_i[:], ptab.partition_broadcast(128), P.newsem("s_sin3"), writes=[R["idx"]])
            P.emit("pool", lambda e: e.iota(io_f[:], pattern=[[0, 1]], base=0, channel_multiplier=1,
                                            allow_small_or_imprecise_dtypes=True), (), [R["idx"]])
            P.copy("pool", pt_f[:], pt_i[:], [R["idx"]], [R["idx"]])
            P.ts("pool", idx[:], pt_f[:], 128.0, io_f[:, 0:1], ALU.mult, ALU.add, [R["idx"]], [R["idx"]])
            P.copy("pool", xs_b[:], xs_f[:], [R["xsf"]], [R["xsb"]])
            pt_, Rpt = next_psT()
            for k in range(8):
                P.tr(pt_[:, k * 128:k * 128 + 16], xs_b[:, k * 128:(k + 1) * 128], identb[0:16, 0:16],
                     [R["xsb"], R_const], [Rpt], sig=(k == 7))
            P.copy("dve", xsT[:], pt_[:].rearrange("p (k n) -> p k n", k=8)[:, :, 0:16], [Rpt], [R_xsT])

            def proj16(cb):
                pk, Rpk = next_ps()
                for k in range(8):
                    P.mm(pk[0:16, :], xsT[:, k, :], winA[:, k, cb * 512:(cb + 1) * 512], k == 0, k == 7,
                         [R_xsT, R_winA], [Rpk], sig=(k == 7))
                return pk, Rpk

            pa, Rpa = proj16(0)
            pb, Rpb = proj16(1)
            P.act(sgs[:], pb[0:16, :], AF.Sigmoid, [Rpb], [R["sgs"]])
            P.tt("dve", us[:], pa[0:16, :], sgs[:], ALU.mult, [Rpa, R["sgs"]], [R["us"]])
            pq, Rpq = proj16(2)
            P.act(qs_b[:], pq[0:16, :], AF.Copy, [Rpq], [R["qsb"]], scale=0.125)
            for b in range(4):
                P.dma("sp", csn[b * 30:b * 30 + 26, :], sc_sb[b * 30 + 4:b * 30 + 30, :], s_so, reads=[R["sc"]])
                P.dma("sp", csn[b * 30 + 26:b * 30 + 30, :], us[b * 4:b * 4 + 4, :], s_so, reads=[R["us"]])
            for b in range(4):
                for which in range(2):
                    dst, Rd, dstb, Rdb, outd = ((kn, R_kn, knb, R_knb, ksn), (vn, R_vn, vnb, R_vnb, vsn))[which]
                    pk, Rpk = next_ps()
                    c0 = 1536 + which * 512
                    for k in range(8):
                        P.mm(pk[0:4, :], xsT[:, k, b * 4:(b + 1) * 4], winA[:, k, c0:c0 + 512], k == 0, k == 7,
                             [R_xsT, R_winA], [Rpk], sig=(k == 7))
                    P.copy("dve", dst[b][:], pk[0:4, :], [Rpk], [Rd[b]])
                    P.dma("sp", outd[b * 4:(b + 1) * 4, :], dst[b][:], s_so, reads=[Rd[b]])
                    P.copy("pool", dstb[b][:], dst[b][:], [Rd[b]], [Rdb[b]])
                pt_, Rpt = next_psT()
                for c in range(4):
                    P.tr(pt_[:, c * 128:c * 128 + 4], knb[b][:, c * 128:(c + 1) * 128], identb[0:4, 0:4],
                         [R_knb[b], R_const], [Rpt], sig=(c == 3))
                P.copy("dve", knT[b][:], pt_[:, 0:512].rearrange("p (k n) -> p k n", k=4)[:, :, 0:4], [Rpt], [R_knT[b]])
            pt_, Rpt = next_psT()
            for c in range(4):
                P.tr(pt_[:, c * 128:c * 128 + 16], qs_b[:, c * 128:(c + 1) * 128], identb[0:16, 0:16],
                     [R["qsb"], R_const], [Rpt], sig=(c == 3))
            P.copy("dve", qsT[:], pt_[:, 0:512].rearrange("p (k n) -> p k n", k=4)[:, :, 0:16], [Rpt], [R["qsT"]])
            P.memset("pool", Qblk[:], 0.0, [R["Qblk"]])
            for c in range(4):
                for hh in range(2):
                    h = 2 * c + hh
                    P.copy("pool", Qblk[hh * 64:(hh + 1) * 64, c, :, h * 4:(h + 1) * 4],
                           qsT[hh * 64:(hh + 1) * 64, c, :].rearrange("p (b q) -> p b q", b=4),
                           [R["qsT"], R["Qblk"]], [R["Qblk"]])
            for c in range(4):
                pf, Rpf = next_ps()
                P.tr(pf[:, 0:120], sc_sb[0:120, c * 128:(c + 1) * 128], identf[0:120, 0:120], [R["sc"], R_const], [Rpf])
                P.tr(pf[:, 128:144], us[0:16, c * 128:(c + 1) * 128], identf[0:16, 0:16], [R["us"], R_const], [Rpf], sig=True)
                P.copy("dve", fullT[:, c, :, 0:30], pf[:, 0:120].rearrange("p (b r) -> p b r", b=4), [Rpf], [R["fullT"]])
                P.copy("dve", fullT[:, c, :, 30:34], pf[:, 128:144].rearrange("p (b r) -> p b r", b=4), [Rpf], [R["fullT"]])
            for w in range(CW):
                for c in range(4):
                    src = fullT[:, c, :, w:w + 4]
                    dst = accs[:, c, :].rearrange("p (b t) -> p b t", b=4)
                    wv = cw_sb[:, c * CW + w:c * CW + w + 1]
                    if w == 0:
                        P.ts("dve", dst, src, wv, cvec_sb[:, c:c + 1], ALU.mult, ALU.add, [R["fullT"], R_const], [R["accs"]])
                    else:
                        P.stt("dve", dst, src, wv, dst, ALU.mult, ALU.add, [R["fullT"], R_const, R["accs"]], [R["accs"]])
            P.copy("pool", cbs[:], accs[:], [R["accs"]], [R["cbs"]])
            pm, Rpm = next_ps()
            for c in range(4):
                P.mm(pm[:, 0:16], onesM[:], cbs[:, c, :], c == 0, c == 3, [R_const, R["cbs"]], [Rpm], sig=(c == 3))
            for c in range(4):
                P.tt("dve", accs[:, c, :], accs[:, c, :], pm[:, 0:16], ALU.subtract, [R["accs"], Rpm], [R["accs"]])
            P.act(sqs[:], accs[:], AF.Square, [R["accs"]], [R["sqs"]])
            pv, Rpv = next_ps()
            for c in range(4):
                P.mm(pv[:, 0:16], onesM[:], sqs[:, c, :], c == 0, c == 3, [R_const, R["sqs"]], [Rpv], sig=(c == 3))
            P.act(rstd_s[:], pv[:, 0:16], AF.Ln, [Rpv], [R["rstd"]], bias=EPS)
            P.act(rstd_s[:], rstd_s[:], AF.Exp, [R["rstd"]], [R["rstd"]], scale=-0.5)
            for c in range(4):
                P.tt("dve", accs[:, c, :], accs[:, c, :], rstd_s[:], ALU.mult, [R["accs"], R["rstd"]], [R["accs"]])
                P.act(s_sT[:, c, :], accs[:, c, :], AF.Silu, [R["accs"], R_const], [R_ssT],
                      bias=cvec_sb[:, 8 + c:9 + c], scale=cvec_sb[:, 4 + c:5 + c])

            P.barrier()
            s1_es.close()
            return dict(locals())

        def sample_S2(SB):
            g_ = SB
            (PB, NBT, NW, R, Qblk, knT, vnb, idx, mnew, s_sT, o_sT, R_ssT, R_osT,
             R_knT, R_vnb, s_kp, s_vp, s_o32, s_so) = [g_[k] for k in (
                "PB", "NBT", "NW", "R", "Qblk", "knT", "vnb", "idx", "mnew",
                "s_sT", "o_sT", "R_ssT", "R_osT", "R_knT", "R_vnb", "s_kp", "s_vp", "s_o32", "s_so")]
            R_kpg, R_vpg, R_kTp, R_aTs = g_["R_kpg"], g_["R_vpg"], g_["R_kTp"], g_["R_aTs"]
            s2_es = ExitStack()
            sb2 = lambda name, shape, dtype: sb("S2_" + name, shape, dtype, s2_es)
            kpg = [sb2("kpg%d" % i, [128, PB, 512], BF16) for i in range(2)]
            vpg = [sb2("vpg%d" % i, [128, PB, 512], BF16) for i in range(2)]
            kTp = [sb2("kTp%d" % i, [128, 512], BF16) for i in range(4)]
            Es = sb2("Es", [128, NW], F32)
            Lbs = sb2("Lbs", [128, PB, 32], F32)
            Lsuf = sb2("Lsuf", [128, PB + 1, 32], F32)
            aTs = [sb2("aTs%d" % i, [128, NW], BF16) for i in range(2)]
            En = sb2("En", [4, 32], F32)
            Lnw = sb2("Lnw", [4, 32], F32)
            an = sb2("an", [4, 32], BF16)
            o32s = sb2("o32s", [32, 512], F32)
            o_tok = sb2("o_tok", [16, 512], F32)
            o_tb = sb2("o_tb", [16, 512], BF16)
            negUf = cst_f[:, 128:256]
            cnt = 0
            rk = 0
            o32, Ro32 = ps[2], R_ps[2]
            zn, Rzn = ps[3], R_ps[3]
            batches = [(b_, nb_) for b_ in range(4) for nb_ in reversed(range(NBT))]

            def gather(i):
                if i >= len(batches):
                    return
                b_, nb_ = batches[i]
                buf_ = i % 2
                for pi in range(PB):
                    j = b_ * NPG + nb_ * PB + pi
                    P.dma("pool", kpg[buf_][:, pi, :], ck, s_kp[buf_], reads=[R["idx"]], writes=[R_kpg[buf_]],
                          indirect=idx[:, j:j + 1])
                    P.dma("pool", vpg[buf_][:, pi, :], cv, s_vp[buf_], reads=[R["idx"]], writes=[R_vpg[buf_]],
                          indirect=idx[:, j:j + 1])

            for b in range(4):
                for c in range(4):
                    P.mm(zn[0:4, 0:32], knT[b][:, c, :], Qblk[:, c, b, :], c == 0, False, [R_knT[b], R["Qblk"]], [Rzn])
                P.mm(zn[0:4, 0:32], cf2[0:1, 128:132], brow_sb[0:1, 0:32], False, True, [R_const], [Rzn], sig=True)
                P.act(En[:], zn[0:4, 0:32], AF.Exp, [Rzn], [R["En"]])
                P.act(Lnw[:], En[:], AF.Ln, [R["En"]], [R["Lnw"]], bias=1.0)
                P.tt("dve", Lnw[:], Lnw[:], mnew, ALU.mult, [R["Lnw"], R_const], [R["Lnw"]])
                P.mm(zn[0:4, 0:32], cst_f[0:4, 128:132], Lnw[:], False, True, [R_const, R["Lnw"]], [Rzn], sig=True)
                P.act(an[:], zn[0:4, 0:32], AF.Exp, [Rzn], [R["an"]])
                P.tt("dve", an[:], an[:], mnew, ALU.mult, [R["an"], R_const], [R["an"]])
                P.memset("pool", Lsuf[:, PB, :], 0.0, [R["Lsuf"]])
                P.copy("pool", Lsuf[0:4, PB, :], Lnw[:], [R["Lnw"], R["Lsuf"]], [R["Lsuf"]])
                P.mm(o32[0:32, :], an[:], vnb[b][:], True, False, [R["an"], R_vnb[b]], [Ro32], sig=True)
                for nb in reversed(range(NBT)):
                    buf = cnt % 2
                    zb, Rz = ps[cnt % 2], R_ps[cnt % 2]
                    if cnt == 0:
                        gather(0)
                    gather(cnt + 1)
                    cnt += 1
                    P.mm(zb[:, 0:NW], cf2[0:1, 128:256], brow_sb[0:1, 0:NW], True, False, [R_const], [Rz], sig=True)
                    for pi in range(PB):
                        pt_, Rpt = next_psT()
                        for c in range(4):
                            P.tr(pt_[:, c * 128:(c + 1) * 128], kpg[buf][:, pi, c * 128:(c + 1) * 128], identb[:],
                                 [R_kpg[buf], R_const], [Rpt], sig=(c == 3))
                        r = rk % 4
                        rk += 1
                        P.copy("act" if r % 2 else "dve", kTp[r][:], pt_[:, 0:512], [Rpt], [R_kTp[r]])
                        for c in range(4):
                            P.mm(zb[:, pi * 32:(pi + 1) * 32], kTp[r][:, c * 128:(c + 1) * 128], Qblk[:, c, b, :], False, False,
                                 [R_kTp[r], R["Qblk"]], [Rz], sig=(c == 3))
                    P.act(Es[:], zb[:, 0:NW], AF.Exp, [Rz], [R["Es"]])
                    P.act(Lbs[:].rearrange("p a b -> p (a b)"), Es[:], AF.Ln, [R["Es"]], [R["Lbs"]], bias=1.0)
                    for pi in reversed(range(PB)):
                        P.tt("dve", Lsuf[:, pi, :], Lsuf[:, pi + 1, :], Lbs[:, pi, :], ALU.add, [R["Lsuf"], R["Lbs"]], [R["Lsuf"]])
                    P.mm(zb[:, 0:NW], negUf, Lbs[:].rearrange("p a b -> p (a b)"), False, False, [R_const, R["Lbs"]], [Rz])
                    P.mm(zb[:, 0:NW], cf2[:, 0:128], Lsuf[:, 1:PB + 1, :].rearrange("p a b -> p (a b)"), False, True,
                         [R_const, R["Lsuf"]], [Rz], sig=True)
                    P.act(aTs[buf][:], zb[:, 0:NW], AF.Exp, [Rz], [R_aTs[buf]])
                    P.copy("dve", Lsuf[:, PB, :], Lsuf[:, 0, :], [R["Lsuf"]], [R["Lsuf"]])
                    for pi in range(PB):
                        P.mm(o32[0:32, :], aTs[buf][:, pi * 32:(pi + 1) * 32], vpg[buf][:, pi, :], False,
                             (nb == 0 and pi == PB - 1), [R_aTs[buf], R_vpg[buf]], [Ro32], sig=(pi == PB - 1))
                P.copy("dve", o32s[:], o32[0:32, :], [Ro32], [R["o32s"]])
                for h in range(H):
                    P.dma("sp", o_tok[b * 4:(b + 1) * 4, h * 64:(h + 1) * 64], o32s[h * 4:(h + 1) * 4, h * 64:(h + 1) * 64],
                          s_o32, reads=[R["o32s"]], writes=[R["otok"]])
            P.copy("pool", o_tb[:], o_tok[:], [R["otok"]], [R["otb"]])
            if DEBUG:
                P.dma("sp", ys[:, 0:512], o_tok[:], s_so, reads=[R["otok"]])
            pt_, Rpt = next_psT()
            for c in range(4):
                P.tr(pt_[:, c * 128:c * 128 + 16], o_tb[:, c * 128:(c + 1) * 128], identb[0:16, 0:16],
                     [R["otb"], R_const], [Rpt], sig=(c == 3))
            P.copy("dve", o_sT[:], pt_[:, 0:512].rearrange("p (k n) -> p k n", k=4)[:, :, 0:16], [Rpt], [R_osT])
            P.barrier()
            s2_es.close()
            return s_sT, o_sT, R_ssT, R_osT

        R_out = Res("outs")
        out_sems = []

        for seq in range(2):
            seq_es = ExitStack()
            sbs = lambda name, shape, dtype: sb("%s_q%d" % (name, seq), shape, dtype, seq_es)
            sT = sbs("sT", [128, 4, S], BF16)
            oT = sbs("oT", [128, 4, S], BF16)
            do_s = with_sample and seq == 1
            if do_s:
                SBP = sample_alloc()
            R_sT = [Res("sT%d" % g) for g in range(NG)]
            R_oT = [Res("oT%d" % g) for g in range(NG)]

            abc_es = ExitStack()
            sba = lambda name, shape, dtype: sb("%s_q%d" % (name, seq), shape, dtype, abc_es)
            qT = sba("qT", [128, 4, S], BF16)
            kT = sba("kT", [128, 4, S], BF16)
            vv = sba("vv", [128, NT, 512], BF16)
            uT = sba("uT", [128, 4, 32 + S], BF16)
            R_qT = [Res("qT%d" % g) for g in range(NG)]
            R_kT = [Res("kT%d" % t) for t in range(NT)]
            R_vv = [Res("vv%d" % t) for t in range(NT)]
            R_uT = [Res("uT%d" % g) for g in range(NG)]
            R_uh = Res("uThist")
            P.memset("pool", uT[:, :, 0:32], 0.0, [R_uh])

            wa_es = ExitStack()
            winA = sb("winA_q%d" % seq, [128, 8, 2560], BF16, wa_es)
            a_es = ExitStack()
            sbA = lambda name, shape, dtype: sb("%s_q%d" % (name, seq), shape, dtype, a_es)
            R_winA.w = None
            R_winA.r = []
            for cb in range(5):
                P.dma("sp", winA[:, :, cb * 512:(cb + 1) * 512], scratchA[cb].rearrange("p (k c) -> p k c", c=512), s_wa,
                      reads=[R_scrA[cb]], writes=[R_winA])
            xst = [sbA("xst%d" % i, [128, D], F32) for i in range(2)]
            xb = [sbA("xb%d" % i, [128, D], BF16) for i in range(2)]
            xT = [sbA("xT%d" % i, [128, 8, 512], BF16) for i in range(1)] * 2
            kst = [sbA("kst%d" % i, [128, 512], F32) for i in range(1)] * 2
            vst = [sbA("vst%d" % i, [128, 512], F32) for i in range(1)] * 2
            kb = [sbA("kb%d" % i, [128, 512], BF16) for i in range(2)]
            sg = [sbA("sg%d" % i, [128, 512], F32) for i in range(2)]
            ust = sbA("ust", [128, 512], F32)
            R_xst = [Res("xst") for _ in range(2)]
            R_xb = [Res("xb") for _ in range(2)]
            R_xT = [Res("xT")] * 2
            R_kst = [Res("kst")] * 2
            R_vst = [Res("vst")] * 2
            R_kb = [Res("kb") for _ in range(2)]
            R_sg = [Res("sg") for _ in range(2)]
            R_ust = Res("ust")
            s_x = [P.newsem("s_x%d_%d" % (seq, i)) for i in range(2)]
            s_ko = [P.newsem("s_ko%d_%d" % (seq, i)) for i in range(2)]
            s_vo = [P.newsem("s_vo%d_%d" % (seq, i)) for i in range(2)]
            s_co = P.newsem("s_co%d" % seq)
            out_sems += s_ko + s_vo + [s_co]

            for g in range(NG):
                xTg, RxTg = xT[g % 2], R_xT[g % 2]
                for t4 in range(4):
                    t = g * 4 + t4
                    b = t % 2
                    row0 = seq * S + t * 128
                    P.dma("sp", xst[b][:], xp[row0:row0 + 128, :], s_x[b], writes=[R_xst[b]])
                    P.copy("pool", xb[b][:], xst[b][:], [R_xst[b]], [R_xb[b]])
                    pt_, Rpt = next_psT()
                    for k in range(8):
                        P.tr(pt_[:, k * 128:(k + 1) * 128], xb[b][:, k * 128:(k + 1) * 128], identb[:],
                             [R_xb[b], R_const], [Rpt], sig=(k == 7))
                    P.copy("dve", xTg[:, :, t4 * 128:(t4 + 1) * 128], pt_[:].rearrange("p (k n) -> p k n", k=8),
                           [Rpt], [RxTg])
                    for which in range(2):
                        pk, Rpk = next_ps()
                        c0 = 1536 + which * 512
                        for k in range(8):
                            P.mm(pk[:], xTg[:, k, t4 * 128:(t4 + 1) * 128], winA[:, k, c0:c0 + 512], k == 0, k == 7,
                                 [RxTg, R_winA], [Rpk], sig=(k == 7))
                        if which == 0:
                            P.copy("dve", kst[b][:], pk[:], [Rpk], [R_kst[b]])
                            P.dma("sp", kp[row0:row0 + 128, :], kst[b][:], s_ko[b], reads=[R_kst[b]])
                            P.copy("pool", kb[b][:], kst[b][:], [R_kst[b]], [R_kb[b]])
                            pt2, Rpt2 = next_psT()
                            for c in range(4):
                                P.tr(pt2[:, c * 128:(c + 1) * 128], kb[b][:, c * 128:(c + 1) * 128], identb[:],
                                     [R_kb[b], R_const], [Rpt2], sig=(c == 3))
                            P.copy("act", kT[:, :, t * 128:(t + 1) * 128],
                                   pt2[:, 0:512].rearrange("p (k n) -> p k n", k=4), [Rpt2], [R_kT[t]])
                        else:
                            P.copy("act", vst[b][:], pk[:], [Rpk], [R_vst[b]])
                            P.dma("sp", vp[row0:row0 + 128, :], vst[b][:], s_vo[b], reads=[R_vst[b]])
                            P.copy("pool", vv[:, t, :], vst[b][:], [R_vst[b]], [R_vv[t]])
                    if t == NT - 1:
                        pa, Rpa = next_ps()
                        pb, Rpb = next_ps()
                        for k in range(8):
                            P.mm(pa[:], xTg[:, k, t4 * 128:(t4 + 1) * 128], winA[:, k, 0:512], k == 0, k == 7,
                                 [RxTg, R_winA], [Rpa], sig=(k == 7))
                        for k in range(8):
                            P.mm(pb[:], xTg[:, k, t4 * 128:(t4 + 1) * 128], winA[:, k, 512:1024], k == 0, k == 7,
                                 [RxTg, R_winA], [Rpb], sig=(k == 7))
                        P.act(sg[0][:], pb[:], AF.Sigmoid, [Rpb], [R_sg[0]])
                        P.tt("dve", ust[:], pa[:], sg[0][:], ALU.mult, [Rpa, R_sg[0]], [R_ust])
                        P.dma("sp", cpo[seq * 30:(seq + 1) * 30, :], ust[98:128, :], s_co, reads=[R_ust])
                gs = slice(g * 512, (g + 1) * 512)
                for c in range(4):
                    pq, Rpq = next_ps()
                    for k in range(8):
                        P.mm(pq[:], winA[:, k, 1024 + c * 128:1024 + (c + 1) * 128], xTg[:, k, :], k == 0, k == 7,
                             [RxTg, R_winA], [Rpq], sig=(k == 7))
                    P.act(qT[:, c, gs], pq[:], AF.Copy, [Rpq], [R_qT[g]], scale=0.125)
                for c in range(4):
                    pa, Rpa = next_ps()
                    pb, Rpb = next_ps()
                    for k in range(8):
                        P.mm(pb[:], winA[:, k, 512 + c * 128:512 + (c + 1) * 128], xTg[:, k, :], k == 0, k == 7,
                             [RxTg, R_winA], [Rpb], sig=(k == 7))
                    for k in range(8):
                        P.mm(pa[:], winA[:, k, c * 128:(c + 1) * 128], xTg[:, k, :], k == 0, k == 7,
                             [RxTg, R_winA], [Rpa], sig=(k == 7))
                    P.act(sg[c % 2][:], pb[:], AF.Sigmoid, [Rpb], [R_sg[c % 2]])
                    P.tt("dve", uT[:, c, 32 + g * 512:32 + (g + 1) * 512], pa[:], sg[c % 2][:], ALU.mult,
                         [Rpa, R_sg[c % 2]], [R_uT[g]])
            P.barrier()
            a_es.close()
            if do_s:
                SB = sample_S1(winA, SBP)
            wa_es.close()

            c_es = ExitStack()
            sbB = lambda name, shape, dtype: sb("%s_q%d" % (name, seq), shape, dtype, c_es)
            acc = [sbB("acc%d" % i, [128, 512], F32) for i in range(4)]
            cbf = [sbB("cbf%d" % i, [128, 512], BF16) for i in range(4)]
            sqb = [sbB("sqb%d" % i, [128, 512], BF16) for i in range(4)]
            rstd = sbB("rstd", [128, 512], F32)
            R_acc = [Res("acc") for _ in range(4)]
            R_cbf = [Res("cbf") for _ in range(4)]
            R_sqb = [Res("sqb") for _ in range(4)]
            R_rstd = Res("rstd")

            def gen_B():
                for g in range(NG):
                    base = 2 + g * 512
                    rd = [R_uT[g], R_uh] + ([R_uT[g - 1]] if g > 0 else [])
                    for w in range(CW):
                        for c in range(4):
                            src_ = uT[:, c, base + w:base + w + 512]
                            wv = cw_sb[:, c * CW + w:c * CW + w + 1]
                            if w == 0:
                                P.ts("dve", acc[c][:], src_, wv, cvec_sb[:, c:c + 1], ALU.mult, ALU.add,
                                     rd + [R_const], [R_acc[c]])
                            else:
                                P.stt("dve", acc[c][:], src_, wv, acc[c][:], ALU.mult, ALU.add,
                                      rd + [R_const, R_acc[c]], [R_acc[c]])
                        yield
                    for c in range(4):
                        P.copy("pool", cbf[c][:], acc[c][:], [R_acc[c]], [R_cbf[c]])
                    yield
                    pm, Rpm = psT[0][:].bitcast(F32), R_psT[0]
                    for c in range(4):
                        P.mm(pm[:], onesM[:], cbf[c][:], c == 0, c == 3, [R_const, R_cbf[c]], [Rpm], sig=(c == 3))
                    yield
                    for c in range(4):
                        P.tt("dve", acc[c][:], acc[c][:], pm[:], ALU.subtract, [R_acc[c], Rpm], [R_acc[c]])
                        P.act(sqb[c][:], acc[c][:], AF.Square, [R_acc[c]], [R_sqb[c]])
                    yield
                    pv, Rpv = psT[1][:].bitcast(F32), R_psT[1]
                    for c in range(4):
                        P.mm(pv[:], onesM[:], sqb[c][:], c == 0, c == 3, [R_const, R_sqb[c]], [Rpv], sig=(c == 3))
                    yield
                    P.act(rstd[:], pv[:], AF.Ln, [Rpv], [R_rstd], bias=EPS)
                    yield
                    P.act(rstd[:], rstd[:], AF.Exp, [R_rstd], [R_rstd], scale=-0.5)
                    yield
                    for c in range(4):
                        P.tt("dve", acc[c][:], acc[c][:], rstd[:], ALU.mult, [R_acc[c], R_rstd], [R_acc[c]])
                        P.act(sT[:, c, g * 512:(g + 1) * 512], acc[c][:], AF.Silu, [R_acc[c], R_const], [R_sT[g]],
                              bias=cvec_sb[:, 8 + c:9 + c], scale=cvec_sb[:, 4 + c:5 + c])
                        yield

            bgs = [gen_B()]
            if seq == 0:
                stfC = [sbB("stfC%d" % i, [128, SLOTW], F32) for i in range(2)]
                stbC = [sbB("stbC%d" % i, [128, SLOTW], BF16) for i in range(2)]
                R_stfC = [Res("stfC") for _ in range(2)]
                R_stbC = [Res("stbC") for _ in range(2)]
                s_cld = [P.newsem("s_cld%d" % i) for i in range(2)]
                s_cst = [P.newsem("s_cst%d" % i) for i in range(2)]

                def conv_load(sl):
                    i = sl % 2
                    for (wn, k0, nk, c0, ncols, off, ks) in slots[sl]:
                        srcv = W[wn][k0 * 128:(k0 + nk) * 128, c0:c0 + ncols].rearrange("(k p) c -> p k c", p=128)
                        dstv = stfC[i][:, :].rearrange("p (k c) -> p k c", c=ks)[:, off // ks:off // ks + nk, off % ks:off % ks + ncols]
                        P.dma("sp", dstv, srcv, s_cld[i], writes=[R_stfC[i]])

                def gen_conv():
                    conv_load(0)
                    yield
                    for sl in range(NSL):
                        i = sl % 2
                        if sl + 1 < NSL:
                            conv_load(sl + 1)
                            yield
                        for q4 in range(4):
                            P.copy("dve", stbC[i][:, q4 * 1024:(q4 + 1) * 1024], stfC[i][:, q4 * 1024:(q4 + 1) * 1024],
                                   [R_stfC[i]], [R_stbC[i]])
                            yield
                        P.dma("sp", scratch[sl], stbC[i][:], s_cst[i], reads=[R_stbC[i]], writes=[WS.scr[sl]])
                        yield

                bgs.append(gen_conv())

            def bg_step():
                for gi in list(bgs):
                    try:
                        next(gi)
                    except StopIteration:
                        bgs.remove(gi)

            sbC = lambda name, shape, dtype: sb("%s_q%d" % (name, seq), shape, dtype, c_es)
            NB = 4
            Ef = [sbC("Ef%d" % i, [128, 512], F32) for i in range(NB)]
            Lb = [sbC("Lb%d" % i, [128, 512], BF16) for i in range(NB)]
            aT = [sbC("aT%d" % i, [128, 512], BF16) for i in range(NB)]
            Lsum = [sbC("Lsum%d" % i, [128, 512], BF16) for i in range(2)]
            R_Ef = [Res("Ef") for _ in range(NB)]
            R_Lb = [Res("Lb") for _ in range(NB)]
            R_aT = [Res("aT") for _ in range(NB)]
            R_Lsum = [Res("Lsum") for _ in range(2)]
            zbank = [(ps[i], R_ps[i]) for i in range(4)]
            obank = [(ps[4], R_ps[4]), (ps[5], R_ps[5])]
            units = []
            hc = 0
            for c in range(NG):
                for hp in range(4):
                    for hh in range(2):
                        h = hp * 2 + hh
                        nkb = 4 * c + 4
                        for ui, kbk in enumerate(range(nkb - 1, -1, -1)):
                            i_ = kbk - 4 * c
                            col0 = max(i_, 0) * 128
                            units.append(dict(h=h, c=c, kb=kbk, diag=(i_ >= 0), col0=col0, first=(ui == 0),
                                              last=(kbk == 0), hc=hc, ob=(c * 4 + hp) % 2))
                        hc += 1
            NU = len(units)

            def P1(u):
                U = units[u]
                zb, Rz = zbank[u % 4]
                h, c, kbk, col0 = U["h"], U["c"], U["kb"], U["col0"]
                p0 = (h % 2) * 64
                P.mm(zb[:, col0:512], kT[p0:p0 + 64, h // 2, kbk * 128:(kbk + 1) * 128],
                     qT[p0:p0 + 64, h // 2, c * 512 + col0:(c + 1) * 512], True, True,
                     [R_kT[kbk], R_qT[c]], [Rz], sig=True)

            def A1(u):
                U = units[u]
                zb, Rz = zbank[u % 4]
                h, col0 = U["h"], U["col0"]
                b = u % NB
                P.act(Ef[b][:, col0:512], zb[:, col0:512], AF.Exp, [Rz, R_const], [R_Ef[b]], bias=biasT[:, h:h + 1])

            def A2(u):
                U = units[u]
                col0 = U["col0"]
                b = u % NB
                P.act(Lb[b][:, col0:512], Ef[b][:, col0:512], AF.Ln, [R_Ef[b]], [R_Lb[b]], bias=1.0)
                if U["diag"]:
                    P.tt("pool", Lb[b][:, col0:col0 + 128], Lb[b][:, col0:col0 + 128], tri[:], ALU.mult,
                         [R_Lb[b], R_const], [R_Lb[b]])

            def P2G(u):
                U = units[u]
                zb, Rz = zbank[u % 4]
                col0 = U["col0"]
                b = u % NB
                ls = U["hc"] % 2
                P.mm(zb[:, col0:512], negU[:], Lb[b][:, col0:512], False, U["first"], [R_const, R_Lb[b]], [Rz],
                     sig=U["first"])
                if not U["first"]:
                    P.mm(zb[:, col0:512], negOnes[:], Lsum[ls][:, col0:512], False, True, [R_const, R_Lsum[ls]], [Rz],
                         sig=True)
                if not U["last"]:
                    if U["first"]:
                        P.memset("pool", Lsum[ls][:], 0.0, [R_Lsum[ls]])
                    P.tt("pool", Lsum[ls][:, col0:512], Lsum[ls][:, col0:512], Lb[b][:, col0:512], ALU.add,
                         [R_Lsum[ls], R_Lb[b]], [R_Lsum[ls]])

            def A3(u):
                U = units[u]
                zb, Rz = zbank[u % 4]
                h, col0 = U["h"], U["col0"]
                b = u % NB
                P.act(aT[b][:, col0:512], zb[:, col0:512], AF.Exp, [Rz, R_const], [R_aT[b]], bias=biasT[:, h:h + 1])
                if U["diag"]:
                    P.tt("pool", aT[b][:, col0:col0 + 128], aT[b][:, col0:col0 + 128], tri[:], ALU.mult,
                         [R_aT[b], R_const], [R_aT[b]])

            def P3(u):
                U = units[u]
                h, c, kbk, col0 = U["h"], U["c"], U["kb"], U["col0"]
                b = u % NB
                ob, Rob = obank[U["ob"]]
                p0 = (h % 2) * 64
                if U["first"]:
                    P.mm(ob[p0:p0 + 64, :], zer[:, 0:64], zer[:, :], True, False, [R_const], [Rob])
                P.mm(ob[p0:p0 + 64, col0:512], vv[:, kbk, h * 64:(h + 1) * 64], aT[b][:, col0:512], False, U["last"],
                     [R_vv[kbk], R_aT[b]], [Rob], sig=True)
                if U["last"]:
                    P.copy("act", oT[p0:p0 + 64, h // 2, c * 512:(c + 1) * 512], ob[p0:p0 + 64, :], [Rob], [R_oT[c]])

            for p in range(-1, NU + 3):
                if 0 <= p - 3 < NU:
                    P3(p - 3)
                if 0 <= p + 1 < NU:
                    P1(p + 1)
                if 0 <= p < NU:
                    A1(p)
                if 0 <= p - 2 < NU:
                    A3(p - 2)
                if 0 <= p < NU:
                    A2(p)
                if 0 <= p - 1 < NU:
                    P2G(p - 1)
                bg_step()
            while bgs:
                bg_step()
            P.barrier()
            c_es.close()
            abc_es.close()

            if do_s:
                s_sT, o_sT, R_ssT, R_osT = sample_S2(SB)
            d_es = ExitStack()
            sbD = lambda name, shape, dtype: sb("%s_q%d" % (name, seq), shape, dtype, d_es)
            ring = sbD("ring", [128, NSLOT, SLOTW], BF16)
            WS.buf = ring
            NTL = 5 if do_s else 4
            WD = 528 if do_s else 512
            xg = sbD("xg", [128, NTL, D], F32)
            xbD = [sbD("xbD%d" % i, [128, D], BF16) for i in range(2)]
            xTd = sbD("xTd", [128, 8, WD], BF16)
            actT = sbD("actT", [128, NJ, WD], BF16)
            mixT = actT
            pst_ = [sbD("pst%d" % i, [128, PLE], F32) for i in range(2)]
            pbD = [sbD("pbD%d" % i, [128, PLE], BF16) for i in range(2)]
            pT = sbD("pT", [128, 2, WD], BF16)
            sgD = [sbD("sgD%d" % i, [128, 512], F32) for i in range(3)]
            stat = sbD("stat", [128, NTL, 16], F32)
            R_xg = [Res("xg%d" % i) for i in range(NTL)]
            R_xbD = [Res("xbD") for _ in range(2)]
            R_xTd = Res("xTd")
            R_actT = Res("actT")
            R_pst = [Res("pst") for _ in range(2)]
            R_pbD = [Res("pbD") for _ in range(2)]
            R_pT = Res("pT")
            R_sgD = [Res("sgD") for _ in range(3)]
            R_st = [Res("stat%d" % i) for i in range(NTL)]
            s_xg = [P.newsem("s_xg%d_%d" % (seq, i)) for i in range(NTL)]
            s_pl = [P.newsem("s_pl%d_%d" % (seq, i)) for i in range(2)]
            s_yo = [P.newsem("s_yo%d_%d" % (seq, i)) for i in range(NTL)]
            out_sems += s_yo
            for r_ in WS.res:
                r_.w = None
                r_.r = []
            WS.limit = (seq + 1) * NG * NSL
            WS.prefetch()
            sgi = 0

            def layer_norm_all(tl, gcol, bcol):
                for hf in range(2):
                    for (t4, npt, _) in tl:
                        P.bn_stats(stat[0:npt, t4, hf * 6:hf * 6 + 6], xg[0:npt, t4, hf * 512:(hf + 1) * 512], [R_xg[t4]], [R_st[t4]])
                for (t4, npt, _) in tl:
                    P.bn_aggr(stat[0:npt, t4, 12:14], stat[0:npt, t4, 0:12], [R_st[t4]], [R_st[t4]])
                for (t4, npt, _) in tl:
                    P.act(stat[0:npt, t4, 14:15], stat[0:npt, t4, 13:14], AF.Ln, [R_st[t4]], [R_st[t4]], bias=EPS)
                for (t4, npt, _) in tl:
                    P.act(stat[0:npt, t4, 14:15], stat[0:npt, t4, 14:15], AF.Exp, [R_st[t4]], [R_st[t4]], scale=-0.5)
                for (t4, npt, _) in tl:
                    xt = xg[0:npt, t4, :]
                    P.ts("dve", xt, xt, stat[0:npt, t4, 12:13], stat[0:npt, t4, 14:15], ALU.subtract, ALU.mult, [R_xg[t4], R_st[t4]], [R_xg[t4]])
                for (t4, npt, _) in tl:
                    xt = xg[0:npt, t4, :]
                    P.tt("pool", xt, xt, lnb[0:npt, gcol, :], ALU.mult, [R_xg[t4], R_const], [R_xg[t4]])
                for (t4, npt, _) in tl:
                    xt = xg[0:npt, t4, :]
                    P.tt("pool", xt, xt, lnb[0:npt, bcol, :], ALU.add, [R_xg[t4], R_const], [R_xg[t4]])

            def to_T(t4, npt, tsl, src_rows, width, dstT, R_dst, stage, R_stage, eng_copy):
                nk = width // 128
                P.copy("pool", stage[0:npt, :], src_rows, R_stage[0], R_stage[1])
                pt_, Rpt = next_psT()
                for k in range(nk):
                    P.tr(pt_[:, k * 128:k * 128 + npt], stage[0:npt, k * 128:(k + 1) * 128], identb[0:npt, 0:npt],
                         R_stage[1] + [R_const], [Rpt], sig=(k == nk - 1))
                P.copy(eng_copy, dstT[:, :, tsl], pt_[:, 0:nk * 128].rearrange("p (k n) -> p k n", k=nk)[:, :, 0:npt],
                       [Rpt], [R_dst])

            for g in range(NG):
                gs = slice(g * 512, (g + 1) * 512)
                last = do_s and g == NG - 1
                tiles = [(t4, 128, slice(t4 * 128, (t4 + 1) * 128)) for t4 in range(4)]
                cgs = [(slice(0, 512), 512, 0)]
                if last:
                    tiles.append((4, 16, slice(512, 528)))
                    cgs.append((slice(512, 528), 16, 1))
                for (t4, npt, tsl) in tiles:
                    b = t4 % 2
                    if t4 < 4:
                        row0 = seq * S + (g * 4 + t4) * 128
                        P.dma("sp", xg[:, t4, :], xp[row0:row0 + 128, :], s_xg[t4], writes=[R_xg[t4]])
                    else:
                        P.dma("sp", xg[0:16, t4, :], xs[:, :], s_xg[t4], writes=[R_xg[t4]])
                    to_T(t4, npt, tsl, xg[0:npt, t4, :], D, xTd, R_xTd, xbD[b], ([R_xg[t4]], [R_xbD[b]]), "dve")
                for mh in range(2):
                    wP, RwP = WS.get(0)
                    wC, RwC = WS.get(1)
                    wA, RwA = WS.get(2)
                    for m4 in range(4):
                        m = mh * 4 + m4
                        cs = slice(m4 * 128, (m4 + 1) * 128)
                        for (csl, n, isS) in cgs:
                            sTs = s_sT if isS else sT
                            oTs = o_sT if isS else oT
                            Rs_ = R_ssT if isS else R_sT[g]
                            Ro_ = R_osT if isS else R_oT[g]
                            ssl = slice(0, 16) if isS else gs
                            pc, Rpc = next_ps()
                            for k in range(8):
                                P.mm(pc[:, 0:n], wC[:, k * 512:(k + 1) * 512][:, cs], xTd[:, k, csl], k == 0, k == 7,
                                     [RwC, R_xTd], [Rpc], sig=(k == 7))
                            s1 = sgi % 3; sgi += 1
                            P.act(sgD[s1][:, 0:n], pc[:, 0:n], AF.Sigmoid, [Rpc], [R_sgD[s1]])
                            pa, Rpa = next_ps()
                            for k in range(8):
                                P.mm(pa[:, 0:n], wA[:, k * 512:(k + 1) * 512][:, cs], xTd[:, k, csl], k == 0, k == 7,
                                     [RwA, R_xTd], [Rpa], sig=(k == 7))
                            s2 = sgi % 3; sgi += 1
                            P.act(sgD[s2][:, 0:n], pa[:, 0:n], AF.Sigmoid, [Rpa], [R_sgD[s2]])
                            po, Rpo = next_ps()
                            for k in range(4):
                                P.mm(po[:, 0:n], wP[:, k * 512:(k + 1) * 512][:, cs], sTs[:, k, ssl], k == 0, k == 3,
                                     [RwP, Rs_], [Rpo], sig=(k == 3))
                            P.tt("dve", sgD[s1][:, 0:n], sgD[s1][:, 0:n], po[:, 0:n], ALU.mult, [R_sgD[s1], Rpo], [R_sgD[s1]])
                            po2, Rpo2 = next_ps()
                            for k in range(4):
                                P.mm(po2[:, 0:n], wP[:, (4 + k) * 512:(5 + k) * 512][:, cs], oTs[:, k, ssl], k == 0, k == 3,
                                     [RwP, Ro_], [Rpo2], sig=(k == 3))
                            P.tt("dve", sgD[s2][:, 0:n], sgD[s2][:, 0:n], po2[:, 0:n], ALU.mult, [R_sgD[s2], Rpo2], [R_sgD[s2]])
                            P.tt("pool", mixT[:, m, csl], sgD[s1][:, 0:n], sgD[s2][:, 0:n], ALU.add, [R_sgD[s1], R_sgD[s2]], [R_actT])
                    WS.done(); WS.done(); WS.done()
                for hf in range(2):
                    wO, RwO = WS.get()
                    for (t4, npt, tsl) in tiles:
                        pw, Rpw = next_ps()
                        for k in range(8):
                            P.mm(pw[0:npt, :], mixT[:, k, tsl], wO[:, k * 512:(k + 1) * 512], k == 0, k == 7,
                                 [R_actT, RwO], [Rpw], sig=(k == 7))
                        P.stt("dve", xg[0:npt, t4, hf * 512:(hf + 1) * 512], xg[0:npt, t4, hf * 512:(hf + 1) * 512], ALPHA, pw[0:npt, :],
                              ALU.mult, ALU.add, [R_xg[t4], Rpw], [R_xg[t4]])
                    WS.done()
                layer_norm_all(tiles, 0, 1)
                for (t4, npt, tsl) in tiles:
                    b = t4 % 2
                    to_T(t4, npt, tsl, xg[0:npt, t4, :], D, xTd, R_xTd, xbD[b], ([R_xg[t4]], [R_xbD[b]]), "dve")
                    if t4 < 4:
                        row0 = seq * S + (g * 4 + t4) * 128
                        P.dma("sp", pst_[b][:], pp[row0:row0 + 128, :], s_pl[b], writes=[R_pst[b]])
                    else:
                        P.dma("sp", pst_[b][0:16, :], pps[:, :], s_pl[b], writes=[R_pst[b]])
                    to_T(t4, npt, tsl, pst_[b][0:npt, :], PLE, pT, R_pT, pbD[b], ([R_pst[b]], [R_pbD[b]]), "act")
                for jp in range(NJ // 2):
                    wU, RwU = WS.get()
                    for jj in range(2):
                        j = jp * 2 + jj
                        for (csl, n, isS) in cgs:
                            pg_, Rpg = next_ps()
                            pu, Rpu = next_ps()
                            for k in range(8):
                                o_ = k * 512 + jj * 128
                                P.mm(pg_[:, 0:n], wU[:, o_:o_ + 128], xTd[:, k, csl], k == 0, k == 7, [RwU, R_xTd], [Rpg], sig=(k == 7))
                            for k in range(8):
                                o_ = k * 512 + 256 + jj * 128
                                P.mm(pu[:, 0:n], wU[:, o_:o_ + 128], xTd[:, k, csl], k == 0, k == 7, [RwU, R_xTd], [Rpu], sig=(k == 7))
                            s1 = sgi % 3; sgi += 1
                            P.act(sgD[s1][:, 0:n], pg_[:, 0:n], AF.Silu, [Rpg], [R_sgD[s1]])
                            P.tt("dve", actT[:, j, csl], sgD[s1][:, 0:n], pu[:, 0:n], ALU.mult, [R_sgD[s1], Rpu], [R_actT])
                    WS.done()
                fbank = [0, 1, 2, 3, 4]
                for hf in range(2):
                    hs = slice(hf * 512, (hf + 1) * 512)
                    wG, RwG = WS.get(0)
                    wL, RwL = WS.get(1)
                    for (t4, npt, tsl) in tiles:
                        pgt, Rpgt = next_ps()
                        for k in range(8):
                            P.mm(pgt[0:npt, :], xTd[:, k, tsl], wG[:, k * 512:(k + 1) * 512], k == 0, k == 7,
                                 [R_xTd, RwG], [Rpgt], sig=(k == 7))
                        s1 = sgi % 3; sgi += 1
                        P.act(sgD[s1][0:npt, :], pgt[0:npt, :], AF.Sigmoid, [Rpgt], [R_sgD[s1]])
                        ppl, Rppl = next_ps()
                        for k in range(2):
                            P.mm(ppl[0:npt, :], pT[:, k, tsl], wL[:, k * 512:(k + 1) * 512], k == 0, k == 1,
                                 [R_pT, RwL], [Rppl], sig=(k == 1))
                        P.tt("dve", sgD[s1][0:npt, :], sgD[s1][0:npt, :], ppl[0:npt, :], ALU.mult, [R_sgD[s1], Rppl], [R_sgD[s1]])
                        P.stt("dve", xg[0:npt, t4, hs], xg[0:npt, t4, hs], ALPHA, sgD[s1][0:npt, :], ALU.mult, ALU.add,
                              [R_xg[t4], R_sgD[s1]], [R_xg[t4]])
                    WS.done()
                    kblk = 0
                    for si_ in range(3):
                        if si_ == 0:
                            wD, RwD, offs = wL, RwL, list(range(2, 8))
                        else:
                            wD, RwD = WS.get()
                            offs = list(range(8))
                        for (t4, npt, tsl) in tiles:
                            fb = fbank[t4]
                            for oi, o_ in enumerate(offs):
                                kk = kblk + oi
                                P.mm(ps[fb][0:npt, :], actT[:, kk, tsl], wD[:, o_ * 512:(o_ + 1) * 512], kk == 0, kk == NJ - 1,
                                     [R_actT, RwD], [R_ps[fb]], sig=(oi == len(offs) - 1))
                        kblk += len(offs)
                        WS.done()
                    for (t4, npt, tsl) in tiles:
                        fb = fbank[t4]
                        P.tt("dve", xg[0:npt, t4, hs], xg[0:npt, t4, hs], ps[fb][0:npt, :], ALU.add, [R_xg[t4], R_ps[fb]], [R_xg[t4]])
                layer_norm_all(tiles, 2, 3)
                for (t4, npt, tsl) in tiles:
                    if t4 < 4:
                        row0 = seq * S + (g * 4 + t4) * 128
                        P.dma("sp", yp[row0:row0 + 128, :], xg[:, t4, :], s_yo[t4], reads=[R_xg[t4]])
                    elif not DEBUG:
                        P.dma("sp", ys[:, :], xg[0:16, t4, :], s_yo[t4], reads=[R_xg[t4]])
            P.barrier()
            d_es.close()
            if do_s:
                samp_es.close()
            seq_es.close()

        P.barrier()
        block = es.enter_context(nc.Block())
        P.replay(block)
    return nc


def host_consts():
    c = np.zeros((128, 4 * 128), np.float32)
    c[:, 0:128] = np.eye(128, dtype=np.float32)
    j = np.arange(128)[:, None]
    s = np.arange(128)[None, :]
    c[:, 128:256] = np.where(j >= s, -1.0, 0.0)
    c[:, 256:384] = np.where(j < s, 1.0, 0.0)
    for jj in range(4):
        for hh in range(H):
            for q in range(4):
                c[jj, 384 + hh * 4 + q] = 1.0 if jj < q else 0.0
    return c


def make_in_maps(inputs, n_cores, S, NPG):
    f = lambda a: np.ascontiguousarray(np.asarray(a))
    x_prompt = f(inputs["x_prompt"]); p_prompt = f(inputs["p_prompt"])[0]
    x_sample = f(inputs["x_sample"]); p_sample = f(inputs["p_sample"])[0]
    ck = f(inputs["cache_k"])[0]; cv = f(inputs["cache_v"])[0]
    npool = ck.shape[0]
    ck = ck.reshape(npool * 128, 512); cv = cv.reshape(npool * 128, 512)
    sconv = f(inputs["state_conv"])[0]
    pt = f(inputs["page_table"]).astype(np.int32)
    cw = f(inputs["conv_w"])[0]
    cwT = np.ascontiguousarray(cw.reshape(CW, 4, 128).transpose(2, 1, 0).reshape(128, 4 * CW))
    vec = lambda a: f(a)[0].reshape(4, 128).T
    cvec = np.ascontiguousarray(np.concatenate([vec(inputs["conv_b"]), vec(inputs["conv_ln_g"]), vec(inputs["conv_ln_b"])], axis=1))
    lnp = np.ascontiguousarray(np.stack([f(inputs["ln1_g"])[0], f(inputs["ln1_b"])[0], f(inputs["ln2_g"])[0], f(inputs["ln2_b"])[0]]))
    sbb = f(inputs["sb_bias"]).reshape(1, H)
    brow = np.ascontiguousarray(np.broadcast_to(sbb.reshape(1, H, 1), (16, H, 4)).reshape(1, 512))
    consts = host_consts()
    shared = {
        "ck": ck, "cv": cv, "cwT": cwT, "cvec": cvec, "lnp": lnp, "sbb": sbb, "consts": consts, "brow": brow,
        "w_in": f(inputs["w_in"])[0], "w_conv_proj": f(inputs["w_conv_proj"])[0], "w_att_proj": f(inputs["w_att_proj"])[0],
        "w_out": f(inputs["w_out"])[0], "w_ffn_up": f(inputs["w_ffn_up"])[0], "w_ffn_down": f(inputs["w_ffn_down"])[0],
        "w_ple_gate": f(inputs["w_ple_gate"])[0], "w_ple": f(inputs["w_ple"])[0],
    }
    maps = []
    for c in range(n_cores):
        m = dict(shared)
        m["xp"] = x_prompt[2 * c:2 * c + 2].reshape(2 * S, D)
        m["pp"] = p_prompt[2 * c:2 * c + 2].reshape(2 * S, PLE)
        m["xs"] = x_sample[4 * c:4 * c + 4].reshape(16, D)
        m["pps"] = p_sample[4 * c:4 * c + 4].reshape(16, PLE)
        m["sconv"] = sconv[4 * c:4 * c + 4].reshape(120, CC)
        m["ptab"] = pt[4 * c:4 * c + 4].reshape(1, 4 * NPG)
        maps.append(m)
    return maps, npool


def assemble(results, n_cores, S):
    cat = lambda k: np.concatenate([np.asarray(r[k]) for r in results], axis=0)
    B = 2 * n_cores
    DB = 4 * n_cores
    y_p = cat("yp").reshape(B, S, D)
    y_s = cat("ys").reshape(DB, 4, D)
    k_p = cat("kp").reshape(1, B, S, H, DH)
    v_p = cat("vp").reshape(1, B, S, H, DH)
    c_p = cat("cpo").reshape(1, B, 30, CC)
    k_s = cat("ksn").reshape(1, DB, 4, H, DH)
    v_s = cat("vsn").reshape(1, DB, 4, H, DH)
    c_s = cat("csn").reshape(1, DB, 30, CC)
    return tuple(np.ascontiguousarray(a, dtype=np.float32) for a in (y_p, y_s, k_p, v_p, c_p, k_s, v_s, c_s))


def run(inputs, n_cores, S, NPG):
    maps, npool = make_in_maps(inputs, n_cores, S, NPG)
    nc = build(S, NPG, npool)
    res = run_bass_kernel_spmd(nc, maps, core_ids=list(range(n_cores)))
    return assemble(res.results, n_cores, S)


def kernel(**inputs):
    return run(inputs, 8, 2048, 128)
```

```python
import numpy as np
from contextlib import ExitStack
import concourse.bass as bass
import concourse.mybir as mybir
from concourse.bass_utils import run_bass_kernel_spmd

F32 = mybir.dt.float32
BF16 = mybir.dt.bfloat16
I32 = mybir.dt.int32
AF = mybir.ActivationFunctionType
ALU = mybir.AluOpType

D = 1024
H = 8
DH = 64
CC = 512
FF = 2816
NJ = FF // 128
PLE = 256
INC = 4608
CW = 31
ALPHA = float(2.0 ** 0.25)
EPS = 1e-5
NSLOT = 6
SLOTW = 4096
SELF_WAIT = ('act', 'dve', 'pool')
DEBUG = False
NTAP_POOL = 0

ENGS = ("pe", "act", "dve", "pool", "sp")


class Sem:
    def __init__(self, h):
        self.h = h
        self.n = 0


class Res:
    __slots__ = ("name", "w", "r")

    def __init__(self, name):
        self.name = name
        self.w = None
        self.r = []


class Prog:
    def __init__(self, nc, es):
        self.nc = nc
        self.es = es
        self.q = {e: [] for e in ENGS}
        self.psem = {e: Sem(es.enter_context(nc.semaphore("prog_" + e))) for e in ENGS}
        self.waited = {e: {} for e in ENGS}
        self.nsem = 0
        self.dma_tickets = []

    def newsem(self, name):
        self.nsem += 1
        return Sem(self.es.enter_context(self.nc.semaphore(name)))

    def _deps(self, eng, reads, writes, extra):
        deps = list(extra)
        for r in reads:
            if r.w is not None:
                deps.append(r.w)
        for w in writes:
            if w.w is not None:
                deps.append(w.w)
            deps.extend(w.r)
        best = {}
        for d in deps:
            if d is None:
                continue
            s, v = d
            if s is self.psem[eng] and eng not in SELF_WAIT:
                continue
            if best.get(id(s), (None, 0))[1] < v:
                best[id(s)] = (s, v)
        out = []
        for k, (s, v) in best.items():
            if self.waited[eng].get(k, 0) < v:
                self.waited[eng][k] = v
                out.append((s, v))
        return out

    def _commit(self, tk, reads, writes):
        for r in reads:
            r.r.append(tk)
            if len(r.r) > 24:
                best = {}
                for s, v in r.r:
                    if best.get(id(s), (None, 0))[1] < v:
                        best[id(s)] = (s, v)
                r.r = list(best.values())
        for w in writes:
            w.w = tk
            w.r = []

    def emit(self, eng, fn, reads=(), writes=(), sig=True, extra=()):
        waits = self._deps(eng, reads, writes, extra)
        ps = self.psem[eng]
        if sig:
            ps.n += 1
            tk = (ps, ps.n)
        else:
            tk = (ps, ps.n + 1)
        self.q[eng].append((waits, fn, sig))
        self._commit(tk, reads, writes)
        return tk

    def dma(self, eng, out, in_, sem, reads=(), writes=(), extra=(), indirect=None):
        waits = self._deps(eng, reads, writes, extra)
        sem.n += 16
        tk = (sem, sem.n)
        if indirect is None:
            fn = lambda e: e.dma_start(out=out, in_=in_)
        else:
            fn = lambda e: e.indirect_dma_start(out=out, out_offset=None, in_=in_,
                                                in_offset=bass.IndirectOffsetOnAxis(ap=indirect, axis=0))
        self.q[eng].append((waits, fn, sem))
        self._commit(tk, reads, writes)
        self.dma_tickets.append(tk)
        return tk

    def barrier(self):
        tks = [(self.psem[e], self.psem[e].n) for e in ENGS if self.psem[e].n > 0]
        best = {}
        for s, v in self.dma_tickets:
            if best.get(id(s), (None, 0))[1] < v:
                best[id(s)] = (s, v)
        tks += list(best.values())
        self.dma_tickets = []
        for e in ENGS:
            waits = self._deps(e, (), (), tks)
            if waits:
                self.q[e].append((waits, None, False))

    def replay(self, block):
        nc = self.nc

        def run(name, e):
            ps = self.psem[name]
            for waits, fn, sig in self.q[name]:
                for s, v in waits:
                    e.wait_ge(s.h, v)
                if fn is None:
                    continue
                ins = fn(e)
                if sig is True:
                    ins.then_inc(ps.h, 1)
                elif sig is not False:
                    ins.then_inc(sig.h, 16)

        @block.tensor
        def _(e):
            run("pe", e)

        @block.scalar
        def _(e):
            run("act", e)

        @block.vector
        def _(e):
            run("dve", e)

        @block.gpsimd
        def _(e):
            run("pool", e)

        @block.sync
        def _(e):
            run("sp", e)

    def mm(self, out, lhsT, rhs, start, stop, reads, writes, sig=False, extra=()):
        return self.emit("pe", lambda e: e.matmul(out, lhsT=lhsT, rhs=rhs, start=start, stop=stop,
                                                  skip_group_check=True),
                         reads, writes, sig, extra)

    def tr(self, out, in_, ident, reads, writes, sig=False):
        return self.emit("pe", lambda e: e.transpose(out=out, in_=in_, identity=ident), reads, writes, sig)

    def act(self, out, in_, func, reads, writes, bias=0.0, scale=1.0, extra=()):
        return self.emit("act", lambda e: e.activation(out=out, in_=in_, func=func, bias=bias, scale=scale),
                         reads, writes, True, extra)

    def copy(self, eng, out, in_, reads, writes):
        if eng == "act":
            return self.act(out, in_, AF.Copy, reads, writes)
        return self.emit(eng, lambda e: e.tensor_copy(out=out, in_=in_), reads, writes)

    def tt(self, eng, out, in0, in1, op, reads, writes):
        return self.emit(eng, lambda e: e.tensor_tensor(out=out, in0=in0, in1=in1, op=op), reads, writes)

    def ts(self, eng, out, in0, s1, s2, op0, op1, reads, writes):
        return self.emit(eng, lambda e: e.tensor_scalar(out=out, in0=in0, scalar1=s1, scalar2=s2, op0=op0, op1=op1),
                         reads, writes)

    def stt(self, eng, out, in0, scalar, in1, op0, op1, reads, writes):
        return self.emit(eng, lambda e: e.scalar_tensor_tensor(out=out, in0=in0, scalar=scalar, in1=in1,
                                                               op0=op0, op1=op1), reads, writes)

    def bn_stats(self, out, in_, reads, writes):
        return self.emit("dve", lambda e: e.bn_stats(out=out, in_=in_), reads, writes)

    def bn_aggr(self, out, in_, reads, writes):
        return self.emit("dve", lambda e: e.bn_aggr(out=out, in_=in_), reads, writes)

    def memset(self, eng, ap, val, writes):
        return self.emit(eng, lambda e: e.memset(ap, val), (), writes)


def slot_plan():
    slots = []
    for mh in range(2):
        c0 = mh * 512
        slots.append([("w_conv_proj", 0, 4, c0, 512, 0, 512), ("w_att_proj", 0, 4, c0, 512, 2048, 512)])
        slots.append([("w_in", 0, 8, 2560 + c0, 512, 0, 512)])
        slots.append([("w_in", 0, 8, 3584 + c0, 512, 0, 512)])
    for hf in range(2):
        slots.append([("w_out", 0, 8, hf * 512, 512, 0, 512)])
    for jp in range(NJ // 2):
        slots.append([("w_ffn_up", 0, 8, 2 * jp * 128, 256, 0, 512), ("w_ffn_up", 0, 8, FF + 2 * jp * 128, 256, 256, 512)])
    for hf in range(2):
        c0 = hf * 512
        slots.append([("w_ple_gate", 0, 8, c0, 512, 0, 512)])
        slots.append([("w_ple", 0, 2, c0, 512, 0, 512), ("w_ffn_down", 0, 6, c0, 512, 1024, 512)])
        slots.append([("w_ffn_down", 6, 8, c0, 512, 0, 512)])
        slots.append([("w_ffn_down", 14, 8, c0, 512, 0, 512)])
    return slots


def build(S, NPG, NPOOL, with_sample=True):
    NT = S // 128
    NG = S // 512
    nc = bass.Bass("TRN2", target_bir_lowering=False)
    dt = nc.dram_tensor

    def din(name, shape, dtype=F32):
        return dt(name, shape, dtype, kind="ExternalInput").ap()

    def dout(name, shape):
        return dt(name, shape, F32, kind="ExternalOutput").ap()

    xp = din("xp", [2 * S, D])
    pp = din("pp", [2 * S, PLE])
    xs = din("xs", [16, D])
    pps = din("pps", [16, PLE])
    ck = din("ck", [NPOOL * 128, 512])
    cv = din("cv", [NPOOL * 128, 512])
    sconv = din("sconv", [120, CC])
    ptab = din("ptab", [1, 4 * NPG], I32)
    W = {
        "w_in": din("w_in", [D, INC]),
        "w_conv_proj": din("w_conv_proj", [CC, D]),
        "w_att_proj": din("w_att_proj", [CC, D]),
        "w_out": din("w_out", [D, D]),
        "w_ffn_up": din("w_ffn_up", [D, 2 * FF]),
        "w_ffn_down": din("w_ffn_down", [FF, D]),
        "w_ple_gate": din("w_ple_gate", [D, D]),
        "w_ple": din("w_ple", [PLE, D]),
    }
    cwT = din("cwT", [128, 4 * CW])
    cvec = din("cvec", [128, 12])
    lnp = din("lnp", [4, D])
    sbb = din("sbb", [1, H])
    consts = din("consts", [128, 4 * 128])
    brow = din("brow", [1, 512])

    yp = dout("yp", [2 * S, D])
    ys = dout("ys", [16, D])
    kp = dout("kp", [2 * S, 512])
    vp = dout("vp", [2 * S, 512])
    cpo = dout("cpo", [60, CC])
    ksn = dout("ksn", [16, 512])
    vsn = dout("vsn", [16, 512])
    csn = dout("csn", [120, CC])

    slots = slot_plan()
    NSL = len(slots)
    scratch = dt("wscratch", [NSL, 128, SLOTW], BF16, kind="Internal").ap()
    scratchA = dt("wscratchA", [5, 128, SLOTW], BF16, kind="Internal").ap()

    with ExitStack() as es:
        P = Prog(nc, es)
        sb = lambda name, shape, dtype, st=es: st.enter_context(nc.sbuf_tensor(name, shape, dtype))
        pst = lambda name, shape, dtype, st=es: st.enter_context(nc.psum_tensor(name, shape, dtype))

        lnb = sb("lnb", [128, 4, D], F32)
        cw_sb = sb("cw_sb", [128, 4 * CW], F32)
        cvec_sb = sb("cvec_sb", [128, 12], F32)
        biasT = sb("biasT", [128, H], F32)
        cst_f = sb("cst_f", [128, 4 * 128], F32)
        identb = sb("identb", [128, 128], BF16)
        negU = sb("negU", [128, 128], BF16)
        negOnes = sb("negOnes", [128, 128], BF16)
        tri = sb("tri", [128, 128], BF16)
        onesM = sb("onesM", [128, 128], BF16)
        zer = sb("zer", [128, 512], BF16)
        identf = cst_f[:, 0:128]
        cf2 = sb("cf2", [128, 256], F32)
        brow_sb = sb("brow_sb", [1, 512], F32)

        psT = [pst("psT%d" % i, [128, 1024], BF16) for i in range(2)]
        ps = [pst("ps%d" % i, [128, 512], F32) for i in range(6)]
        R_psT = [Res("psT%d" % i) for i in range(2)]
        R_ps = [Res("ps%d" % i) for i in range(6)]
        rot = {"i": 0, "t": 0}

        def next_ps():
            rot["i"] = (rot["i"] + 1) % 6
            return ps[rot["i"]], R_ps[rot["i"]]

        def next_psT():
            rot["t"] = (rot["t"] + 1) % 2
            return psT[rot["t"]], R_psT[rot["t"]]

        R_const = Res("const")
        R_winA = Res("winA")
        s_c = P.newsem("s_const")
        for (o, i_) in ((lnb[:].rearrange("p a d -> p (a d)"), lnp.rearrange("a d -> (a d)").partition_broadcast(128)),
                        (cw_sb[:], cwT), (cvec_sb[:], cvec), (biasT[:], sbb.partition_broadcast(128)),
                        (cst_f[:], consts)):
            P.dma("sp", o, i_, s_c, writes=[R_const])
        P.copy("pool", identb[:], cst_f[:, 0:128], [R_const], [R_const])
        P.copy("pool", negU[:], cst_f[:, 128:256], [R_const], [R_const])
        P.copy("pool", tri[:], cst_f[:, 256:384], [R_const], [R_const])
        P.memset("pool", negOnes[:], -1.0, [R_const])
        P.memset("pool", onesM[:], 1.0 / 512.0, [R_const])
        P.memset("pool", zer[:], 0.0, [R_const])
        P.memset("pool", cf2[:, 0:128], -1.0, [R_const])
        P.memset("pool", cf2[:, 128:256], 1.0, [R_const])
        P.dma("sp", brow_sb[:], brow, P.newsem("s_brow"), writes=[R_const])
        R_winA = Res("winA")
        R_scrA = [Res("scrA%d" % i) for i in range(5)]
        s_wa = P.newsem("s_winA")

        class WStream:
            def __init__(self):
                self.n = 0
                self.sem = [P.newsem("s_slot%d" % i) for i in range(NSLOT)]
                self.res = [Res("slot%d" % i) for i in range(NSLOT)]
                self.scr = [Res("scr%d" % i) for i in range(NSL)]
                self.buf = None
                self.issued = 0
                self.total = 0

            def prefetch(self):
                while self.issued < self.limit and self.issued < self.n + NSLOT:
                    u = self.issued
                    si = u % NSLOT
                    sl = u % NSL
                    dst = self.buf[:, si, :]
                    P.dma("sp", dst, scratch[sl], self.sem[si], reads=[self.scr[sl]], writes=[self.res[si]])
                    self.issued += 1

            def get(self, ahead=0):
                u = self.n + ahead
                assert u < self.issued, "weight unit consumed before its load was emitted"
                si = u % NSLOT
                return self.buf[:, si, :], self.res[si]

            def done(self):
                self.n += 1
                self.prefetch()

        WS = WStream()

        pro_es = ExitStack()
        stf = [sb("stf%d" % i, [128, SLOTW], F32, pro_es) for i in range(3)]
        stb = [sb("stb%d" % i, [128, SLOTW], BF16, pro_es) for i in range(3)]
        R_stf = [Res("stf") for _ in range(3)]
        R_stb = [Res("stb") for _ in range(3)]
        s_pld = [P.newsem("s_pld%d" % i) for i in range(3)]
        s_pst = [P.newsem("s_pst%d" % i) for i in range(3)]
        allslots = [(scratchA[cb], [("w_in", 0, 8, cb * 512, 512, 0, 512)], R_scrA[cb]) for cb in range(5)]
        for n, (dst_dram, plan, Rscr) in enumerate(allslots):
            i = n % 3
            for (wn, k0, nk, c0, ncols, off, ks) in plan:
                srcv = W[wn][k0 * 128:(k0 + nk) * 128, c0:c0 + ncols].rearrange("(k p) c -> p k c", p=128)
                dstv = stf[i][:, :].rearrange("p (k c) -> p k c", c=ks)[:, off // ks:off // ks + nk, off % ks:off % ks + ncols]
                P.dma("sp", dstv, srcv, s_pld[i], writes=[R_stf[i]])
            P.copy(("act", "dve", "pool")[n % 3], stb[i][:], stf[i][:], [R_stf[i]], [R_stb[i]])
            P.dma("sp", dst_dram, stb[i][:], s_pst[i], reads=[R_stb[i]], writes=[Rscr])
        P.barrier()
        pro_es.close()
        WS.limit = 0


        samp_es = ExitStack()

        def sample_alloc():
            sbP = lambda name, shape, dtype: sb("SP_" + name, shape, dtype, samp_es)
            d_ = dict(
                s_sT=sbP("s_sT", [128, 4, 16], BF16), o_sT=sbP("o_sT", [128, 4, 16], BF16),
                Qblk=sbP("Qblk", [128, 4, 4, 32], BF16),
                knT=[sbP("knT%d" % b, [128, 4, 4], BF16) for b in range(4)],
                vnb=[sbP("vnb%d" % b, [4, 512], BF16) for b in range(4)],
                idx=sbP("idx", [128, 4 * NPG], I32))
            return d_

        def sample_S1(winA, SBP):
            s1_es = ExitStack()
            sbS = lambda name, shape, dtype: sb("S_" + name, shape, dtype, s1_es)
            xsT = sbS("xsT", [128, 8, 16], BF16)
            s_sT, o_sT, Qblk, knT, vnb, idx = [SBP[k] for k in ("s_sT", "o_sT", "Qblk", "knT", "vnb", "idx")]
            R_xsT, R_ssT, R_osT = Res("xsT"), Res("s_sT"), Res("o_sT")
            PB = min(16, NPG)
            NBT = NPG // PB
            NW = PB * 32
            xs_f = sbS("xs_f", [16, D], F32)
            xs_b = sbS("xs_b", [16, D], BF16)
            qs_b = sbS("qs_b", [16, 512], BF16)
            us = sbS("us", [16, 512], F32)
            sgs = sbS("sgs", [16, 512], F32)
            sc_sb = sbS("sc_sb", [120, CC], F32)
            fullT = sbS("fullT", [128, 4, 4, 34], F32)
            kn = [sbS("kn%d" % b, [4, 512], F32) for b in range(4)]
            vn = [sbS("vn%d" % b, [4, 512], F32) for b in range(4)]
            knb = [sbS("knb%d" % b, [4, 512], BF16) for b in range(4)]
            qsT = sbS("qsT", [128, 4, 16], BF16)
            accs = sbS("accs", [128, 4, 16], F32)
            cbs = sbS("cbs", [128, 4, 16], BF16)
            sqs = sbS("sqs", [128, 4, 16], BF16)
            rstd_s = sbS("rstd_s", [128, 16], F32)
            pt_i = sbS("pt_i", [128, 4 * NPG], I32)
            pt_f = sbS("pt_f", [128, 4 * NPG], F32)
            io_f = sbS("io_f", [128, 1], F32)
            R = {n: Res(n) for n in ("xsf", "xsb", "qsb", "us", "sgs", "sc", "fullT", "qsT", "Qblk", "accs", "cbs", "sqs",
                                     "rstd", "idx", "En", "Lnw", "an", "Es", "Lbs", "Lsuf", "o32s", "otok", "otb")}
            R_kn = [Res("kn") for _ in range(4)]; R_vn = [Res("vn") for _ in range(4)]
            R_knb = [Res("knb") for _ in range(4)]; R_vnb = [Res("vnb") for _ in range(4)]
            R_knT = [Res("knT") for _ in range(4)]
            R_kpg = [Res("kpg") for _ in range(2)]; R_vpg = [Res("vpg") for _ in range(2)]
            R_kTp = [Res("kTp") for _ in range(4)]
            R_aTs = [Res("aTs") for _ in range(2)]
            s_in = P.newsem("s_sin")
            s_so = P.newsem("s_sout")
            s_kp = [P.newsem("s_kp%d" % i) for i in range(2)]
            s_vp = [P.newsem("s_vp%d" % i) for i in range(2)]
            s_o32 = P.newsem("s_o32")
            mnew = cst_f[0:4, 384:416]

            P.dma("sp", xs_f[:], xs[:, :], s_in, writes=[R["xsf"]])
            P.dma("sp", sc_sb[:], sconv[:, :], P.newsem("s_sin2"), writes=[R["sc"]])
            P.dma("sp", pt_i[:], ptab.partition_broadcast(128), P.newsem("s_sin3"), writes=[R["idx"]])
            P.emit("pool", lambda e: e.iota(io_f[:], pattern=[[0, 1]], base=0, channel_multiplier=1,
                                            allow_small_or_imprecise_dtypes=True), (), [R["idx"]])
            P.copy("pool", pt_f[:], pt_i[:], [R["idx"]], [R["idx"]])
            P.ts("pool", idx[:], pt_f[:], 128.0, io_f[:, 0:1], ALU.mult, ALU.add, [R["idx"]], [R["idx"]])
            P.copy("pool", xs_b[:], xs_f[:], [R["xsf"]], [R["xsb"]])
            pt_, Rpt = next_psT()
            for k in range(8):
                P.tr(pt_[:, k * 128:k * 128 + 16], xs_b[:, k * 128:(k + 1) * 128], identb[0:16, 0:16],
                     [R["xsb"], R_const], [Rpt], sig=(k == 7))
            P.copy("dve", xsT[:], pt_[:].rearrange("p (k n) -> p k n", k=8)[:, :, 0:16], [Rpt], [R_xsT])

            def proj16(cb):
                pk, Rpk = next_ps()
                for k in range(8):
                    P.mm(pk[0:16, :], xsT[:, k, :], winA[:, k, cb * 512:(cb + 1) * 512], k == 0, k == 7,
                         [R_xsT, R_winA], [Rpk], sig=(k == 7))
                return pk, Rpk

            pa, Rpa = proj16(0)
            pb, Rpb = proj16(1)
            P.act(sgs[:], pb[0:16, :], AF.Sigmoid, [Rpb], [R["sgs"]])
            P.tt("dve", us[:], pa[0:16, :], sgs[:], ALU.mult, [Rpa, R["sgs"]], [R["us"]])
            pq, Rpq = proj16(2)
            P.act(qs_b[:], pq[0:16, :], AF.Copy, [Rpq], [R["qsb"]], scale=0.125)
            for b in range(4):
                P.dma("sp", csn[b * 30:b * 30 + 26, :], sc_sb[b * 30 + 4:b * 30 + 30, :], s_so, reads=[R["sc"]])
                P.dma("sp", csn[b * 30 + 26:b * 30 + 30, :], us[b * 4:b * 4 + 4, :], s_so, reads=[R["us"]])
            for b in range(4):
                for which in range(2):
                    dst, Rd, dstb, Rdb, outd = ((kn, R_kn, knb, R_knb, ksn), (vn, R_vn, vnb, R_vnb, vsn))[which]
                    pk, Rpk = next_ps()
                    c0 = 1536 + which * 512
                    for k in range(8):
                        P.mm(pk[0:4, :], xsT[:, k, b * 4:(b + 1) * 4], winA[:, k, c0:c0 + 512], k == 0, k == 7,
                             [R_xsT, R_winA], [Rpk], sig=(k == 7))
                    P.copy("dve", dst[b][:], pk[0:4, :], [Rpk], [Rd[b]])
                    P.dma("sp", outd[b * 4:(b + 1) * 4, :], dst[b][:], s_so, reads=[Rd[b]])
                    P.copy("pool", dstb[b][:], dst[b][:], [Rd[b]], [Rdb[b]])
                pt_, Rpt = next_psT()
                for c in range(4):
                    P.tr(pt_[:, c * 128:c * 128 + 4], knb[b][:, c * 128:(c + 1) * 128], identb[0:4, 0:4],
                         [R_knb[b], R_const], [Rpt], sig=(c == 3))
                P.copy("dve", knT[b][:], pt_[:, 0:512].rearrange("p (k n) -> p k n", k=4)[:, :, 0:4], [Rpt], [R_knT[b]])
            pt_, Rpt = next_psT()
            for c in range(4):
                P.tr(pt_[:, c * 128:c * 128 + 16], qs_b[:, c * 128:(c + 1) * 128], identb[0:16, 0:16],
                     [R["qsb"], R_const], [Rpt], sig=(c == 3))
            P.copy("dve", qsT[:], pt_[:, 0:512].rearrange("p (k n) -> p k n", k=4)[:, :, 0:16], [Rpt], [R["qsT"]])
            P.memset("pool", Qblk[:], 0.0, [R["Qblk"]])
            for c in range(4):
                for hh in range(2):
                    h = 2 * c + hh
                    P.copy("pool", Qblk[hh * 64:(hh + 1) * 64, c, :, h * 4:(h + 1) * 4],
                           qsT[hh * 64:(hh + 1) * 64, c, :].rearrange("p (b q) -> p b q", b=4),
                           [R["qsT"], R["Qblk"]], [R["Qblk"]])
            for c in range(4):
                pf, Rpf = next_ps()
                P.tr(pf[:, 0:120], sc_sb[0:120, c * 128:(c + 1) * 128], identf[0:120, 0:120], [R["sc"], R_const], [Rpf])
                P.tr(pf[:, 128:144], us[0:16, c * 128:(c + 1) * 128], identf[0:16, 0:16], [R["us"], R_const], [Rpf], sig=True)
                P.copy("dve", fullT[:, c, :, 0:30], pf[:, 0:120].rearrange("p (b r) -> p b r", b=4), [Rpf], [R["fullT"]])
                P.copy("dve", fullT[:, c, :, 30:34], pf[:, 128:144].rearrange("p (b r) -> p b r", b=4), [Rpf], [R["fullT"]])
            for w in range(CW):
                for c in range(4):
                    src = fullT[:, c, :, w:w + 4]
                    dst = accs[:, c, :].rearrange("p (b t) -> p b t", b=4)
                    wv = cw_sb[:, c * CW + w:c * CW + w + 1]
                    if w == 0:
                        P.ts("dve", dst, src, wv, cvec_sb[:, c:c + 1], ALU.mult, ALU.add, [R["fullT"], R_const], [R["accs"]])
                    else:
                        P.stt("dve", dst, src, wv, dst, ALU.mult, ALU.add, [R["fullT"], R_const, R["accs"]], [R["accs"]])
            P.copy("pool", cbs[:], accs[:], [R["accs"]], [R["cbs"]])
            pm, Rpm = next_ps()
            for c in range(4):
                P.mm(pm[:, 0:16], onesM[:], cbs[:, c, :], c == 0, c == 3, [R_const, R["cbs"]], [Rpm], sig=(c == 3))
            for c in range(4):
                P.tt("dve", accs[:, c, :], accs[:, c, :], pm[:, 0:16], ALU.subtract, [R["accs"], Rpm], [R["accs"]])
            P.act(sqs[:], accs[:], AF.Square, [R["accs"]], [R["sqs"]])
            pv, Rpv = next_ps()
            for c in range(4):
                P.mm(pv[:, 0:16], onesM[:], sqs[:, c, :], c == 0, c == 3, [R_const, R["sqs"]], [Rpv], sig=(c == 3))
            P.act(rstd_s[:], pv[:, 0:16], AF.Ln, [Rpv], [R["rstd"]], bias=EPS)
            P.act(rstd_s[:], rstd_s[:], AF.Exp, [R["rstd"]], [R["rstd"]], scale=-0.5)
            for c in range(4):
                P.tt("dve", accs[:, c, :], accs[:, c, :], rstd_s[:], ALU.mult, [R["accs"], R["rstd"]], [R["accs"]])
                P.act(s_sT[:, c, :], accs[:, c, :], AF.Silu, [R["accs"], R_const], [R_ssT],
                      bias=cvec_sb[:, 8 + c:9 + c], scale=cvec_sb[:, 4 + c:5 + c])

            P.barrier()
            s1_es.close()
            return dict(locals())

        def sample_S2(SB):
            g_ = SB
            (PB, NBT, NW, R, Qblk, knT, vnb, idx, mnew, s_sT, o_sT, R_ssT, R_osT,
             R_knT, R_vnb, s_kp, s_vp, s_o32, s_so) = [g_[k] for k in (
                "PB", "NBT", "NW", "R", "Qblk", "knT", "vnb", "idx", "mnew",
                "s_sT", "o_sT", "R_ssT", "R_osT", "R_knT", "R_vnb", "s_kp", "s_vp", "s_o32", "s_so")]
            R_kpg, R_vpg, R_kTp, R_aTs = g_["R_kpg"], g_["R_vpg"], g_["R_kTp"], g_["R_aTs"]
            s2_es = ExitStack()
            sb2 = lambda name, shape, dtype: sb("S2_" + name, shape, dtype, s2_es)
            kpg = [sb2("kpg%d" % i, [128, PB, 512], BF16) for i in range(2)]
            vpg = [sb2("vpg%d" % i, [128, PB, 512], BF16) for i in range(2)]
            kTp = [sb2("kTp%d" % i, [128, 512], BF16) for i in range(4)]
            Es = sb2("Es", [128, NW], F32)
            Lbs = sb2("Lbs", [128, PB, 32], F32)
            Lsuf = sb2("Lsuf", [128, PB + 1, 32], F32)
            aTs = [sb2("aTs%d" % i, [128, NW], BF16) for i in range(2)]
            En = sb2("En", [4, 32], F32)
            Lnw = sb2("Lnw", [4, 32], F32)
            an = sb2("an", [4, 32], BF16)
            o32s = sb2("o32s", [32, 512], F32)
            o_tok = sb2("o_tok", [16, 512], F32)
            o_tb = sb2("o_tb", [16, 512], BF16)
            negUf = cst_f[:, 128:256]
            cnt = 0
            rk = 0
            o32, Ro32 = ps[2], R_ps[2]
            zn, Rzn = ps[3], R_ps[3]
            batches = [(b_, nb_) for b_ in range(4) for nb_ in reversed(range(NBT))]

            def gather(i):
                if i >= len(batches):
                    return
                b_, nb_ = batches[i]
                buf_ = i % 2
                for pi in range(PB):
                    j = b_ * NPG + nb_ * PB + pi
                    P.dma("pool", kpg[buf_][:, pi, :], ck, s_kp[buf_], reads=[R["idx"]], writes=[R_kpg[buf_]],
                          indirect=idx[:, j:j + 1])
                    P.dma("pool", vpg[buf_][:, pi, :], cv, s_vp[buf_], reads=[R["idx"]], writes=[R_vpg[buf_]],
                          indirect=idx[:, j:j + 1])

            for b in range(4):
                for c in range(4):
                    P.mm(zn[0:4, 0:32], knT[b][:, c, :], Qblk[:, c, b, :], c == 0, False, [R_knT[b], R["Qblk"]], [Rzn])
                P.mm(zn[0:4, 0:32], cf2[0:1, 128:132], brow_sb[0:1, 0:32], False, True, [R_const], [Rzn], sig=True)
                P.act(En[:], zn[0:4, 0:32], AF.Exp, [Rzn], [R["En"]])
                P.act(Lnw[:], En[:], AF.Ln, [R["En"]], [R["Lnw"]], bias=1.0)
                P.tt("dve", Lnw[:], Lnw[:], mnew, ALU.mult, [R["Lnw"], R_const], [R["Lnw"]])
                P.mm(zn[0:4, 0:32], cst_f[0:4, 128:132], Lnw[:], False, True, [R_const, R["Lnw"]], [Rzn], sig=True)
                P.act(an[:], zn[0:4, 0:32], AF.Exp, [Rzn], [R["an"]])
                P.tt("dve", an[:], an[:], mnew, ALU.mult, [R["an"], R_const], [R["an"]])
                P.memset("pool", Lsuf[:, PB, :], 0.0, [R["Lsuf"]])
                P.copy("pool", Lsuf[0:4, PB, :], Lnw[:], [R["Lnw"], R["Lsuf"]], [R["Lsuf"]])
                P.mm(o32[0:32, :], an[:], vnb[b][:], True, False, [R["an"], R_vnb[b]], [Ro32], sig=True)
                for nb in reversed(range(NBT)):
                    buf = cnt % 2
                    zb, Rz = ps[cnt % 2], R_ps[cnt % 2]
                    if cnt == 0:
                        gather(0)
                    gather(cnt + 1)
                    cnt += 1
                    P.mm(zb[:, 0:NW], cf2[0:1, 128:256], brow_sb[0:1, 0:NW], True, False, [R_const], [Rz], sig=True)
                    for pi in range(PB):
                        pt_, Rpt = next_psT()
                        for c in range(4):
                            P.tr(pt_[:, c * 128:(c + 1) * 128], kpg[buf][:, pi, c * 128:(c + 1) * 128], identb[:],
                                 [R_kpg[buf], R_const], [Rpt], sig=(c == 3))
                        r = rk % 4
                        rk += 1
                        P.copy("act" if r % 2 else "dve", kTp[r][:], pt_[:, 0:512], [Rpt], [R_kTp[r]])
                        for c in range(4):
                            P.mm(zb[:, pi * 32:(pi + 1) * 32], kTp[r][:, c * 128:(c + 1) * 128], Qblk[:, c, b, :], False, False,
                                 [R_kTp[r], R["Qblk"]], [Rz], sig=(c == 3))
                    P.act(Es[:], zb[:, 0:NW], AF.Exp, [Rz], [R["Es"]])
                    P.act(Lbs[:].rearrange("p a b -> p (a b)"), Es[:], AF.Ln, [R["Es"]], [R["Lbs"]], bias=1.0)
                    for pi in reversed(range(PB)):
                        P.tt("dve", Lsuf[:, pi, :], Lsuf[:, pi + 1, :], Lbs[:, pi, :], ALU.add, [R["Lsuf"], R["Lbs"]], [R["Lsuf"]])
                    P.mm(zb[:, 0:NW], negUf, Lbs[:].rearrange("p a b -> p (a b)"), False, False, [R_const, R["Lbs"]], [Rz])
                    P.mm(zb[:, 0:NW], cf2[:, 0:128], Lsuf[:, 1:PB + 1, :].rearrange("p a b -> p (a b)"), False, True,
                         [R_const, R["Lsuf"]], [Rz], sig=True)
                    P.act(aTs[buf][:], zb[:, 0:NW], AF.Exp, [Rz], [R_aTs[buf]])
                    P.copy("dve", Lsuf[:, PB, :], Lsuf[:, 0, :], [R["Lsuf"]], [R["Lsuf"]])
                    for pi in range(PB):
                        P.mm(o32[0:32, :], aTs[buf][:, pi * 32:(pi + 1) * 32], vpg[buf][:, pi, :], False,
                             (nb == 0 and pi == PB - 1), [R_aTs[buf], R_vpg[buf]], [Ro32], sig=(pi == PB - 1))
                P.copy("dve", o32s[:], o32[0:32, :], [Ro32], [R["o32s"]])
                for h in range(H):
                    P.dma("sp", o_tok[b * 4:(b + 1) * 4, h * 64:(h + 1) * 64], o32s[h * 4:(h + 1) * 4, h * 64:(h + 1) * 64],
                          s_o32, reads=[R["o32s"]], writes=[R["otok"]])
            P.copy("pool", o_tb[:], o_tok[:], [R["otok"]], [R["otb"]])
            if DEBUG:
                P.dma("sp", ys[:, 0:512], o_tok[:], s_so, reads=[R["otok"]])
            pt_, Rpt = next_psT()
            for c in range(4):
                P.tr(pt_[:, c * 128:c * 128 + 16], o_tb[:, c * 128:(c + 1) * 128], identb[0:16, 0:16],
                     [R["otb"], R_const], [Rpt], sig=(c == 3))
            P.copy("dve", o_sT[:], pt_[:, 0:512].rearrange("p (k n) -> p k n", k=4)[:, :, 0:16], [Rpt], [R_osT])
            P.barrier()
            s2_es.close()
            return s_sT, o_sT, R_ssT, R_osT

        R_out = Res("outs")
        out_sems = []

        for seq in range(2):
            seq_es = ExitStack()
            sbs = lambda name, shape, dtype: sb("%s_q%d" % (name, seq), shape, dtype, seq_es)
            sT = sbs("sT", [128, 4, S], BF16)
            oT = sbs("oT", [128, 4, S], BF16)
            do_s = with_sample and seq == 1
            if do_s:
                SBP = sample_alloc()
            R_sT = [Res("sT%d" % g) for g in range(NG)]
            R_oT = [Res("oT%d" % g) for g in range(NG)]

            abc_es = ExitStack()
            sba = lambda name, shape, dtype: sb("%s_q%d" % (name, seq), shape, dtype, abc_es)
            qT = sba("qT", [128, 4, S], BF16)
            kT = sba("kT", [128, 4, S], BF16)
            vv = sba("vv", [128, NT, 512], BF16)
            uT = sba("uT", [128, 4, 32 + S], BF16)
            R_qT = [Res("qT%d" % g) for g in range(NG)]
            R_kT = [Res("kT%d" % t) for t in range(NT)]
            R_vv = [Res("vv%d" % t) for t in range(NT)]
            R_uT = [Res("uT%d" % g) for g in range(NG)]
            R_uh = Res("uThist")
            P.memset("pool", uT[:, :, 0:32], 0.0, [R_uh])

            wa_es = ExitStack()
            winA = sb("winA_q%d" % seq, [128, 8, 2560], BF16, wa_es)
            a_es = ExitStack()
            sbA = lambda name, shape, dtype: sb("%s_q%d" % (name, seq), shape, dtype, a_es)
            R_winA.w = None
            R_winA.r = []
            for cb in range(5):
                P.dma("sp", winA[:, :, cb * 512:(cb + 1) * 512], scratchA[cb].rearrange("p (k c) -> p k c", c=512), s_wa,
                      reads=[R_scrA[cb]], writes=[R_winA])
            xst = [sbA("xst%d" % i, [128, D], F32) for i in range(2)]
            xb = [sbA("xb%d" % i, [128, D], BF16) for i in range(2)]
            xT = [sbA("xT%d" % i, [128, 8, 512], BF16) for i in range(1)] * 2
            kst = [sbA("kst%d" % i, [128, 512], F32) for i in range(1)] * 2
            vst = [sbA("vst%d" % i, [128, 512], F32) for i in range(1)] * 2
            kb = [sbA("kb%d" % i, [128, 512], BF16) for i in range(2)]
            sg = [sbA("sg%d" % i, [128, 512], F32) for i in range(2)]
            ust = sbA("ust", [128, 512], F32)
            R_xst = [Res("xst") for _ in range(2)]
            R_xb = [Res("xb") for _ in range(2)]
            R_xT = [Res("xT")] * 2
            R_kst = [Res("kst")] * 2
            R_vst = [Res("vst")] * 2
            R_kb = [Res("kb") for _ in range(2)]
            R_sg = [Res("sg") for _ in range(2)]
            R_ust = Res("ust")
            s_x = [P.newsem("s_x%d_%d" % (seq, i)) for i in range(2)]
            s_ko = [P.newsem("s_ko%d_%d" % (seq, i)) for i in range(2)]
            s_vo = [P.newsem("s_vo%d_%d" % (seq, i)) for i in range(2)]
            s_co = P.newsem("s_co%d" % seq)
            out_sems += s_ko + s_vo + [s_co]

            for g in range(NG):
                xTg, RxTg = xT[g % 2], R_xT[g % 2]
                for t4 in range(4):
                    t = g * 4 + t4
                    b = t % 2
                    row0 = seq * S + t * 128
                    if t == 0:
                        P.dma("sp", xst[0][:], xp[row0:row0 + 128, :], s_x[0], writes=[R_xst[0]])
                    if t + 1 < NT:
                        P.dma("sp", xst[(t + 1) % 2][:], xp[row0 + 128:row0 + 256, :], s_x[(t + 1) % 2], writes=[R_xst[(t + 1) % 2]])
                    P.copy("pool", xb[b][:], xst[b][:], [R_xst[b]], [R_xb[b]])
                    pt_, Rpt = next_psT()
                    for k in range(8):
                        P.tr(pt_[:, k * 128:(k + 1) * 128], xb[b][:, k * 128:(k + 1) * 128], identb[:],
                             [R_xb[b], R_const], [Rpt], sig=(k == 7))
                    P.copy("dve", xTg[:, :, t4 * 128:(t4 + 1) * 128], pt_[:].rearrange("p (k n) -> p k n", k=8),
                           [Rpt], [RxTg])
                    for which in range(2):
                        pk, Rpk = next_ps()
                        c0 = 1536 + which * 512
                        for k in range(8):
                            P.mm(pk[:], xTg[:, k, t4 * 128:(t4 + 1) * 128], winA[:, k, c0:c0 + 512], k == 0, k == 7,
                                 [RxTg, R_winA], [Rpk], sig=(k == 7))
                        if which == 0:
                            P.copy("dve", kst[b][:], pk[:], [Rpk], [R_kst[b]])
                            P.dma("sp", kp[row0:row0 + 128, :], kst[b][:], s_ko[b], reads=[R_kst[b]])
                            P.copy("pool", kb[b][:], kst[b][:], [R_kst[b]], [R_kb[b]])
                            pt2, Rpt2 = next_psT()
                            for c in range(4):
                                P.tr(pt2[:, c * 128:(c + 1) * 128], kb[b][:, c * 128:(c + 1) * 128], identb[:],
                                     [R_kb[b], R_const], [Rpt2], sig=(c == 3))
                            P.copy("act", kT[:, :, t * 128:(t + 1) * 128],
                                   pt2[:, 0:512].rearrange("p (k n) -> p k n", k=4), [Rpt2], [R_kT[t]])
                        else:
                            P.copy("act", vst[b][:], pk[:], [Rpk], [R_vst[b]])
                            P.dma("sp", vp[row0:row0 + 128, :], vst[b][:], s_vo[b], reads=[R_vst[b]])
                            P.copy("pool", vv[:, t, :], vst[b][:], [R_vst[b]], [R_vv[t]])
                    if t == NT - 1:
                        pa, Rpa = next_ps()
                        pb, Rpb = next_ps()
                        for k in range(8):
                            P.mm(pa[:], xTg[:, k, t4 * 128:(t4 + 1) * 128], winA[:, k, 0:512], k == 0, k == 7,
                                 [RxTg, R_winA], [Rpa], sig=(k == 7))
                        for k in range(8):
                            P.mm(pb[:], xTg[:, k, t4 * 128:(t4 + 1) * 128], winA[:, k, 512:1024], k == 0, k == 7,
                                 [RxTg, R_winA], [Rpb], sig=(k == 7))
                        P.act(sg[0][:], pb[:], AF.Sigmoid, [Rpb], [R_sg[0]])
                        P.tt("dve", ust[:], pa[:], sg[0][:], ALU.mult, [Rpa, R_sg[0]], [R_ust])
                        P.dma("sp", cpo[seq * 30:(seq + 1) * 30, :], ust[98:128, :], s_co, reads=[R_ust])
                gs = slice(g * 512, (g + 1) * 512)
                for c in range(4):
                    pq, Rpq = next_ps()
                    for k in range(8):
                        P.mm(pq[:], winA[:, k, 1024 + c * 128:1024 + (c + 1) * 128], xTg[:, k, :], k == 0, k == 7,
                             [RxTg, R_winA], [Rpq], sig=(k == 7))
                    P.act(qT[:, c, gs], pq[:], AF.Copy, [Rpq], [R_qT[g]], scale=0.125)
                for c in range(4):
                    pa, Rpa = next_ps()
                    pb, Rpb = next_ps()
                    for k in range(8):
                        P.mm(pb[:], winA[:, k, 512 + c * 128:512 + (c + 1) * 128], xTg[:, k, :], k == 0, k == 7,
                             [RxTg, R_winA], [Rpb], sig=(k == 7))
                    for k in range(8):
                        P.mm(pa[:], winA[:, k, c * 128:(c + 1) * 128], xTg[:, k, :], k == 0, k == 7,
                             [RxTg, R_winA], [Rpa], sig=(k == 7))
                    P.act(sg[c % 2][:], pb[:], AF.Sigmoid, [Rpb], [R_sg[c % 2]])
                    P.tt("dve", uT[:, c, 32 + g * 512:32 + (g + 1) * 512], pa[:], sg[c % 2][:], ALU.mult,
                         [Rpa, R_sg[c % 2]], [R_uT[g]])
            P.barrier()
            a_es.close()
            if do_s:
                SB = sample_S1(winA, SBP)
            wa_es.close()

            c_es = ExitStack()
            sbB = lambda name, shape, dtype: sb("%s_q%d" % (name, seq), shape, dtype, c_es)
            acc = [sbB("acc%d" % i, [128, 512], F32) for i in range(4)]
            cbf = [sbB("cbf%d" % i, [128, 512], BF16) for i in range(4)]
            sqb = [sbB("sqb%d" % i, [128, 512], BF16) for i in range(4)]
            rstd = sbB("rstd", [128, 512], F32)
            R_acc = [Res("acc") for _ in range(4)]
            R_cbf = [Res("cbf") for _ in range(4)]
            R_sqb = [Res("sqb") for _ in range(4)]
            R_rstd = Res("rstd")

            def gen_B():
                for g in range(NG):
                    base = 2 + g * 512
                    rd = [R_uT[g], R_uh] + ([R_uT[g - 1]] if g > 0 else [])
                    for w in range(CW):
                        for c in range(4):
                            src_ = uT[:, c, base + w:base + w + 512]
                            wv = cw_sb[:, c * CW + w:c * CW + w + 1]
                            if w == 0:
                                P.ts("dve", acc[c][:], src_, wv, cvec_sb[:, c:c + 1], ALU.mult, ALU.add,
                                     rd + [R_const], [R_acc[c]])
                            else:
                                P.stt("dve", acc[c][:], src_, wv, acc[c][:], ALU.mult, ALU.add,
                                      rd + [R_const, R_acc[c]], [R_acc[c]])
                        yield
                    for c in range(4):
                        P.copy("pool", cbf[c][:], acc[c][:], [R_acc[c]], [R_cbf[c]])
                    yield
                    pm, Rpm = psT[0][:].bitcast(F32), R_psT[0]
                    for c in range(4):
                        P.mm(pm[:], onesM[:], cbf[c][:], c == 0, c == 3, [R_const, R_cbf[c]], [Rpm], sig=(c == 3))
                    yield
                    for c in range(4):
                        P.tt("dve", acc[c][:], acc[c][:], pm[:], ALU.subtract, [R_acc[c], Rpm], [R_acc[c]])
                        P.act(sqb[c][:], acc[c][:], AF.Square, [R_acc[c]], [R_sqb[c]])
                    yield
                    pv, Rpv = psT[1][:].bitcast(F32), R_psT[1]
                    for c in range(4):
                        P.mm(pv[:], onesM[:], sqb[c][:], c == 0, c == 3, [R_const, R_sqb[c]], [Rpv], sig=(c == 3))
                    yield
                    P.act(rstd[:], pv[:], AF.Ln, [Rpv], [R_rstd], bias=EPS)
                    yield
                    P.act(rstd[:], rstd[:], AF.Exp, [R_rstd], [R_rstd], scale=-0.5)
                    yield
                    for c in range(4):
                        P.tt("dve", acc[c][:], acc[c][:], rstd[:], ALU.mult, [R_acc[c], R_rstd], [R_acc[c]])
                        P.act(sT[:, c, g * 512:(g + 1) * 512], acc[c][:], AF.Silu, [R_acc[c], R_const], [R_sT[g]],
                              bias=cvec_sb[:, 8 + c:9 + c], scale=cvec_sb[:, 4 + c:5 + c])
                        yield

            bgs = [gen_B()]
            if seq == 0:
                stfC = [sbB("stfC%d" % i, [128, SLOTW], F32) for i in range(2)]
                stbC = [sbB("stbC%d" % i, [128, SLOTW], BF16) for i in range(2)]
                R_stfC = [Res("stfC") for _ in range(2)]
                R_stbC = [Res("stbC") for _ in range(2)]
                s_cld = [P.newsem("s_cld%d" % i) for i in range(2)]
                s_cst = [P.newsem("s_cst%d" % i) for i in range(2)]

                def conv_load(sl):
                    i = sl % 2
                    for (wn, k0, nk, c0, ncols, off, ks) in slots[sl]:
                        srcv = W[wn][k0 * 128:(k0 + nk) * 128, c0:c0 + ncols].rearrange("(k p) c -> p k c", p=128)
                        dstv = stfC[i][:, :].rearrange("p (k c) -> p k c", c=ks)[:, off // ks:off // ks + nk, off % ks:off % ks + ncols]
                        P.dma("sp", dstv, srcv, s_cld[i], writes=[R_stfC[i]])

                def gen_conv():
                    conv_load(0)
                    yield
                    for sl in range(NSL):
                        i = sl % 2
                        if sl + 1 < NSL:
                            conv_load(sl + 1)
                            yield
                        for q4 in range(4):
                            P.copy("dve", stbC[i][:, q4 * 1024:(q4 + 1) * 1024], stfC[i][:, q4 * 1024:(q4 + 1) * 1024],
                                   [R_stfC[i]], [R_stbC[i]])
                            yield
                        P.dma("sp", scratch[sl], stbC[i][:], s_cst[i], reads=[R_stbC[i]], writes=[WS.scr[sl]])
                        yield

                bgs.append(gen_conv())

            def bg_step():
                for gi in list(bgs):
                    try:
                        next(gi)
                    except StopIteration:
                        bgs.remove(gi)

            sbC = lambda name, shape, dtype: sb("%s_q%d" % (name, seq), shape, dtype, c_es)
            NB = 4
            Ef = [sbC("Ef%d" % i, [128, 512], F32) for i in range(NB)]
            Lb = [sbC("Lb%d" % i, [128, 512], BF16) for i in range(NB)]
            aT = [sbC("aT%d" % i, [128, 512], BF16) for i in range(NB)]
            Lsum = [sbC("Lsum%d" % i, [128, 512], BF16) for i in range(2)]
            R_Ef = [Res("Ef") for _ in range(NB)]
            R_Lb = [Res("Lb") for _ in range(NB)]
            R_aT = [Res("aT") for _ in range(NB)]
            R_Lsum = [Res("Lsum") for _ in range(2)]
            zbank = [(ps[i], R_ps[i]) for i in range(4)]
            obank = [(ps[4], R_ps[4]), (ps[5], R_ps[5])]
            units = []
            hc = 0
            for c in range(NG):
                for hp in range(4):
                    for hh in range(2):
                        h = hp * 2 + hh
                        nkb = 4 * c + 4
                        for ui, kbk in enumerate(range(nkb - 1, -1, -1)):
                            i_ = kbk - 4 * c
                            col0 = max(i_, 0) * 128
                            units.append(dict(h=h, c=c, kb=kbk, diag=(i_ >= 0), col0=col0, first=(ui == 0),
                                              last=(kbk == 0), hc=hc, ob=(c * 4 + hp) % 2))
                        hc += 1
            NU = len(units)

            def P1(u):
                U = units[u]
                zb, Rz = zbank[u % 4]
                h, c, kbk, col0 = U["h"], U["c"], U["kb"], U["col0"]
                p0 = (h % 2) * 64
                P.mm(zb[:, col0:512], kT[p0:p0 + 64, h // 2, kbk * 128:(kbk + 1) * 128],
                     qT[p0:p0 + 64, h // 2, c * 512 + col0:(c + 1) * 512], True, True,
                     [R_kT[kbk], R_qT[c]], [Rz], sig=True)

            def A1(u):
                U = units[u]
                zb, Rz = zbank[u % 4]
                h, col0 = U["h"], U["col0"]
                b = u % NB
                P.act(Ef[b][:, col0:512], zb[:, col0:512], AF.Exp, [Rz, R_const], [R_Ef[b]], bias=biasT[:, h:h + 1])

            def A2(u):
                U = units[u]
                col0 = U["col0"]
                b = u % NB
                P.act(Lb[b][:, col0:512], Ef[b][:, col0:512], AF.Ln, [R_Ef[b]], [R_Lb[b]], bias=1.0)
                if U["diag"]:
                    P.tt("pool", Lb[b][:, col0:col0 + 128], Lb[b][:, col0:col0 + 128], tri[:], ALU.mult,
                         [R_Lb[b], R_const], [R_Lb[b]])

            def P2G(u):
                U = units[u]
                zb, Rz = zbank[u % 4]
                col0 = U["col0"]
                b = u % NB
                ls = U["hc"] % 2
                P.mm(zb[:, col0:512], negU[:], Lb[b][:, col0:512], False, U["first"], [R_const, R_Lb[b]], [Rz],
                     sig=U["first"])
                if not U["first"]:
                    P.mm(zb[:, col0:512], negOnes[:], Lsum[ls][:, col0:512], False, True, [R_const, R_Lsum[ls]], [Rz],
                         sig=True)
                if not U["last"]:
                    if U["first"]:
                        P.memset("pool", Lsum[ls][:], 0.0, [R_Lsum[ls]])
                    P.tt("pool", Lsum[ls][:, col0:512], Lsum[ls][:, col0:512], Lb[b][:, col0:512], ALU.add,
                         [R_Lsum[ls], R_Lb[b]], [R_Lsum[ls]])

            def A3(u):
                U = units[u]
                zb, Rz = zbank[u % 4]
                h, col0 = U["h"], U["col0"]
                b = u % NB
                P.act(aT[b][:, col0:512], zb[:, col0:512], AF.Exp, [Rz, R_const], [R_aT[b]], bias=biasT[:, h:h + 1])
                if U["diag"]:
                    P.tt("pool", aT[b][:, col0:col0 + 128], aT[b][:, col0:col0 + 128], tri[:], ALU.mult,
                         [R_aT[b], R_const], [R_aT[b]])

            def P3(u):
                U = units[u]
                h, c, kbk, col0 = U["h"], U["c"], U["kb"], U["col0"]
                b = u % NB
                ob, Rob = obank[U["ob"]]
                p0 = (h % 2) * 64
                if U["first"]:
                    P.mm(ob[p0:p0 + 64, :], zer[:, 0:64], zer[:, :], True, False, [R_const], [Rob])
                P.mm(ob[p0:p0 + 64, col0:512], vv[:, kbk, h * 64:(h + 1) * 64], aT[b][:, col0:512], False, U["last"],
                     [R_vv[kbk], R_aT[b]], [Rob], sig=True)
                if U["last"]:
                    P.copy("act", oT[p0:p0 + 64, h // 2, c * 512:(c + 1) * 512], ob[p0:p0 + 64, :], [Rob], [R_oT[c]])

            for p in range(-1, NU + 3):
                if 0 <= p - 3 < NU:
                    P3(p - 3)
                if 0 <= p + 1 < NU:
                    P1(p + 1)
                if 0 <= p < NU:
                    A1(p)
                if 0 <= p - 2 < NU:
                    A3(p - 2)
                if 0 <= p < NU:
                    A2(p)
                if 0 <= p - 1 < NU:
                    P2G(p - 1)
                bg_step()
            while bgs:
                bg_step()
            P.barrier()
            c_es.close()
            abc_es.close()

            if do_s:
                s_sT, o_sT, R_ssT, R_osT = sample_S2(SB)
            d_es = ExitStack()
            sbD = lambda name, shape, dtype: sb("%s_q%d" % (name, seq), shape, dtype, d_es)
            ring = sbD("ring", [128, NSLOT, SLOTW], BF16)
            WS.buf = ring
            NTL = 5 if do_s else 4
            WD = 528 if do_s else 512
            xg = sbD("xg", [128, NTL, D], F32)
            xbD = [sbD("xbD%d" % i, [128, D], BF16) for i in range(2)]
            xTd = sbD("xTd", [128, 8, WD], BF16)
            actT = sbD("actT", [128, NJ, WD], BF16)
            mixT = actT
            pst_ = [sbD("pst%d" % i, [128, PLE], F32) for i in range(2)]
            pbD = [sbD("pbD%d" % i, [128, PLE], BF16) for i in range(2)]
            pT = sbD("pT", [128, 2, WD], BF16)
            sgD = [sbD("sgD%d" % i, [128, 512], F32) for i in range(3)]
            stat = sbD("stat", [128, NTL, 16], F32)
            R_xg = [Res("xg%d" % i) for i in range(NTL)]
            R_xbD = [Res("xbD") for _ in range(2)]
            R_xTd = Res("xTd")
            R_actT = Res("actT")
            R_pst = [Res("pst") for _ in range(2)]
            R_pbD = [Res("pbD") for _ in range(2)]
            R_pT = Res("pT")
            R_sgD = [Res("sgD") for _ in range(3)]
            R_st = [Res("stat%d" % i) for i in range(NTL)]
            s_xg = [P.newsem("s_xg%d_%d" % (seq, i)) for i in range(NTL)]
            s_pl = [P.newsem("s_pl%d_%d" % (seq, i)) for i in range(2)]
            s_yo = [P.newsem("s_yo%d_%d" % (seq, i)) for i in range(NTL)]
            out_sems += s_yo
            for r_ in WS.res:
                r_.w = None
                r_.r = []
            WS.limit = (seq + 1) * NG * NSL
            WS.prefetch()
            sgi = 0

            def layer_norm_all(tl, gcol, bcol):
                for hf in range(2):
                    for (t4, npt, _) in tl:
                        P.bn_stats(stat[0:npt, t4, hf * 6:hf * 6 + 6], xg[0:npt, t4, hf * 512:(hf + 1) * 512], [R_xg[t4]], [R_st[t4]])
                for (t4, npt, _) in tl:
                    P.bn_aggr(stat[0:npt, t4, 12:14], stat[0:npt, t4, 0:12], [R_st[t4]], [R_st[t4]])
                for (t4, npt, _) in tl:
                    P.act(stat[0:npt, t4, 14:15], stat[0:npt, t4, 13:14], AF.Ln, [R_st[t4]], [R_st[t4]], bias=EPS)
                for (t4, npt, _) in tl:
                    P.act(stat[0:npt, t4, 14:15], stat[0:npt, t4, 14:15], AF.Exp, [R_st[t4]], [R_st[t4]], scale=-0.5)
                for (t4, npt, _) in tl:
                    xt = xg[0:npt, t4, :]
                    P.ts("dve", xt, xt, stat[0:npt, t4, 12:13], stat[0:npt, t4, 14:15], ALU.subtract, ALU.mult, [R_xg[t4], R_st[t4]], [R_xg[t4]])
                for (t4, npt, _) in tl:
                    xt = xg[0:npt, t4, :]
                    P.tt("pool", xt, xt, lnb[0:npt, gcol, :], ALU.mult, [R_xg[t4], R_const], [R_xg[t4]])
                for (t4, npt, _) in tl:
                    xt = xg[0:npt, t4, :]
                    P.tt("pool", xt, xt, lnb[0:npt, bcol, :], ALU.add, [R_xg[t4], R_const], [R_xg[t4]])

            def to_T(t4, npt, tsl, src_rows, width, dstT, R_dst, stage, R_stage, eng_copy):
                nk = width // 128
                P.copy("pool", stage[0:npt, :], src_rows, R_stage[0], R_stage[1])
                pt_, Rpt = next_psT()
                for k in range(nk):
                    P.tr(pt_[:, k * 128:k * 128 + npt], stage[0:npt, k * 128:(k + 1) * 128], identb[0:npt, 0:npt],
                         R_stage[1] + [R_const], [Rpt], sig=(k == nk - 1))
                P.copy(eng_copy, dstT[:, :, tsl], pt_[:, 0:nk * 128].rearrange("p (k n) -> p k n", k=nk)[:, :, 0:npt],
                       [Rpt], [R_dst])

            for g in range(NG):
                gs = slice(g * 512, (g + 1) * 512)
                last = do_s and g == NG - 1
                tiles = [(t4, 128, slice(t4 * 128, (t4 + 1) * 128)) for t4 in range(4)]
                cgs = [(slice(0, 512), 512, 0)]
                if last:
                    tiles.append((4, 16, slice(512, 528)))
                    cgs.append((slice(512, 528), 16, 1))
                for (t4, npt, tsl) in tiles:
                    b = t4 % 2
                    if t4 < 4:
                        row0 = seq * S + (g * 4 + t4) * 128
                        P.dma("sp", xg[:, t4, :], xp[row0:row0 + 128, :], s_xg[t4], writes=[R_xg[t4]])
                    else:
                        P.dma("sp", xg[0:16, t4, :], xs[:, :], s_xg[t4], writes=[R_xg[t4]])
                    to_T(t4, npt, tsl, xg[0:npt, t4, :], D, xTd, R_xTd, xbD[b], ([R_xg[t4]], [R_xbD[b]]), "dve")
                for mh in range(2):
                    wP, RwP = WS.get(0)
                    wC, RwC = WS.get(1)
                    wA, RwA = WS.get(2)
                    for m4 in range(4):
                        m = mh * 4 + m4
                        cs = slice(m4 * 128, (m4 + 1) * 128)
                        for (csl, n, isS) in cgs:
                            sTs = s_sT if isS else sT
                            oTs = o_sT if isS else oT
                            Rs_ = R_ssT if isS else R_sT[g]
                            Ro_ = R_osT if isS else R_oT[g]
                            ssl = slice(0, 16) if isS else gs
                            pc, Rpc = next_ps()
                            for k in range(8):
                                P.mm(pc[:, 0:n], wC[:, k * 512:(k + 1) * 512][:, cs], xTd[:, k, csl], k == 0, k == 7,
                                     [RwC, R_xTd], [Rpc], sig=(k == 7))
                            s1 = sgi % 3; sgi += 1
                            P.act(sgD[s1][:, 0:n], pc[:, 0:n], AF.Sigmoid, [Rpc], [R_sgD[s1]])
                            pa, Rpa = next_ps()
                            for k in range(8):
                                P.mm(pa[:, 0:n], wA[:, k * 512:(k + 1) * 512][:, cs], xTd[:, k, csl], k == 0, k == 7,
                                     [RwA, R_xTd], [Rpa], sig=(k == 7))
                            s2 = sgi % 3; sgi += 1
                            P.act(sgD[s2][:, 0:n], pa[:, 0:n], AF.Sigmoid, [Rpa], [R_sgD[s2]])
                            po, Rpo = next_ps()
                            for k in range(4):
                                P.mm(po[:, 0:n], wP[:, k * 512:(k + 1) * 512][:, cs], sTs[:, k, ssl], k == 0, k == 3,
                                     [RwP, Rs_], [Rpo], sig=(k == 3))
                            P.tt("dve", sgD[s1][:, 0:n], sgD[s1][:, 0:n], po[:, 0:n], ALU.mult, [R_sgD[s1], Rpo], [R_sgD[s1]])
                            po2, Rpo2 = next_ps()
                            for k in range(4):
                                P.mm(po2[:, 0:n], wP[:, (4 + k) * 512:(5 + k) * 512][:, cs], oTs[:, k, ssl], k == 0, k == 3,
                                     [RwP, Ro_], [Rpo2], sig=(k == 3))
                            P.tt("dve", sgD[s2][:, 0:n], sgD[s2][:, 0:n], po2[:, 0:n], ALU.mult, [R_sgD[s2], Rpo2], [R_sgD[s2]])
                            P.tt("pool", mixT[:, m, csl], sgD[s1][:, 0:n], sgD[s2][:, 0:n], ALU.add, [R_sgD[s1], R_sgD[s2]], [R_actT])
                    WS.done(); WS.done(); WS.done()
                for hf in range(2):
                    wO, RwO = WS.get()
                    for (t4, npt, tsl) in tiles:
                        pw, Rpw = next_ps()
                        for k in range(8):
                            P.mm(pw[0:npt, :], mixT[:, k, tsl], wO[:, k * 512:(k + 1) * 512], k == 0, k == 7,
                                 [R_actT, RwO], [Rpw], sig=(k == 7))
                        P.stt("dve", xg[0:npt, t4, hf * 512:(hf + 1) * 512], xg[0:npt, t4, hf * 512:(hf + 1) * 512], ALPHA, pw[0:npt, :],
                              ALU.mult, ALU.add, [R_xg[t4], Rpw], [R_xg[t4]])
                    WS.done()
                layer_norm_all(tiles, 0, 1)
                for (t4, npt, tsl) in tiles:
                    b = t4 % 2
                    to_T(t4, npt, tsl, xg[0:npt, t4, :], D, xTd, R_xTd, xbD[b], ([R_xg[t4]], [R_xbD[b]]), "dve")
                    if t4 < 4:
                        row0 = seq * S + (g * 4 + t4) * 128
                        P.dma("sp", pst_[b][:], pp[row0:row0 + 128, :], s_pl[b], writes=[R_pst[b]])
                    else:
                        P.dma("sp", pst_[b][0:16, :], pps[:, :], s_pl[b], writes=[R_pst[b]])
                    to_T(t4, npt, tsl, pst_[b][0:npt, :], PLE, pT, R_pT, pbD[b], ([R_pst[b]], [R_pbD[b]]), "act")
                for jp in range(NJ // 2):
                    wU, RwU = WS.get()
                    for jj in range(2):
                        j = jp * 2 + jj
                        for (csl, n, isS) in cgs:
                            pg_, Rpg = next_ps()
                            pu, Rpu = next_ps()
                            for k in range(8):
                                o_ = k * 512 + jj * 128
                                P.mm(pg_[:, 0:n], wU[:, o_:o_ + 128], xTd[:, k, csl], k == 0, k == 7, [RwU, R_xTd], [Rpg], sig=(k == 7))
                            for k in range(8):
                                o_ = k * 512 + 256 + jj * 128
                                P.mm(pu[:, 0:n], wU[:, o_:o_ + 128], xTd[:, k, csl], k == 0, k == 7, [RwU, R_xTd], [Rpu], sig=(k == 7))
                            s1 = sgi % 3; sgi += 1
                            P.act(sgD[s1][:, 0:n], pg_[:, 0:n], AF.Silu, [Rpg], [R_sgD[s1]])
                            P.tt("dve", actT[:, j, csl], sgD[s1][:, 0:n], pu[:, 0:n], ALU.mult, [R_sgD[s1], Rpu], [R_actT])
                    WS.done()
                fbank = [0, 1, 2, 3, 4]
                for hf in range(2):
                    hs = slice(hf * 512, (hf + 1) * 512)
                    wG, RwG = WS.get(0)
                    wL, RwL = WS.get(1)
                    for (t4, npt, tsl) in tiles:
                        pgt, Rpgt = next_ps()
                        for k in range(8):
                            P.mm(pgt[0:npt, :], xTd[:, k, tsl], wG[:, k * 512:(k + 1) * 512], k == 0, k == 7,
                                 [R_xTd, RwG], [Rpgt], sig=(k == 7))
                        s1 = sgi % 3; sgi += 1
                        P.act(sgD[s1][0:npt, :], pgt[0:npt, :], AF.Sigmoid, [Rpgt], [R_sgD[s1]])
                        ppl, Rppl = next_ps()
                        for k in range(2):
                            P.mm(ppl[0:npt, :], pT[:, k, tsl], wL[:, k * 512:(k + 1) * 512], k == 0, k == 1,
                                 [R_pT, RwL], [Rppl], sig=(k == 1))
                        P.tt("dve", sgD[s1][0:npt, :], sgD[s1][0:npt, :], ppl[0:npt, :], ALU.mult, [R_sgD[s1], Rppl], [R_sgD[s1]])
                        P.stt("dve", xg[0:npt, t4, hs], xg[0:npt, t4, hs], ALPHA, sgD[s1][0:npt, :], ALU.mult, ALU.add,
                              [R_xg[t4], R_sgD[s1]], [R_xg[t4]])
                    WS.done()
                    kblk = 0
                    for si_ in range(3):
                        if si_ == 0:
                            wD, RwD, offs = wL, RwL, list(range(2, 8))
                        else:
                            wD, RwD = WS.get()
                            offs = list(range(8))
                        for (t4, npt, tsl) in tiles:
                            fb = fbank[t4]
                            for oi, o_ in enumerate(offs):
                                kk = kblk + oi
                                P.mm(ps[fb][0:npt, :], actT[:, kk, tsl], wD[:, o_ * 512:(o_ + 1) * 512], kk == 0, kk == NJ - 1,
                                     [R_actT, RwD], [R_ps[fb]], sig=(oi == len(offs) - 1))
                        kblk += len(offs)
                        WS.done()
                    for (t4, npt, tsl) in tiles:
                        fb = fbank[t4]
                        P.tt("dve", xg[0:npt, t4, hs], xg[0:npt, t4, hs], ps[fb][0:npt, :], ALU.add, [R_xg[t4], R_ps[fb]], [R_xg[t4]])
                layer_norm_all(tiles, 2, 3)
                for (t4, npt, tsl) in tiles:
                    if t4 < 4:
                        row0 = seq * S + (g * 4 + t4) * 128
                        P.dma("sp", yp[row0:row0 + 128, :], xg[:, t4, :], s_yo[t4], reads=[R_xg[t4]])
                    elif not DEBUG:
                        P.dma("sp", ys[:, :], xg[0:16, t4, :], s_yo[t4], reads=[R_xg[t4]])
            P.barrier()
            d_es.close()
            if do_s:
                samp_es.close()
            seq_es.close()

        P.barrier()
        block = es.enter_context(nc.Block())
        P.replay(block)
    return nc


def host_consts():
    c = np.zeros((128, 4 * 128), np.float32)
    c[:, 0:128] = np.eye(128, dtype=np.float32)
    j = np.arange(128)[:, None]
    s = np.arange(128)[None, :]
    c[:, 128:256] = np.where(j >= s, -1.0, 0.0)
    c[:, 256:384] = np.where(j < s, 1.0, 0.0)
    for jj in range(4):
        for hh in range(H):
            for q in range(4):
                c[jj, 384 + hh * 4 + q] = 1.0 if jj < q else 0.0
    return c


def make_in_maps(inputs, n_cores, S, NPG):
    f = lambda a: np.ascontiguousarray(np.asarray(a))
    x_prompt = f(inputs["x_prompt"]); p_prompt = f(inputs["p_prompt"])[0]
    x_sample = f(inputs["x_sample"]); p_sample = f(inputs["p_sample"])[0]
    ck = f(inputs["cache_k"])[0]; cv = f(inputs["cache_v"])[0]
    npool = ck.shape[0]
    ck = ck.reshape(npool * 128, 512); cv = cv.reshape(npool * 128, 512)
    sconv = f(inputs["state_conv"])[0]
    pt = f(inputs["page_table"]).astype(np.int32)
    cw = f(inputs["conv_w"])[0]
    cwT = np.ascontiguousarray(cw.reshape(CW, 4, 128).transpose(2, 1, 0).reshape(128, 4 * CW))
    vec = lambda a: f(a)[0].reshape(4, 128).T
    cvec = np.ascontiguousarray(np.concatenate([vec(inputs["conv_b"]), vec(inputs["conv_ln_g"]), vec(inputs["conv_ln_b"])], axis=1))
    lnp = np.ascontiguousarray(np.stack([f(inputs["ln1_g"])[0], f(inputs["ln1_b"])[0], f(inputs["ln2_g"])[0], f(inputs["ln2_b"])[0]]))
    sbb = f(inputs["sb_bias"]).reshape(1, H)
    brow = np.ascontiguousarray(np.broadcast_to(sbb.reshape(1, H, 1), (16, H, 4)).reshape(1, 512))
    consts = host_consts()
    shared = {
        "ck": ck, "cv": cv, "cwT": cwT, "cvec": cvec, "lnp": lnp, "sbb": sbb, "consts": consts, "brow": brow,
        "w_in": f(inputs["w_in"])[0], "w_conv_proj": f(inputs["w_conv_proj"])[0], "w_att_proj": f(inputs["w_att_proj"])[0],
        "w_out": f(inputs["w_out"])[0], "w_ffn_up": f(inputs["w_ffn_up"])[0], "w_ffn_down": f(inputs["w_ffn_down"])[0],
        "w_ple_gate": f(inputs["w_ple_gate"])[0], "w_ple": f(inputs["w_ple"])[0],
    }
    maps = []
    for c in range(n_cores):
        m = dict(shared)
        m["xp"] = x_prompt[2 * c:2 * c + 2].reshape(2 * S, D)
        m["pp"] = p_prompt[2 * c:2 * c + 2].reshape(2 * S, PLE)
        m["xs"] = x_sample[4 * c:4 * c + 4].reshape(16, D)
        m["pps"] = p_sample[4 * c:4 * c + 4].reshape(16, PLE)
        m["sconv"] = sconv[4 * c:4 * c + 4].reshape(120, CC)
        m["ptab"] = pt[4 * c:4 * c + 4].reshape(1, 4 * NPG)
        maps.append(m)
    return maps, npool


def assemble(results, n_cores, S):
    cat = lambda k: np.concatenate([np.asarray(r[k]) for r in results], axis=0)
    B = 2 * n_cores
    DB = 4 * n_cores
    y_p = cat("yp").reshape(B, S, D)
    y_s = cat("ys").reshape(DB, 4, D)
    k_p = cat("kp").reshape(1, B, S, H, DH)
    v_p = cat("vp").reshape(1, B, S, H, DH)
    c_p = cat("cpo").reshape(1, B, 30, CC)
    k_s = cat("ksn").reshape(1, DB, 4, H, DH)
    v_s = cat("vsn").reshape(1, DB, 4, H, DH)
    c_s = cat("csn").reshape(1, DB, 30, CC)
    return tuple(np.ascontiguousarray(a, dtype=np.float32) for a in (y_p, y_s, k_p, v_p, c_p, k_s, v_s, c_s))


def run(inputs, n_cores, S, NPG):
    maps, npool = make_in_maps(inputs, n_cores, S, NPG)
    nc = build(S, NPG, npool)
    res = run_bass_kernel_spmd(nc, maps, core_ids=list(range(n_cores)))
    return assemble(res.results, n_cores, S)


def kernel(**inputs):
    return run(inputs, 8, 2048, 128)
```
